# Optimizing a Trainium2 kernel written in Bass

```python
import jax, jax.numpy as jnp
from jax import lax
import numpy as np

D_MODEL = 1024
BATCH = 8
SEQ = 2048
DEPTH = 4
DEC_BATCH = 128
DEC_SEQ = 8
PAST_LEN = 16384
PAGE_SIZE = 128

N_MIXERS = 2
EXPAND = 2
D_INNER = EXPAND * D_MODEL
HEAD_DIM = 64
N_HEADS = D_INNER // HEAD_DIM
N_GROUPS = 8
HEADS_PER_GROUP = N_HEADS // N_GROUPS
D_STATE = 128
CONV_WIDTH = 4
CONV_DIM = D_INNER + 2 * N_GROUPS * D_STATE
D_IN_PROJ = 2 * D_INNER + 2 * N_GROUPS * D_STATE + N_HEADS
CHUNK = 128
POOL_WIDTH = EXPAND * D_MODEL
POOL_WINDOWS = (2, 4, 8, 16)
N_POOL_GROUPS = len(POOL_WINDOWS)
POOL_GROUP_DIM = POOL_WIDTH // N_POOL_GROUPS
POOL_STATE = max(POOL_WINDOWS) - 1
N_SSD_LAYERS = (DEPTH + 1) // 2
N_POOL_LAYERS = DEPTH // 2
EPS = 1e-6

kernel_name = "hybrid_ssd_pool_adaln_step"


def rmsnorm(x, g):
    xf = x.astype(jnp.float32)
    return xf * lax.rsqrt(jnp.mean(xf * xf, axis=-1, keepdims=True) + EPS) * g.astype(jnp.float32)


def ssd_scan(x, dt, A, B, C, h0):
    b, L, H, P = x.shape
    T = min(CHUNK, L)
    nc = L // T
    G, R, N = N_GROUPS, HEADS_PER_GROUP, D_STATE
    x = x.reshape(b, nc, T, G, R, P)
    dt = dt.reshape(b, nc, T, G, R)
    B = B.reshape(b, nc, T, G, N)
    C = C.reshape(b, nc, T, G, N)
    acum = jnp.cumsum(dt * A.reshape(G, R), axis=2)
    seg = acum[:, :, :, None] - acum[:, :, None, :]
    mask = jnp.tril(jnp.ones((T, T), dtype=bool))[:, :, None, None]
    decay = jnp.where(mask, jnp.exp(jnp.where(mask, seg, 0.0)), 0.0)
    CB = jnp.einsum('bctgn,bcsgn->bctsg', C, B)
    xdt = x * dt[..., None]
    y_intra = jnp.einsum('bctsgr,bcsgrp->bctgrp', CB[..., None] * decay, xdt)
    decay_end = jnp.exp(acum[:, :, -1:] - acum)
    chunk_states = jnp.einsum('bctgn,bctgr,bctgrp->bcgrpn', B, decay_end, xdt)
    chunk_decay = jnp.exp(acum[:, :, -1])

    def step(h, inp):
        s, d = inp
        return d[..., None, None] * h + s, h

    h_final, h_prev = lax.scan(step, h0.reshape(b, G, R, P, N),
                               (jnp.moveaxis(chunk_states, 1, 0), jnp.moveaxis(chunk_decay, 1, 0)))
    h_prev = jnp.moveaxis(h_prev, 0, 1)
    y_inter = jnp.einsum('bctgn,bctgr,bcgrpn->bctgrp', C, jnp.exp(acum), h_prev)
    y = (y_intra + y_inter).reshape(b, L, H, P)
    return y, h_final.reshape(b, H, P, N)


def ssd_branch(h, conv_prev, ssm_prev, w_in, conv_w, conv_b, dt_bias, a_log, d_skip, norm_g, w_out):
    b, L, _ = h.shape
    zxbcdt = h @ w_in
    z = zxbcdt[..., :D_INNER]
    xbc = zxbcdt[..., D_INNER:D_INNER + CONV_DIM].astype(jnp.float32)
    dt_raw = zxbcdt[..., D_INNER + CONV_DIM:].astype(jnp.float32)
    xp = jnp.concatenate([conv_prev.astype(jnp.float32), xbc], axis=1)
    conv = conv_b.astype(jnp.float32) + sum(xp[:, k:k + L] * conv_w[k].astype(jnp.float32)
                                            for k in range(CONV_WIDTH))
    new_conv = xp[:, -(CONV_WIDTH - 1):]
    xbc_act = jax.nn.silu(conv)
    xs = xbc_act[..., :D_INNER].reshape(b, L, N_HEADS, HEAD_DIM)
    Bm = xbc_act[..., D_INNER:D_INNER + N_GROUPS * D_STATE].reshape(b, L, N_GROUPS, D_STATE)
    Cm = xbc_act[..., D_INNER + N_GROUPS * D_STATE:].reshape(b, L, N_GROUPS, D_STATE)
    dt = jax.nn.softplus(dt_raw + dt_bias.astype(jnp.float32))
    A = -jnp.exp(a_log.astype(jnp.float32))
    y, new_ssm = ssd_scan(xs, dt, A, Bm, Cm, ssm_prev.astype(jnp.float32))
    y = (y + d_skip.astype(jnp.float32)[:, None] * xs).reshape(b, L, D_INNER)
    gated = (y * jax.nn.silu(z.astype(jnp.float32))).reshape(b, L, N_GROUPS, D_INNER // N_GROUPS)
    gated = gated * lax.rsqrt(jnp.mean(gated * gated, axis=-1, keepdims=True) + EPS)
    gated = gated.reshape(b, L, D_INNER) * norm_g.astype(jnp.float32)
    return gated.astype(h.dtype) @ w_out, new_conv, new_ssm


def pool_branch(h, pool_prev, start_pos, w_in, w_group, ch_scale, w_out):
    b, L, _ = h.shape
    uz = h @ w_in
    u = uz[..., :POOL_WIDTH].astype(jnp.float32)
    z = uz[..., POOL_WIDTH:].astype(jnp.float32)
    up = jnp.concatenate([pool_prev.astype(jnp.float32), u], axis=1)
    cs = jnp.concatenate([jnp.zeros((b, 1, POOL_WIDTH), jnp.float32), jnp.cumsum(up, axis=1)], axis=1)
    off = POOL_STATE + 1
    pos = start_pos + jnp.arange(L)
    groups = []
    for g, w in enumerate(POOL_WINDOWS):
        sl = slice(g * POOL_GROUP_DIM, (g + 1) * POOL_GROUP_DIM)
        win_sum = cs[:, off:off + L, sl] - cs[:, off - w:off - w + L, sl]
        cnt = jnp.minimum(w, pos + 1).astype(jnp.float32)[None, :, None]
        groups.append(win_sum / cnt - u[..., sl])
    pooled = jnp.stack(groups, axis=2)
    mixed = jnp.einsum('blgc,gcd->blgd', pooled, w_group.astype(jnp.float32)).reshape(b, L, POOL_WIDTH)
    mixed = mixed * ch_scale.astype(jnp.float32) * jax.nn.silu(z)
    return mixed.astype(h.dtype) @ w_out, up[:, -POOL_STATE:]


def trunk(x, c, start_pos, ssm_in, conv_in, pool_in, ada_w, ada_b, norm_g,
          ssd_w_in, ssd_conv_w, ssd_conv_b, ssd_dt_bias, ssd_a_log, ssd_d, ssd_norm_g, ssd_w_out,
          pool_w_in, pool_w_group, pool_scale, pool_w_out, final_norm_g):
    new_ssm, new_conv, new_pool = [], [], []
    sc = jax.nn.silu(c.astype(jnp.float32))
    for i in range(DEPTH):
        mod = sc @ ada_w[i].astype(jnp.float32) + ada_b[i].astype(jnp.float32)
        shift, scale, gate = jnp.split(mod, 3, axis=-1)
        hn = (rmsnorm(x, norm_g[i]) * (1.0 + scale[:, None]) + shift[:, None]).astype(x.dtype)
        j = i // N_MIXERS
        if i % N_MIXERS == 0:
            out, cv, st = ssd_branch(hn, conv_in[j], ssm_in[j], ssd_w_in[j], ssd_conv_w[j], ssd_conv_b[j],
                                     ssd_dt_bias[j], ssd_a_log[j], ssd_d[j], ssd_norm_g[j], ssd_w_out[j])
            new_conv.append(cv)
            new_ssm.append(st)
        else:
            out, ps = pool_branch(hn, pool_in[j], start_pos, pool_w_in[j], pool_w_group[j],
                                  pool_scale[j], pool_w_out[j])
            new_pool.append(ps)
        x = (x.astype(jnp.float32) + (1.0 + gate[:, None]) * out.astype(jnp.float32)).astype(x.dtype)
    y = rmsnorm(x, final_norm_g).astype(x.dtype)
    return y, jnp.stack(new_ssm), jnp.stack(new_conv), jnp.stack(new_pool)


def setup_inputs(seed: int = 0) -> dict:
    key = jax.random.key(seed)
    ks = jax.random.split(key, 24)
    nrm = lambda k, shape, s: jax.random.normal(k, shape, jnp.float32) * s
    dt0 = jnp.exp(jax.random.uniform(ks[10], (N_SSD_LAYERS, N_HEADS), jnp.float32,
                                     np.log(1e-3), np.log(1e-1)))
    return {
        "x_prompt": nrm(ks[0], (BATCH, SEQ, D_MODEL), 1.0),
        "x_sample": nrm(ks[1], (DEC_BATCH, DEC_SEQ, D_MODEL), 1.0),
        "state_ssm": nrm(ks[2], (N_SSD_LAYERS, DEC_BATCH, N_HEADS, HEAD_DIM, D_STATE), 0.1),
        "state_conv": nrm(ks[3], (N_SSD_LAYERS, DEC_BATCH, CONV_WIDTH - 1, CONV_DIM), 1.0),
        "state_pool": nrm(ks[4], (N_POOL_LAYERS, DEC_BATCH, POOL_STATE, POOL_WIDTH), 1.0),
        "c_prompt": nrm(ks[5], (BATCH, D_MODEL), 1.0),
        "c_sample": nrm(ks[6], (DEC_BATCH, D_MODEL), 1.0),
        "ada_w": nrm(ks[7], (DEPTH, D_MODEL, 3 * D_MODEL), 0.1 * D_MODEL ** -0.5),
        "ada_b": nrm(ks[8], (DEPTH, 3 * D_MODEL), 0.02),
        "norm_g": 1.0 + nrm(ks[9], (DEPTH, D_MODEL), 0.02),
        "ssd_w_in": nrm(ks[11], (N_SSD_LAYERS, D_MODEL, D_IN_PROJ), D_MODEL ** -0.5),
        "ssd_conv_w": nrm(ks[12], (N_SSD_LAYERS, CONV_WIDTH, CONV_DIM), CONV_WIDTH ** -0.5),
        "ssd_conv_b": nrm(ks[13], (N_SSD_LAYERS, CONV_DIM), 0.02),
        "ssd_dt_bias": dt0 + jnp.log(-jnp.expm1(-dt0)),
        "ssd_a_log": jnp.log(jax.random.uniform(ks[14], (N_SSD_LAYERS, N_HEADS), jnp.float32, 1.0, 16.0)),
        "ssd_d": 1.0 + nrm(ks[15], (N_SSD_LAYERS, N_HEADS), 0.02),
        "ssd_norm_g": 1.0 + nrm(ks[16], (N_SSD_LAYERS, D_INNER), 0.02),
        "ssd_w_out": nrm(ks[17], (N_SSD_LAYERS, D_INNER, D_MODEL), D_INNER ** -0.5),
        "pool_w_in": nrm(ks[18], (N_POOL_LAYERS, D_MODEL, 2 * POOL_WIDTH), D_MODEL ** -0.5),
        "pool_w_group": nrm(ks[19], (N_POOL_LAYERS, N_POOL_GROUPS, POOL_GROUP_DIM, POOL_GROUP_DIM), POOL_GROUP_DIM ** -0.5),
        "pool_scale": 1.0 + nrm(ks[20], (N_POOL_LAYERS, POOL_WIDTH), 0.1),
        "pool_w_out": nrm(ks[21], (N_POOL_LAYERS, POOL_WIDTH, D_MODEL), POOL_WIDTH ** -0.5),
        "final_norm_g": 1.0 + nrm(ks[22], (D_MODEL,), 0.02),
    }


def reference(x_prompt, x_sample, state_ssm, state_conv, state_pool, c_prompt, c_sample,
              ada_w, ada_b, norm_g, ssd_w_in, ssd_conv_w, ssd_conv_b, ssd_dt_bias, ssd_a_log, ssd_d,
              ssd_norm_g, ssd_w_out, pool_w_in, pool_w_group, pool_scale, pool_w_out, final_norm_g):
    weights = (ada_w, ada_b, norm_g, ssd_w_in, ssd_conv_w, ssd_conv_b, ssd_dt_bias, ssd_a_log, ssd_d,
               ssd_norm_g, ssd_w_out, pool_w_in, pool_w_group, pool_scale, pool_w_out, final_norm_g)
    b = x_prompt.shape[0]
    ssm0 = jnp.zeros((N_SSD_LAYERS, b, N_HEADS, HEAD_DIM, D_STATE), jnp.float32)
    conv0 = jnp.zeros((N_SSD_LAYERS, b, CONV_WIDTH - 1, CONV_DIM), jnp.float32)
    pool0 = jnp.zeros((N_POOL_LAYERS, b, POOL_STATE, POOL_WIDTH), jnp.float32)
    y_prompt, ssm_p, conv_p, pool_p = trunk(x_prompt, c_prompt, 0, ssm0, conv0, pool0, *weights)
    y_sample, ssm_s, conv_s, pool_s = trunk(x_sample, c_sample, PAST_LEN, state_ssm, state_conv,
                                            state_pool, *weights)
    return (y_prompt, y_sample, ssm_p, conv_p, pool_p, ssm_s, conv_s, pool_s)
```

```python
import numpy as np
import concourse.bass as bass
import concourse.mybir as mybir
from concourse.bass_utils import run_bass_kernel_spmd

F32 = mybir.dt.float32
BF16 = mybir.dt.bfloat16
AF = mybir.ActivationFunctionType
ALU = mybir.AluOpType

NCORES = 8
D = 1024
NCH = 17
SC = 16
NTOK = NCH * 128
EPS = 1e-6
POOL_W = (2, 4, 8, 16)
NEG = -30000.0
SAME_SYNC = True
SEG = 4000
ND = 8
STOP = None
EACF = AF.Exp
EVAR = 0


class StopBuild(Exception):
    pass


def ckpt(name):
    if STOP == name:
        raise StopBuild()


class Tl:
    __slots__ = ("ap", "key", "lw", "rd", "bank")

    def __init__(self, ap, key=None, bank=None):
        self.ap = ap
        self.key = key if key is not None else self
        self.lw = None
        self.rd = []
        self.bank = bank


class Op:
    __slots__ = ("eng", "fn", "deps", "dma", "idx")


class Sched:
    ENG = ("pe", "act", "dve", "pool", "sp")

    def __init__(self):
        self.ops = []
        self.bar = {}
        self.lastop = {}
        self.dma_since = []
        self.out_dmas = []
        self.bank_last = {}

    def add(self, eng, fn, r=(), w=(), dma=0, nobar=False, out=False):
        o = Op()
        o.idx = len(self.ops)
        o.eng = eng
        o.fn = fn
        o.dma = dma
        deps = set()
        for t in r:
            k = t.key
            if k.lw is not None:
                deps.add(k.lw)
        for t in w:
            k = t.key
            if k.lw is not None:
                deps.add(k.lw)
            deps.update(k.rd)
        for t in r:
            t.key.rd.append(o.idx)
        for t in w:
            k = t.key
            k.lw = o.idx
            k.rd = []
        for t in list(r) + list(w):
            if t.bank is not None:
                bl = self.bank_last.setdefault(t.bank, {})
                for e2, i2 in bl.items():
                    if e2 != eng:
                        deps.add(i2)
        for t in list(r) + list(w):
            if t.bank is not None:
                self.bank_last[t.bank][eng] = o.idx
        if not nobar and eng in self.bar:
            deps.update(self.bar.pop(eng))
        deps.discard(o.idx)
        o.deps = deps
        self.ops.append(o)
        self.lastop[eng] = o.idx
        if dma:
            self.dma_since.append(o.idx)
            if out:
                self.out_dmas.append(o.idx)
        return o.idx

    def barrier(self):
        deps = set(self.lastop.values()) | set(self.dma_since)
        self.dma_since = []
        for e in self.ENG:
            self.bar[e] = set(deps) | self.bar.get(e, set())

    def emit(self, nc):
        ops = self.ops
        needs = [False] * len(ops)
        for o in ops:
            for d in o.deps:
                po = ops[d]
                if po.dma:
                    continue
                if po.eng != o.eng or SAME_SYNC:
                    needs[d] = True
        cnt = {e: 0 for e in self.ENG}
        val = {}
        for o in ops:
            if o.dma:
                continue
            if needs[o.idx]:
                cnt[o.eng] += 1
                val[o.idx] = cnt[o.eng]
        esems = {e: [nc.alloc_semaphore(f"c_{e}_{i}") for i in range(cnt[e] // SEG + 1)] for e in self.ENG}
        dpool = {e: [nc.alloc_semaphore(f"d_{e}_{i}") for i in range(ND)] for e in ("sp", "pool", "act")}
        dtot = {e: [0] * ND for e in dpool}
        dcount = {e: 0 for e in dpool}
        dinfo = {}
        for o in ops:
            if o.dma:
                j = dcount[o.eng] % ND
                dcount[o.eng] += 1
                prev = dtot[o.eng][j]
                dtot[o.eng][j] = prev + 16 * o.dma
                dinfo[o.idx] = (dpool[o.eng][j], prev, dtot[o.eng][j])
        by_eng = {e: [o for o in ops if o.eng == e] for e in self.ENG}

        def run(eng_name, e):
            waited_c = {x: 0 for x in self.ENG}
            waited_d = {}
            for o in by_eng[eng_name]:
                for d in sorted(o.deps):
                    po = ops[d]
                    if po.dma:
                        sem, _, tot = dinfo[d]
                        if waited_d.get(sem.num if hasattr(sem, "num") else id(sem), 0) < tot:
                            e.wait_ge(sem, tot)
                            waited_d[sem.num if hasattr(sem, "num") else id(sem)] = tot
                    elif po.eng != eng_name or SAME_SYNC:
                        n = val[d]
                        if waited_c[po.eng] < n:
                            e.wait_ge(esems[po.eng][(n - 1) // SEG], (n - 1) % SEG + 1)
                            waited_c[po.eng] = n
                if o.dma:
                    sem, prev, tot = dinfo[o.idx]
                    key = sem.num if hasattr(sem, "num") else id(sem)
                    if prev > 0 and waited_d.get(key, 0) < prev:
                        e.wait_ge(sem, prev)
                        waited_d[key] = prev
                    ins = o.fn(e)
                    if not isinstance(ins, (list, tuple)):
                        ins = [ins]
                    assert len(ins) == o.dma, (len(ins), o.dma)
                    for i_ in ins:
                        i_.then_inc(sem, 16)
                else:
                    ins = o.fn(e)
                    if isinstance(ins, (list, tuple)):
                        ins = ins[-1]
                    if needs[o.idx] and ins is not None:
                        n = val[o.idx]
                        ins.then_inc(esems[eng_name][(n - 1) // SEG], 1)

        with nc.Block() as block:
            @block.tensor
            def _(e):
                run("pe", e)

            @block.scalar
            def _(e):
                run("act", e)

            @block.vector
            def _(e):
                run("dve", e)

            @block.gpsimd
            def _(e):
                run("pool", e)

            @block.sync
            def _(e):
                run("sp", e)


def _consts():
    k = np.arange(128)
    seq = k // 8
    ident = np.eye(128, dtype=np.float32)
    Lp = (k[:, None] <= k[None, :]).astype(np.float32)
    same = (seq[:, None] == seq[None, :])
    Ls = (same & (k[:, None] <= k[None, :])).astype(np.float32)
    Ss = same.astype(np.float32)
    NEGp = np.where(k[:, None] <= k[None, :], 0.0, NEG).astype(np.float32)
    NEGs = np.where(same & (k[:, None] <= k[None, :]), 0.0, NEG).astype(np.float32)
    u = np.arange(224)
    colmask = ((u[None, :] >= 96 + 8 * np.arange(4)[:, None]) & (u[None, :] < 104 + 8 * np.arange(4)[:, None])).astype(np.float32)
    colmask = np.broadcast_to(colmask.reshape(1, 4 * 224), (128, 4 * 224))
    rowmask = (seq[:, None] == np.arange(16)[None, :]).astype(np.float32)
    base = np.concatenate([ident, Lp, Ls, Ss, rowmask], axis=1)
    negs = np.concatenate([np.tile(NEGp, (1, 4)), np.tile(NEGs, (1, 4))], axis=1)
    return base.astype(np.float32), negs.astype(np.float32), np.ascontiguousarray(colmask, dtype=np.float32)


def _pool_mats():
    import ml_dtypes
    s = np.arange(128)[:, None]
    t = np.arange(128)[None, :]
    seq_s, ts = s // 8, s % 8
    seq_t, tt = t // 8, t % 8
    mats = []
    for w in POOL_W:
        band = ((t - s >= 0) & (t - s < w)).astype(np.float64)
        pcur = band / w - np.eye(128)
        cnt0 = np.minimum(w, np.arange(128) + 1)[None, :]
        p0 = band / cnt0 - np.eye(128)
        p0hi = p0.astype(np.float32).astype(ml_dtypes.bfloat16).astype(np.float64)
        p0lo = p0 - p0hi
        pprev = (s > 128 + t - w).astype(np.float64) / w
        pcs = ((seq_s == seq_t) & (tt - ts >= 0) & (tt - ts < w)).astype(np.float64) / w - np.eye(128)
        pps = []
        for hf in range(2):
            r = np.arange(128)[:, None]
            sl, j = r // 15, r % 15
            m = ((r < 120) & (seq_t == 8 * hf + sl) & (j > 15 + tt - w)).astype(np.float64) / w
            pps.append(m)
        mats += [pcur, p0hi, p0lo, pprev, pcs, pps[0], pps[1]]
    return np.concatenate(mats, axis=1).astype(np.float32)


def build_nc(NL=4):
    nc = bass.Bass("TRN2", target_bir_lowering=False)
    S = Sched()
    NSSD = (NL + 1) // 2
    NPOOL = NL // 2

    def din(name, shape):
        return nc.dram_tensor(name, list(shape), F32, kind="ExternalInput").ap()

    def dout(name, shape):
        return nc.dram_tensor(name, list(shape), F32, kind="ExternalOutput").ap()

    xin = din("xin", [NTOK, D])
    cTd = din("cT", [128, 8 * 17])
    st_ssm = din("st_ssm", [2, 8, 128, 16, 256])
    st_conv = din("st_conv", [2, 8, 128, 4 * 16 * 3])
    st_pool = din("st_pool", [2, 240, 2048])
    ada_w = din("ada_w", [4, 1024, 3072])
    ada_b = din("ada_b", [4, 3072])
    normg_d = din("normg_bc", [4, 128, 1024])
    fng_d = din("fng_bc", [128, 1024])
    wsi = din("w_ssd_in", [2, 8, 128, 8 * 768])
    wsd = din("w_ssd_dt", [2, 128, 8 * 32])
    wso = din("w_ssd_out", [2, 8, 128, 2 * 1024])
    convw_d = din("convw", [2, 128, 32 * 4])
    convb_d = din("convb", [2, 128, 32])
    vec32_d = din("vec32", [2, 128, 96])
    sng_d = din("ssd_normg_bc", [2, 128, 2048])
    wpi = din("w_pool_in", [2, 4, 128, 8 * 1024])
    wpm = din("w_pool_mix", [2, 4, 128, 4 * 512])
    wpo = din("w_pool_out", [2, 4, 128, 4 * 1024])
    psc_d = din("pscaleT", [2, 128, 16])
    cbase_d = din("cbase", [128, 528])
    cneg_d = din("cneg", [128, 1024])
    ccol_d = din("ccol", [128, 896])
    pmat_d = din("pmats", [128, 28 * 128])

    yout = dout("yout", [NTOK, D])
    o_ssm_p = dout("o_ssm_p", [2, 8, 128, 256])
    o_ssm_s = dout("o_ssm_s", [2, 8, 128, 16, 256])
    o_conv_p = dout("o_conv_p", [2, 8, 128, 12])
    o_conv_s = dout("o_conv_s", [2, 8, 128, 192])
    o_pool_p = dout("o_pool_p", [2, 15, 2048])
    o_pool_s = dout("o_pool_s", [2, 16, 15, 2048])

    def sb(name, shape, dt=F32):
        return nc.alloc_sbuf_tensor("s_" + name, list(shape), dt)

    res_t = sb("res", [128, NCH, D])
    res = [Tl(res_t[:, c, :]) for c in range(NCH)]
    hnT_t = sb("hnT", [128, 8, NTOK], BF16)
    hnT = [Tl(hnT_t[:, :, c * 128:(c + 1) * 128]) for c in range(NCH)]
    wslot_t = [sb(f"wslot{i}", [128, 8192], BF16) for i in range(2)]
    wslot = [[Tl(t[:, q * 2048:(q + 1) * 2048]) for q in range(4)] for t in wslot_t]
    cbase = sb("cbase", [128, 528])
    identf = cbase[:, 0:128]
    Lp_ap, Ls_ap, Ss_ap = cbase[:, 128:256], cbase[:, 256:384], cbase[:, 384:512]
    rowmask = cbase[:, 512:528]
    T_cbase = Tl(cbase[:])
    identb_t = sb("identb", [128, 128], BF16)
    T_identb = Tl(identb_t[:])
    cneg = sb("cneg", [128, 1024], BF16)
    T_cneg = Tl(cneg[:])
    ccol = sb("ccol", [128, 4, 224], BF16)
    T_ccol = Tl(ccol[:])
    ones_t = sb("ones", [128, 256])
    onesf, negonesf = ones_t[:, 0:128], ones_t[:, 128:256]
    T_ones = Tl(ones_t[:])
    cT = sb("cTs", [128, 8, 17])
    T_cT = Tl(cT[:])
    gt1p = sb("gt1p", [128, D]); T_gt1p = Tl(gt1p[:])
    gt1s = sb("gt1s", [128, D]); T_gt1s = Tl(gt1s[:])
    tmp = None; T_tmp = None
    big8 = sb("big8", [128, 2048])
    T_big8 = Tl(big8[:])
    pmat = big8[:].bitcast(BF16)
    lvec = sb("lvec", [128, 128 + 32 + 96 + 16])
    convw, convb, vec32, pscT = lvec[:, 0:128], lvec[:, 128:160], lvec[:, 160:256], lvec[:, 256:272]
    T_lvec = Tl(lvec[:])
    wdt_t = sb("wdt", [128, 8, 32], BF16); T_wdt = Tl(wdt_t[:])
    pp = sb("pp", [128, 5, NCH * 32])
    dt_a, dtd_a, nac_a, eac_a, dat_a = (pp[:, i, :].rearrange("p (c h) -> p c h", c=NCH) for i in range(5))
    T_pp = [Tl(pp[:, i, :]) for i in range(5)] + [None]
    dats = sb("dats", [128, 16, 32]); T_dats = Tl(dats[:])
    small = sb("small", [128, 16])
    T_ssq, T_rt, T_rstd = Tl(small[:, 0:1]), Tl(small[:, 1:2]), Tl(small[:, 2:3])
    Acol = sb("Acol", [128, 32]); T_Acol = Tl(Acol[:])

    WK = sb("wk", [128, 34 * 1024 // 4])
    wk_off = [0]

    def wk_reset():
        wk_off[0] = 0

    def wk(shape, dt=F32):
        n = int(np.prod(shape[1:]))
        nb = n * (4 if dt == F32 else 2)
        nb4 = (nb + 31) // 32 * 8
        a = WK[:, wk_off[0]:wk_off[0] + nb4]
        wk_off[0] += nb4
        assert wk_off[0] <= 34 * 1024 // 4, wk_off[0]
        if dt == BF16:
            a = a.bitcast(BF16)[:, 0:n]
        else:
            a = a[:, 0:n]
        if len(shape) == 3:
            a = a.rearrange("p (a b) -> p a b", a=shape[1])
        elif len(shape) == 4:
            a = a.rearrange("p (a b c) -> p a b c", a=shape[1], b=shape[2])
        return a

    banks = [nc.alloc_psum_tensor(f"bank{i}", [128, 512], F32) for i in range(6)]
    bankGH = nc.alloc_psum_tensor("bankGH", [128, 1024], F32)

    TB = [Tl(b_[:], bank=i_) for i_, b_ in enumerate(banks)]

    def v3(ap, a):
        return ap.rearrange("p (a b) -> p a b", a=a)

    def bc_last(ap2, n):
        return ap2.unsqueeze(2).to_broadcast([128, ap2.shape[1], n])

    S.add("sp", lambda e: e.dma_start(out=cbase[:], in_=cbase_d), w=[T_cbase], dma=1)
    S.add("pool", lambda e: e.dma_start(out=cneg[:], in_=cneg_d), w=[T_cneg], dma=1)
    S.add("pool", lambda e: e.dma_start(out=ccol[:], in_=ccol_d.rearrange("p (a b) -> p a b", a=4)), w=[T_ccol], dma=1)
    S.add("sp", lambda e: e.dma_start(out=cT[:].rearrange("p a b -> p (a b)"), in_=cTd), w=[T_cT], dma=1)

    def _ones(e):
        e.memset(onesf, 1.0)
        return e.memset(negonesf, -1.0)
    S.add("dve", _ones, w=[T_ones])
    S.add("act", lambda e: e.activation(out=identb_t[:], in_=identf, func=AF.Identity), r=[T_cbase], w=[T_identb])
    for c in range(NCH):
        S.add("sp", lambda e, c=c: e.dma_start(out=res_t[:, c, :], in_=xin[c * 128:(c + 1) * 128, :]), w=[res[c]], dma=1)

    def rms_stats(src_tl, src_ap, junk_tl, junk_ap):
        S.add("act", lambda e: e.activation(out=junk_ap, in_=src_ap, func=AF.Square, accum_out=small[:, 0:1]),
              r=[src_tl], w=[junk_tl, T_ssq])
        S.add("act", lambda e: e.activation(out=small[:, 1:2], in_=small[:, 0:1], func=AF.Sqrt, scale=1.0 / D, bias=small[:, 8:9]),
              r=[T_ssq, T_eps], w=[T_rt])
        S.add("dve", lambda e: e.reciprocal(out=small[:, 2:3], in_=small[:, 1:2]), r=[T_rt], w=[T_rstd])

    T_eps = Tl(small[:, 8:10])

    def _eps(e):
        e.memset(small[:, 8:9], EPS)
        return e.memset(small[:, 9:10], 1.0)
    S.add("dve", _eps, w=[T_eps])

    wctr = [0]

    def next_slot():
        i = wctr[0] % 2
        wctr[0] += 1
        return i

    try:
        _layers(locals())
    except StopBuild:
        S.barrier()
    _final(locals())
    return nc


def _layers(L_):
    globals().update({k: v for k, v in L_.items() if not k.startswith("__")})
    L_ = dict(L_)
    for li in range(NL):
        is_ssd = (li % 2 == 0)
        lj = li // 2
        wk_reset()
        G_p, SH_p, G_s, SH_s = (wk([128, D]) for _ in range(4))
        T_G_p, T_SH_p, T_G_s, T_SH_s = Tl(G_p), Tl(SH_p), Tl(G_s), Tl(SH_s)
        scTp = wk([128, 8, 128], BF16); T_scTp = Tl(scTp)
        scTs = wk([128, 8, 128], BF16); T_scTs = Tl(scTs)
        hnb = wk([128, D], BF16); T_hnb = Tl(hnb)
        junk, T_junk = hnb, T_hnb
        adab1 = wk([128, 512]); adab = [adab1, adab1]
        T_adab1 = Tl(adab1); T_adab = [T_adab1, T_adab1]
        scr2 = wk([128, NCH * 32]); scr_a = scr2.rearrange("p (c h) -> p c h", c=NCH)
        T_pp[5] = Tl(scr2)
        Dg = wk([128, 16, 32]); T_Dg = Tl(Dg)
        tmpA = wk([128, D]); T_tmpA = Tl(tmpA)
        normg = big8[:, 0:1024]
        S.add("sp", lambda e, li=li: e.dma_start(out=normg, in_=normg_d[li]), w=[T_big8], dma=1)
        S.add("act", lambda e: e.activation(out=scTp, in_=cT[:, :, 0:1].to_broadcast([128, 8, 128]), func=AF.Silu),
              r=[T_cT], w=[T_scTp])
        S.add("act", lambda e: e.activation(out=scTs.rearrange("p k (s t) -> p k s t", s=16),
                                            in_=cT[:, :, 1:17].unsqueeze(3).to_broadcast([128, 8, 16, 8]), func=AF.Silu),
              r=[T_cT], w=[T_scTs])
        psA = [TB[0], TB[1]]
        psB = [TB[2], TB[3]]
        for nb in range(6):
            si = next_slot()
            wv = wslot_t[si][:, 0:4096].rearrange("p (k n) -> p k n", k=8)
            S.add("pool", lambda e, wv=wv, li=li, nb=nb: e.dma_start(
                out=wv, in_=ada_w[li].rearrange("(k p) n -> p k n", p=128)[:, :, nb * 512:(nb + 1) * 512]),
                w=wslot[si][0:2], dma=1, nobar=True)
            ab = nb % 2
            S.add("sp", lambda e, ab=ab, li=li, nb=nb: e.dma_start(out=adab[ab][0:1, :], in_=ada_b[li:li + 1, nb * 512:(nb + 1) * 512]),
                  w=[T_adab[ab]], dma=1, nobar=True)
            for which, (lhs, T_lhs, pst, pbank) in enumerate(((scTp, T_scTp, psA[nb % 2], banks[nb % 2]),
                                                               (scTs, T_scTs, psB[nb % 2], banks[2 + nb % 2]))):
                def _mm(e, lhs=lhs, pbank=pbank, wv=wv, ab=ab):
                    e.matmul(pbank[:], lhsT=onesf[0:1, :], rhs=adab[ab][0:1, :], start=True, stop=False)
                    for k in range(8):
                        ins = e.matmul(pbank[:], lhsT=lhs[:, k, :], rhs=wv[:, k, :], start=False, stop=(k == 7))
                    return ins
                S.add("pe", _mm, r=[T_lhs, T_adab[ab], T_ones] + wslot[si][0:2], w=[pst])
                blk = slice((nb % 2) * 512, (nb % 2) * 512 + 512)
                if nb < 2:
                    dst, T_dst = (SH_p, T_SH_p) if which == 0 else (SH_s, T_SH_s)
                    S.add("act", lambda e, dst=dst, pbank=pbank, blk=blk: e.activation(out=dst[:, blk], in_=pbank[:], func=AF.Identity),
                          r=[pst], w=[T_dst])
                elif nb < 4:
                    dst, T_dst = (G_p, T_G_p) if which == 0 else (G_s, T_G_s)
                    S.add("dve", lambda e, dst=dst, pbank=pbank, blk=blk: e.scalar_tensor_tensor(
                        out=dst[:, blk], in0=pbank[:], scalar=1.0, in1=normg[:, blk], op0=ALU.add, op1=ALU.mult),
                        r=[pst, T_big8], w=[T_dst])
                else:
                    dst, T_dst = (gt1p, T_gt1p) if which == 0 else (gt1s, T_gt1s)
                    S.add("dve", lambda e, dst=dst, pbank=pbank, blk=blk: e.tensor_scalar(
                        out=dst[:, blk], in0=pbank[:], scalar1=1.0, scalar2=None, op0=ALU.add),
                        r=[pst], w=[T_dst])
        ckpt("MOD%d" % li)
        psT = [TB[4], TB[5]]
        for c in range(NCH):
            G, T_G, SH, T_SH = (G_p, T_G_p, SH_p, T_SH_p) if c < SC else (G_s, T_G_s, SH_s, T_SH_s)
            rms_stats(res[c], res_t[:, c, :], T_junk, junk)
            S.add("dve", lambda e, c=c, G=G: e.scalar_tensor_tensor(out=tmpA, in0=res_t[:, c, :], scalar=small[:, 2:3], in1=G,
                                                                   op0=ALU.mult, op1=ALU.mult),
                  r=[res[c], T_rstd, T_G], w=[T_tmpA])
            S.add("pool", lambda e, SH=SH: e.tensor_tensor(out=hnb, in0=tmpA, in1=SH, op=ALU.add), r=[T_tmpA, T_SH], w=[T_hnb])
            pb = banks[4 + c % 2]
            pbv = pb[:].bitcast(BF16)

            def _tr(e, pbv=pbv):
                for k in range(8):
                    ins = e.transpose(out=pbv[:, k * 128:(k + 1) * 128], in_=hnb[:, k * 128:(k + 1) * 128], identity=identb_t[:])
                return ins
            S.add("pe", _tr, r=[T_hnb, T_identb], w=[psT[c % 2]])
            S.add("act", lambda e, c=c, pbv=pbv: e.activation(out=hnT_t[:, :, c * 128:(c + 1) * 128],
                                                             in_=pbv[:, 0:1024].rearrange("p (k t) -> p k t", k=8), func=AF.Identity),
                  r=[psT[c % 2]], w=[hnT[c]])

        ckpt("A%d" % li)
        if is_ssd:
            S.add("sp", lambda e, lj=lj: e.dma_start(out=convw, in_=convw_d[lj]), w=[T_lvec], dma=1)
            S.add("sp", lambda e, lj=lj: e.dma_start(out=convb, in_=convb_d[lj]), w=[T_lvec], dma=1)
            S.add("sp", lambda e, lj=lj: e.dma_start(out=vec32, in_=vec32_d[lj]), w=[T_lvec], dma=1)
            S.add("pool", lambda e, lj=lj: e.dma_start(out=wdt_t[:].rearrange("p a b -> p (a b)"), in_=wsd[lj]), w=[T_wdt], dma=1)
            dtb, alog, Dcol = vec32[:, 0:32], vec32[:, 32:64], vec32[:, 64:96]
            S.add("act", lambda e: e.activation(out=Acol[:], in_=alog, func=AF.Exp), r=[T_lvec], w=[T_Acol])
            S.add("dve", lambda e: e.tensor_scalar(out=Acol[:], in0=Acol[:], scalar1=-1.0, scalar2=None, op0=ALU.mult),
                  r=[T_Acol], w=[T_Acol])
            T_psd = [TB[0], TB[1]]
            pd0 = banks[0][:, 0:512].rearrange("p (c h) -> p c h", c=16)
            pd1 = banks[1][:, 0:32]

            def _dtmm(e):
                for c in range(NCH):
                    o = pd0[:, c, :] if c < 16 else pd1
                    for k in range(8):
                        ins = e.matmul(o, lhsT=hnT_t[:, k, c * 128:(c + 1) * 128], rhs=wdt_t[:, k, :], start=(k == 0), stop=(k == 7))
                return ins
            S.add("pe", _dtmm, r=hnT + [T_wdt], w=T_psd)
            ckpt("P1")
            def _v(e):
                e.tensor_tensor(out=scr_a[:, 0:16, :], in0=pd0, in1=dtb.unsqueeze(1).to_broadcast([128, 16, 32]), op=ALU.add)
                return e.tensor_tensor(out=scr_a[:, 16, :], in0=pd1, in1=dtb, op=ALU.add)
            S.add("dve", _v, r=T_psd + [T_lvec], w=[T_pp[5]])
            S.add("act", lambda e: e.activation(out=scr2, in_=scr2, func=AF.Exp), r=[T_pp[5]], w=[T_pp[5]])
            S.add("act", lambda e: e.activation(out=pp[:, 0, :], in_=scr2, func=AF.Ln, bias=small[:, 9:10]),
                  r=[T_pp[5], T_eps], w=[T_pp[0]])
            S.add("dve", lambda e: e.tensor_tensor(out=scr_a, in0=dt_a, in1=Acol[:].unsqueeze(1).to_broadcast([128, NCH, 32]), op=ALU.mult),
                  r=[T_pp[0], T_Acol], w=[T_pp[5]])
            ckpt("P2")
            T_pac = [TB[2], TB[3]]
            T_pat = [TB[4], TB[5]]
            pa0 = banks[2][:, 0:512].rearrange("p (c h) -> p c h", c=16); pa1 = banks[3][:, 0:32]
            pt0 = banks[4][:, 0:512].rearrange("p (c h) -> p c h", c=16); pt1 = banks[5][:, 0:32]

            def _acmm(e):
                for c in range(NCH):
                    oa = pa0[:, c, :] if c < 16 else pa1
                    ot = pt0[:, c, :] if c < 16 else pt1
                    e.matmul(oa, lhsT=(Lp_ap if c < 16 else Ls_ap), rhs=scr_a[:, c, :], start=True, stop=True)
                    ins = e.matmul(ot, lhsT=(onesf if c < 16 else Ss_ap), rhs=scr_a[:, c, :], start=True, stop=True)
                return ins
            S.add("pe", _acmm, r=[T_pp[5], T_cbase, T_ones], w=T_pac + T_pat)
            ckpt("P3")

            def _nac(e):
                e.tensor_scalar(out=nac_a[:, 0:16, :], in0=pa0, scalar1=-1.0, scalar2=None, op0=ALU.mult)
                return e.tensor_scalar(out=nac_a[:, 16, :], in0=pa1, scalar1=-1.0, scalar2=None, op0=ALU.mult)
            S.add("dve", _nac, r=T_pac, w=[T_pp[2]])
            ckpt("P3a")

            def _eac(e):
                if EVAR == 1:
                    return e.activation(out=eac_a[:, 0:16, :], in_=pa0, func=EACF)
                if EVAR == 2:
                    return e.activation(out=eac_a[:, 16, :], in_=pa1, func=EACF)
                if EVAR == 4:
                    return e.activation(out=pp[:, 3, 0:512], in_=pp[:, 0, 0:512], func=EACF)
                if EVAR in (6, 7):
                    return e.activation(out=pp[:, 3, 0:512], in_=banks[2][:, 0:512], func=AF.Copy)
                if EVAR == 3:
                    return e.activation(out=pp[:, 3, 0:512], in_=banks[2][:, 0:512], func=EACF)
                e.activation(out=eac_a[:, 0:16, :], in_=pa0, func=EACF)
                e.activation(out=eac_a[:, 16, :], in_=pa1, func=EACF)
                e.activation(out=dat_a[:, 0:16, :], in_=pt0, func=EACF)
                return e.activation(out=dat_a[:, 16, :], in_=pt1, func=EACF)
            S.add("act", _eac, r=T_pac + T_pat + ([T_pp[2]] if EVAR == 7 else []), w=[T_pp[3], T_pp[4]])
            ckpt("P3b")

            def _dd(e):
                e.tensor_tensor(out=dtd_a[:, 0:16, :], in0=pt0, in1=nac_a[:, 0:16, :], op=ALU.add)
                return e.tensor_tensor(out=dtd_a[:, 16, :], in0=pt1, in1=nac_a[:, 16, :], op=ALU.add)
            S.add("dve", _dd, r=T_pat + [T_pp[2]], w=[T_pp[1]])
            ckpt("P3c")
            S.add("act", lambda e: e.activation(out=pp[:, 1, :], in_=pp[:, 1, :], func=AF.Exp), r=[T_pp[1]], w=[T_pp[1]])
            S.add("dve", lambda e: e.tensor_tensor(out=pp[:, 1, :], in0=pp[:, 1, :], in1=pp[:, 0, :], op=ALU.mult),
                  r=[T_pp[1], T_pp[0]], w=[T_pp[1]])
            ckpt("P4")
            S.add("sp", lambda e, lj=lj: e.dma_start(out=big8[:], in_=sng_d[lj]), w=[T_big8], dma=1)
            S.add("dve", lambda e: e.tensor_tensor(out=Dg, in0=scr_a[:, 16:17, :].to_broadcast([128, 16, 32]),
                                                   in1=rowmask.unsqueeze(2).to_broadcast([128, 16, 32]), op=ALU.mult),
                  r=[T_pp[5], T_cbase], w=[T_Dg])
            T_pds = TB[0]
            S.add("pe", lambda e: e.matmul(banks[0][:], lhsT=onesf, rhs=Dg.rearrange("p a b -> p (a b)"), start=True, stop=True),
                  r=[T_Dg, T_ones], w=[T_pds])
            S.add("act", lambda e: e.activation(out=dats[:].rearrange("p a b -> p (a b)"), in_=banks[0][:], func=AF.Exp),
                  r=[T_pds], w=[T_dats])
            S.barrier()
            ckpt("PRE%d" % li)
            L2 = dict(L_); L2.update(locals())
            ssd_layer(nc, S, L2, lj)
        else:
            S.add("sp", lambda e, lj=lj: e.dma_start(out=pscT, in_=psc_d[lj]), w=[T_lvec], dma=1)
            S.add("pool", lambda e: e.dma_start(out=pmat[:, 0:3584].rearrange("p (a b) -> p a b", a=28), in_=pmat_d.rearrange("p (a b) -> p a b", a=28)), w=[T_big8], dma=1)
            S.barrier()
            ckpt("PRE%d" % li)
            L2 = dict(L_); L2.update(locals())
            pool_layer(nc, S, L2, lj)
        S.barrier()
        ckpt("L%d" % li)


def _final(L_):
    globals().update({k: v for k, v in L_.items() if not k.startswith("__")})
    wk_reset()
    junk = wk([128, D], BF16); T_junk = Tl(junk)
    yb = [wk([128, D]) for _ in range(2)]; T_yb = [Tl(a) for a in yb]
    S.add("sp", lambda e: e.dma_start(out=big8[:, 0:1024], in_=fng_d), w=[T_big8], dma=1)
    for c in range(NCH):
        rms_stats(res[c], res_t[:, c, :], T_junk, junk)
        S.add("dve", lambda e, c=c: e.scalar_tensor_tensor(out=yb[c % 2], in0=res_t[:, c, :], scalar=small[:, 2:3], in1=big8[:, 0:1024],
                                                           op0=ALU.mult, op1=ALU.mult),
              r=[res[c], T_rstd, T_big8], w=[T_yb[c % 2]])
        S.add("sp", lambda e, c=c: e.dma_start(out=yout[c * 128:(c + 1) * 128, :], in_=yb[c % 2]), r=[T_yb[c % 2]], dma=1, out=True)
    o = Op(); o.idx = len(S.ops); o.eng = "sp"; o.fn = lambda e: None
    o.dma = 0; o.deps = set(S.out_dmas)
    S.ops.append(o)
    S.emit(nc)


def _cat_lvec(lj):
    return None


def ssd_layer(nc, S, L, lj):
    g_ = L
    (banks, bankGH, wk, wk_reset, wslot, wslot_t, next_slot, hnT, hnT_t, res, res_t, tmp, T_tmp) = (
        g_[k] for k in ("banks", "bankGH", "wk", "wk_reset", "wslot", "wslot_t", "next_slot", "hnT", "hnT_t", "res", "res_t", "tmp", "T_tmp"))
    identf, identb_t, T_identb, T_cbase, rowmask = g_["identf"], g_["identb_t"], g_["T_identb"], g_["T_cbase"], g_["rowmask"]
    cneg, T_cneg, ccol, T_ccol = g_["cneg"], g_["T_cneg"], g_["ccol"], g_["T_ccol"]
    negonesf, T_ones = g_["negonesf"], g_["T_ones"]
    convw, convb, vec32, T_lvec = g_["convw"], g_["convb"], g_["vec32"], g_["T_lvec"]
    dt_a, dtd_a, nac_a, eac_a, dat_a = g_["dt_a"], g_["dtd_a"], g_["nac_a"], g_["eac_a"], g_["dat_a"]
    T_pp, dats, T_dats = g_["T_pp"], g_["dats"], g_["T_dats"]
    gt1p, T_gt1p, gt1s, T_gt1s = g_["gt1p"], g_["T_gt1p"], g_["gt1s"], g_["T_gt1s"]
    big8, T_big8, small, T_eps = g_["big8"], g_["T_big8"], g_["small"], g_["T_eps"]
    st_ssm, st_conv, wsi, wso = g_["st_ssm"], g_["st_conv"], g_["wsi"], g_["wso"]
    o_ssm_p, o_ssm_s, o_conv_p, o_conv_s = g_["o_ssm_p"], g_["o_ssm_s"], g_["o_conv_p"], g_["o_conv_s"]
    Dcol = vec32[:, 64:96]

    def v3(ap, a):
        return ap.rearrange("p (a b) -> p a b", a=a)

    wk_reset()
    xps = wk([128, 4, 16, 11]); T_xp = Tl(xps)
    xp = xps.rearrange("p a b c -> p (a b c)")[:, 0:4 * 131].rearrange("p (a b) -> p a b", a=4)
    acc = wk([128, 4, 128]); T_acc = Tl(acc)
    xa = wk([128, 4, 128], BF16); T_xa = Tl(xa)
    xs = wk([128, 256], BF16); T_xs = Tl(xs)
    xdt = wk([128, 256], BF16); T_xdt = Tl(xdt)
    xdtd = wk([128, 256], BF16); T_xdtd = Tl(xdtd)
    Bsb = wk([128, 128], BF16); T_Bsb = Tl(Bsb)
    Dexp = wk([128, 4, 128]); T_Dexp = Tl(Dexp)
    Eb = wk([128, 4, 128]); T_E = Tl(Eb)
    MT = wk([128, 4, 128], BF16); T_MT = Tl(MT)
    y1 = wk([128, 256]); T_y1 = Tl(y1)
    sz = wk([128, 256]); T_sz = Tl(sz)
    junk = sz; T_junk = T_sz
    gn = wk([128, 256], BF16); T_gn = Tl(gn)
    gT = wk([128, 2, 128], BF16); T_gT = Tl(gT)
    hT = wk([128, 256]); T_hT = Tl(hT)
    hTb = wk([128, 256], BF16); T_hTb = Tl(hTb)
    h0f = [wk([128, 4, 256]) for _ in range(2)]; T_h0f = [Tl(a) for a in h0f]
    h0b = [wk([128, 4, 256], BF16) for _ in range(2)]; T_h0b = [Tl(a) for a in h0b]
    Bm = [wk([128, 4, 128], BF16) for _ in range(2)]; T_Bm = [Tl(a) for a in Bm]
    CTm = [wk([128, 4, 128], BF16) for _ in range(2)]; T_CTm = [Tl(a) for a in CTm]
    cvs = wk([128, 192]); T_cvs = Tl(cvs)
    cvp = wk([128, 12]); T_cvp = Tl(cvp)
    sq = wk([128, 4]); T_sq = Tl(sq)

    pA = Tl(banks[0][:], bank=0)
    pZ = Tl(banks[1][:, 0:256], bank=1); pCB = Tl(banks[1][:, 256:384], bank=1)
    pC = banks[2][:].bitcast(BF16)
    pTx = Tl(pC[:, 0:384], bank=2); pTg = Tl(pC[:, 512:768], bank=2)
    pD = Tl(banks[3][:], bank=3)
    pYa = Tl(banks[4][:, 0:256], bank=4); pYb = Tl(banks[4][:, 256:512], bank=4)
    pF = Tl(banks[5][:], bank=5)
    pO = Tl(bankGH[:], bank=6)
    hcnt = [0]

    for g in range(8):
        g4 = g * 4
        si = next_slot()
        win = wslot_t[si][:, 0:6144].rearrange("p (k n) -> p k n", k=8)
        wout = wslot_t[si][:, 6144:8192].rearrange("p (j n) -> p j n", j=2)
        T_wi = wslot[si][0:3]
        T_wo = wslot[si][3:4]
        S.add("pool", lambda e, win=win, g=g: e.dma_start(out=win, in_=wsi[lj, g].rearrange("p (k n) -> p k n", k=8)),
              w=T_wi, dma=1, nobar=True)
        S.add("pool", lambda e, wout=wout, g=g: e.dma_start(out=wout, in_=wso[lj, g].rearrange("p (j n) -> p j n", j=2)),
              w=T_wo, dma=1, nobar=True)
        cw = v3(convw, 32)
        for c in [SC] + list(range(16)):
            samp = (c == SC)
            tok = slice(c * 128, (c + 1) * 128)
            def _xbc(e, win=win, tok=tok):
                for j in range(4):
                    for k in range(8):
                        ins = e.matmul(banks[0][:, j * 128:(j + 1) * 128], lhsT=win[:, k, j * 128:(j + 1) * 128], rhs=hnT_t[:, k, tok],
                                       start=(k == 0), stop=(k == 7))
                return ins
            S.add("pe", _xbc, r=T_wi + [hnT[c]], w=[pA])
            def _z(e, win=win, tok=tok):
                for k in range(8):
                    ins = e.matmul(banks[1][:, 0:256], lhsT=hnT_t[:, k, tok], rhs=win[:, k, 512:768], start=(k == 0), stop=(k == 7))
                return ins
            S.add("pe", _z, r=T_wi + [hnT[c]], w=[pZ])
            if samp:
                S.add("sp", lambda e, g=g: e.dma_start(out=cvs, in_=st_conv[lj, g]), w=[T_cvs], dma=1)
                S.add("pool", lambda e: e.tensor_copy(out=xps[:, :, :, 0:3], in_=cvs.rearrange("p (a b c) -> p a b c", a=4, b=16)),
                      r=[T_cvs], w=[T_xp])
                S.add("act", lambda e: e.activation(out=xps[:, :, :, 3:11], in_=banks[0][:].rearrange("p (a b c) -> p a b c", a=4, b=16),
                                                    func=AF.Identity), r=[pA], w=[T_xp])
                S.add("pool", lambda e: e.tensor_copy(out=cvs.rearrange("p (a b c) -> p a b c", a=4, b=16), in_=xps[:, :, :, 8:11]),
                      r=[T_xp], w=[T_cvs])
                S.add("sp", lambda e, g=g: e.dma_start(out=o_conv_s[lj, g], in_=cvs), r=[T_cvs], dma=1, out=True)
                src = lambda j, k: xps[:, j, :, k:k + 8]
                accv = lambda j: acc[:, j, :].rearrange("p (a b) -> p a b", a=16)
            else:
                if c == 0:
                    S.add("pool", lambda e: e.memset(xp[:, :, 0:3], 0.0), w=[T_xp])
                S.add("act", lambda e: e.activation(out=xp[:, :, 3:131], in_=v3(banks[0][:], 4), func=AF.Identity), r=[pA], w=[T_xp])
                src = lambda j, k: xp[:, j, k:k + 128]
                accv = lambda j: acc[:, j, :]

            for k in range(4):
                def _conv(e, src=src, accv=accv, g4=g4, k=k):
                    for j in range(4):
                        ti = g4 + j
                        if k == 0:
                            ins = e.tensor_scalar(out=accv(j), in0=src(j, 0), scalar1=cw[:, ti, 0:1], scalar2=convb[:, ti:ti + 1], op0=ALU.mult, op1=ALU.add)
                        else:
                            ins = e.scalar_tensor_tensor(out=accv(j), in0=src(j, k), scalar=cw[:, ti, k:k + 1], in1=accv(j), op0=ALU.mult, op1=ALU.add)
                    return ins
                S.add("dve", _conv, r=[T_xp, T_lvec] + ([T_acc] if k else []), w=[T_acc])
            if not samp:
                if c == 15:
                    S.add("pool", lambda e: e.tensor_copy(out=v3(cvp, 4), in_=xp[:, :, 128:131]), r=[T_xp], w=[T_cvp])
                    S.add("sp", lambda e, g=g: e.dma_start(out=o_conv_p[lj, g], in_=cvp), r=[T_cvp], dma=1, out=True)
                else:
                    S.add("pool", lambda e: e.tensor_copy(out=xp[:, :, 0:3], in_=xp[:, :, 128:131]), r=[T_xp], w=[T_xp])
            S.add("act", lambda e: e.activation(out=xa, in_=acc, func=AF.Silu), r=[T_acc], w=[T_xa])
            def _trx(e):
                e.transpose(out=pC[:, 0:128], in_=xa[:, 0, :], identity=identb_t[:])
                e.transpose(out=pC[:, 128:256], in_=xa[:, 1, :], identity=identb_t[:])
                return e.transpose(out=pC[:, 256:384], in_=xa[:, 2, :], identity=identb_t[:])
            S.add("pe", _trx, r=[T_xa, T_identb], w=[pTx])
            S.add("act", lambda e: e.activation(out=xs, in_=pC[:, 0:256], func=AF.Identity), r=[pTx], w=[T_xs])
            S.add("act", lambda e: e.activation(out=Bsb, in_=pC[:, 256:384], func=AF.Identity), r=[pTx], w=[T_Bsb])
            S.add("dve", lambda e, c=c, g4=g4: e.tensor_tensor(out=v3(xdt, 4), in0=v3(pC[:, 0:256], 4),
                                                              in1=dt_a[:, c, g4:g4 + 4].unsqueeze(2).to_broadcast([128, 4, 64]), op=ALU.mult),
                  r=[pTx, T_pp[0]], w=[T_xdt])
            S.add("dve", lambda e, c=c, g4=g4: e.tensor_tensor(out=v3(xdtd, 4), in0=v3(pC[:, 0:256], 4),
                                                              in1=dtd_a[:, c, g4:g4 + 4].unsqueeze(2).to_broadcast([128, 4, 64]), op=ALU.mult),
                  r=[pTx, T_pp[1]], w=[T_xdtd])
            S.add("pe", lambda e: e.matmul(banks[1][:, 256:384], lhsT=xa[:, 2, :], rhs=xa[:, 3, :], start=True, stop=True),
                  r=[T_xa], w=[pCB])
            S.add("dve", lambda e, c=c, g4=g4: e.tensor_tensor(out=Dexp, in0=identf.unsqueeze(1).to_broadcast([128, 4, 128]),
                                                              in1=nac_a[:, c, g4:g4 + 4].unsqueeze(2).to_broadcast([128, 4, 128]), op=ALU.mult),
                  r=[T_cbase, T_pp[2]], w=[T_Dexp])
            ncol = slice(512, 1024) if samp else slice(0, 512)

            def _seg(e, ncol=ncol):
                e.matmul(banks[3][:], lhsT=negonesf, rhs=Dexp.rearrange("p a b -> p (a b)"), start=True, stop=False)
                return e.matmul(banks[3][:], lhsT=identb_t[:], rhs=cneg[:, ncol], start=False, stop=True)
            S.add("pe", _seg, r=[T_Dexp, T_ones, T_identb, T_cneg], w=[pD])

            def _E(e, c=c, g4=g4):
                for h in range(4):
                    ins = e.activation(out=Eb[:, h, :], in_=banks[3][:, h * 128:(h + 1) * 128], func=AF.Exp, bias=nac_a[:, c, g4 + h:g4 + h + 1])
                return ins
            S.add("act", _E, r=[pD, T_pp[2]], w=[T_E])
            S.add("dve", lambda e: e.tensor_tensor(out=MT, in0=Eb, in1=banks[1][:, 256:384].unsqueeze(1).to_broadcast([128, 4, 128]), op=ALU.mult),
                  r=[T_E, pCB], w=[T_MT])
            def _yi(e):
                for h in range(4):
                    ins = e.matmul(banks[4][:, h * 64:(h + 1) * 64], lhsT=MT[:, h, :], rhs=xdt[:, h * 64:(h + 1) * 64], start=True, stop=True)
                return ins
            S.add("pe", _yi, r=[T_MT, T_xdt], w=[pYa])
            if not samp:
                if c == 0:
                    S.add("pool", lambda e: e.memset(hT, 0.0), w=[T_hT])
                    S.add("pool", lambda e: e.memset(hTb, 0.0), w=[T_hTb])
                S.add("pe", lambda e: e.matmul(banks[4][:, 256:512], lhsT=xa[:, 3, :], rhs=hTb, start=True, stop=True), r=[T_xa, T_hTb], w=[pYb])
                S.add("pe", lambda e: e.matmul(banks[5][:, 0:256], lhsT=Bsb, rhs=xdtd, start=True, stop=True), r=[T_Bsb, T_xdtd], w=[pF])
                S.add("dve", lambda e, c=c, g4=g4: e.tensor_tensor(out=v3(hT, 4), in0=v3(hT, 4),
                                                                  in1=dat_a[:, c, g4:g4 + 4].unsqueeze(2).to_broadcast([128, 4, 64]), op=ALU.mult),
                      r=[T_hT, T_pp[4], pYb], w=[T_hT])
                S.add("dve", lambda e: e.tensor_tensor(out=hT, in0=hT, in1=banks[5][:, 0:256], op=ALU.add), r=[T_hT, pF], w=[T_hT])
                if c == 15:
                    S.add("sp", lambda e, g=g: e.dma_start(out=o_ssm_p[lj, g], in_=hT), r=[T_hT], dma=1, out=True)
                else:
                    S.add("act", lambda e: e.activation(out=hTb, in_=hT, func=AF.Identity), r=[T_hT], w=[T_hTb])
            else:
                for pc in range(4):
                    b = hcnt[0] % 2
                    hcnt[0] += 1
                    S.add("sp", lambda e, b=b, g=g, pc=pc: e.dma_start(out=h0f[b], in_=st_ssm[lj, g, :, pc * 4:(pc + 1) * 4, :]),
                          w=[T_h0f[b]], dma=1, nobar=True)
                    S.add("pool", lambda e, b=b: e.tensor_copy(out=h0b[b], in_=h0f[b]), r=[T_h0f[b]], w=[T_h0b[b]])
                    S.add("pool", lambda e, b=b, pc=pc: e.tensor_tensor(out=CTm[b], in0=xa[:, 3, :].unsqueeze(1).to_broadcast([128, 4, 128]),
                                                                       in1=ccol[:, :, 96 - 32 * pc: 224 - 32 * pc], op=ALU.mult),
                          r=[T_xa, T_ccol], w=[T_CTm[b]])
                    S.add("pool", lambda e, b=b, pc=pc: e.tensor_tensor(out=Bm[b], in0=Bsb.unsqueeze(1).to_broadcast([128, 4, 128]),
                                                                       in1=rowmask[:, pc * 4:(pc + 1) * 4].unsqueeze(2).to_broadcast([128, 4, 128]), op=ALU.mult),
                          r=[T_Bsb, T_cbase], w=[T_Bm[b]])

                    def _yis(e, b=b, pc=pc):
                        for s_ in range(4):
                            ins = e.matmul(banks[4][:, 256:512], lhsT=CTm[b][:, s_, :], rhs=h0b[b][:, s_, :],
                                           start=(pc == 0 and s_ == 0), stop=(pc == 3 and s_ == 3))
                        return ins
                    S.add("pe", _yis, r=[T_CTm[b], T_h0b[b]], w=[pYb])
                    for hp in range(2):
                        def _sts(e, b=b, hp=hp):
                            for s_ in range(2):
                                ins = e.matmul(banks[5][:, s_ * 256:(s_ + 1) * 256], lhsT=Bm[b][:, hp * 2 + s_, :], rhs=xdtd, start=True, stop=True)
                            return ins
                        S.add("pe", _sts, r=[T_Bm[b], T_xdtd], w=[pF])
                        seq0 = pc * 4 + hp * 2
                        hv = h0f[b][:, hp * 2:hp * 2 + 2, :].rearrange("p s (h q) -> p s h q", h=4)
                        S.add("dve", lambda e, hv=hv, seq0=seq0, g4=g4: e.tensor_tensor(
                            out=hv, in0=hv, in1=dats[:, seq0:seq0 + 2, g4:g4 + 4].unsqueeze(3).to_broadcast([128, 2, 4, 64]), op=ALU.mult),
                            r=[T_h0f[b], T_dats, T_h0b[b]], w=[T_h0f[b]])
                        hv2 = h0f[b][:, hp * 2:hp * 2 + 2, :]
                        S.add("dve", lambda e, hv2=hv2: e.tensor_tensor(out=hv2, in0=hv2, in1=banks[5][:].rearrange("p (s q) -> p s q", s=2), op=ALU.add),
                              r=[T_h0f[b], pF], w=[T_h0f[b]])
                    S.add("sp", lambda e, b=b, g=g, pc=pc: e.dma_start(out=o_ssm_s[lj, g, :, pc * 4:(pc + 1) * 4, :], in_=h0f[b]),
                          r=[T_h0f[b]], dma=1, out=True)
            S.add("dve", lambda e, c=c, g4=g4: e.tensor_tensor(out=v3(y1, 4), in0=v3(banks[4][:, 256:512], 4),
                                                              in1=eac_a[:, c, g4:g4 + 4].unsqueeze(2).to_broadcast([128, 4, 64]), op=ALU.mult),
                  r=[pYb, T_pp[3]], w=[T_y1])
            S.add("dve", lambda e: e.tensor_tensor(out=y1, in0=y1, in1=banks[4][:, 0:256], op=ALU.add), r=[T_y1, pYa], w=[T_y1])

            def _dsk(e, g4=g4):
                for h in range(4):
                    hs = slice(h * 64, (h + 1) * 64)
                    ins = e.scalar_tensor_tensor(out=y1[:, hs], in0=xs[:, hs], scalar=Dcol[:, g4 + h:g4 + h + 1], in1=y1[:, hs], op0=ALU.mult, op1=ALU.add)
                return ins
            S.add("dve", _dsk, r=[T_xs, T_y1, T_lvec], w=[T_y1])
            S.add("act", lambda e: e.activation(out=sz, in_=banks[1][:, 0:256], func=AF.Silu), r=[pZ], w=[T_sz])
            S.add("dve", lambda e: e.tensor_tensor(out=y1, in0=y1, in1=sz, op=ALU.mult), r=[T_y1, T_sz], w=[T_y1])
            S.add("act", lambda e: e.activation(out=junk, in_=y1, func=AF.Square, accum_out=sq[:, 0:1]), r=[T_y1], w=[T_junk, T_sq])
            S.add("act", lambda e: e.activation(out=sq[:, 1:2], in_=sq[:, 0:1], func=AF.Sqrt, scale=1.0 / 256, bias=small[:, 8:9]),
                  r=[T_sq, T_eps], w=[T_sq])
            S.add("dve", lambda e: e.reciprocal(out=sq[:, 2:3], in_=sq[:, 1:2]), r=[T_sq], w=[T_sq])
            S.add("dve", lambda e, g=g: e.scalar_tensor_tensor(out=gn, in0=y1, scalar=sq[:, 2:3], in1=big8[:, g * 256:(g + 1) * 256],
                                                              op0=ALU.mult, op1=ALU.mult), r=[T_y1, T_sq, T_big8], w=[T_gn])

            def _trg(e):
                e.transpose(out=pC[:, 512:640], in_=gn[:, 0:128], identity=identb_t[:])
                return e.transpose(out=pC[:, 640:768], in_=gn[:, 128:256], identity=identb_t[:])
            S.add("pe", _trg, r=[T_gn, T_identb], w=[pTg])
            S.add("act", lambda e: e.activation(out=gT, in_=v3(pC[:, 512:768], 2), func=AF.Identity), r=[pTg], w=[T_gT])

            def _out(e, wout=wout):
                for half in range(2):
                    for j in range(2):
                        ins = e.matmul(bankGH[:, half * 512:(half + 1) * 512], lhsT=gT[:, j, :], rhs=wout[:, j, half * 512:(half + 1) * 512],
                                       start=(j == 0), stop=(j == 1))
                return ins
            S.add("pe", _out, r=[T_gT] + T_wo, w=[pO])
            if samp:
                S.add("dve", lambda e: e.tensor_tensor(out=bankGH[:], in0=bankGH[:], in1=gt1s[:], op=ALU.mult), r=[pO, T_gt1s], w=[pO])
                S.add("dve", lambda e, c=c: e.tensor_tensor(out=res_t[:, c, :], in0=res_t[:, c, :], in1=bankGH[:], op=ALU.add),
                      r=[res[c], pO], w=[res[c]])
                S.add("pool", lambda e, wout=wout: e.tensor_tensor(out=wout, in0=wout, in1=gt1p[:].unsqueeze(1).to_broadcast([128, 2, 1024]), op=ALU.mult),
                      r=T_wo + [T_gt1p], w=T_wo)
            else:
                S.add("dve", lambda e, c=c: e.tensor_tensor(out=res_t[:, c, :], in0=res_t[:, c, :], in1=bankGH[:], op=ALU.add),
                      r=[res[c], pO], w=[res[c]])


def pool_layer(nc, S, L, lj):
    g_ = L
    (banks, bankGH, wk, wk_reset, wslot, wslot_t, next_slot, hnT, hnT_t, res, res_t, tmp, T_tmp) = (
        g_[k] for k in ("banks", "bankGH", "wk", "wk_reset", "wslot", "wslot_t", "next_slot", "hnT", "hnT_t", "res", "res_t", "tmp", "T_tmp"))
    pmat, T_big8, pscT, T_lvec = g_["pmat"], g_["T_big8"], g_["pscT"], g_["T_lvec"]
    gt1p, T_gt1p, gt1s, T_gt1s = g_["gt1p"], g_["T_gt1p"], g_["gt1s"], g_["T_gt1s"]
    st_pool, wpi, wpm, wpo, o_pool_p, o_pool_s = g_["st_pool"], g_["wpi"], g_["wpm"], g_["wpo"], g_["o_pool_p"], g_["o_pool_s"]

    def v3(ap, a):
        return ap.rearrange("p (a b) -> p a b", a=a)

    wk_reset()
    ub = [wk([128, 512], BF16) for _ in range(2)]; T_ub = [Tl(a) for a in ub]
    uf = wk([128, 512]); T_uf = Tl(uf)
    plT = wk([128, 4, 128], BF16); T_plT = Tl(plT)
    szT = wk([128, 4, 128]); T_szT = Tl(szT)
    m2T = wk([128, 4, 128], BF16); T_m2T = Tl(m2T)
    prevS = wk([128, 2, 512], BF16); T_prevS = Tl(prevS)

    pU = Tl(banks[0][:], bank=0); pP = Tl(banks[1][:], bank=1); pM = Tl(banks[2][:], bank=2); pZ = Tl(banks[3][:], bank=3); pO = Tl(bankGH[:], bank=6)
    S.add("sp", lambda e: e.dma_start(out=o_pool_s[lj, :, 0:7, :], in_=st_pool[lj].rearrange("(s j) c -> s j c", j=15)[:, 8:15, :]),
          dma=1, out=True)
    ucnt = [0]
    ckpt("Q0")
    for g in range(4):
        if g == 1:
            ckpt("Q4")
        si = next_slot()
        win = wslot_t[si][:].rearrange("p (k n) -> p k n", k=8)
        T_wi = wslot[si][0:4]
        S.add("pool", lambda e, win=win, g=g: e.dma_start(out=win, in_=wpi[lj, g].rearrange("p (k n) -> p k n", k=8)),
              w=T_wi, dma=1, nobar=True)
        si2 = next_slot()
        wmix = wslot_t[si2][:, 0:2048].rearrange("p (k n) -> p k n", k=4)
        wout = wslot_t[si2][:, 2048:6144].rearrange("p (k n) -> p k n", k=4)
        T_wm = wslot[si2][0:1]
        T_wo = wslot[si2][1:3]
        S.add("pool", lambda e, wmix=wmix, g=g: e.dma_start(out=wmix, in_=wpm[lj, g].rearrange("p (k n) -> p k n", k=4)),
              w=T_wm, dma=1, nobar=True)
        S.add("pool", lambda e, wout=wout, g=g: e.dma_start(out=wout, in_=wpo[lj, g].rearrange("p (k n) -> p k n", k=4)),
              w=T_wo, dma=1, nobar=True)
        S.add("pool", lambda e, g=g: e.dma_start(out=prevS[0:120, :, :],
                                                 in_=st_pool[lj].rearrange("(h r) c -> r h c", h=2)[:, :, g * 512:(g + 1) * 512]),
              w=[T_prevS], dma=1)
        mb = g * 7 * 128
        Pcur, P0hi, P0lo, Pprev, PcS, PpS0, PpS1 = (pmat[:, mb + i * 128: mb + (i + 1) * 128] for i in range(7))
        prev_u = None
        for c in [SC] + list(range(16)):
            samp = (c == SC)
            tok = slice(c * 128, (c + 1) * 128)
            bi = ucnt[0] % 2
            ucnt[0] += 1
            cur = ub[bi]; T_cur = T_ub[bi]

            def _u(e, win=win, tok=tok):
                for k in range(8):
                    ins = e.matmul(banks[0][:], lhsT=hnT_t[:, k, tok], rhs=win[:, k, 0:512], start=(k == 0), stop=(k == 7))
                return ins
            S.add("pe", _u, r=T_wi + [hnT[c]], w=[pU])
            S.add("act", lambda e, cur=cur: e.activation(out=cur, in_=banks[0][:], func=AF.Identity), r=[pU], w=[T_cur])
            if samp or c == 15:
                S.add("dve", lambda e: e.tensor_copy(out=uf, in_=banks[0][:]), r=[pU], w=[T_uf])
                if samp:
                    def _us(e, g=g):
                        return [e.dma_start(out=o_pool_s[lj, s_, 7:15, g * 512:(g + 1) * 512], in_=uf[s_ * 8:(s_ + 1) * 8, :]) for s_ in range(16)]
                    S.add("sp", _us, r=[T_uf], dma=16, out=True)
                else:
                    S.add("sp", lambda e, g=g: e.dma_start(out=o_pool_p[lj, :, g * 512:(g + 1) * 512], in_=uf[113:128, :]), r=[T_uf], dma=1, out=True)
            if samp:
                def _pl(e, cur=cur, PcS=PcS, PpS0=PpS0, PpS1=PpS1):
                    for ct in range(4):
                        cs = slice(ct * 128, (ct + 1) * 128)
                        o = banks[1][:, cs]
                        e.matmul(o, lhsT=cur[:, cs], rhs=PcS, start=True, stop=False)
                        e.matmul(o, lhsT=prevS[0:120, 0, cs], rhs=PpS0[0:120, :], start=False, stop=False)
                        ins = e.matmul(o, lhsT=prevS[0:120, 1, cs], rhs=PpS1[0:120, :], start=False, stop=True)
                    return ins
                S.add("pe", _pl, r=[T_cur, T_prevS, T_big8], w=[pP])
            elif c == 0:
                def _pl(e, cur=cur, P0hi=P0hi, P0lo=P0lo):
                    for ct in range(4):
                        cs = slice(ct * 128, (ct + 1) * 128)
                        o = banks[1][:, cs]
                        e.matmul(o, lhsT=cur[:, cs], rhs=P0hi, start=True, stop=False)
                        ins = e.matmul(o, lhsT=cur[:, cs], rhs=P0lo, start=False, stop=True)
                    return ins
                S.add("pe", _pl, r=[T_cur, T_big8], w=[pP])
            else:
                pu, T_pu = prev_u

                def _pl(e, cur=cur, pu=pu, Pcur=Pcur, Pprev=Pprev):
                    for ct in range(4):
                        cs = slice(ct * 128, (ct + 1) * 128)
                        o = banks[1][:, cs]
                        e.matmul(o, lhsT=cur[:, cs], rhs=Pcur, start=True, stop=False)
                        ins = e.matmul(o, lhsT=pu[:, cs], rhs=Pprev, start=False, stop=True)
                    return ins
                S.add("pe", _pl, r=[T_cur, T_pu, T_big8], w=[pP])
            prev_u = (cur, T_cur)
            S.add("act", lambda e: e.activation(out=plT, in_=v3(banks[1][:], 4), func=AF.Identity), r=[pP], w=[T_plT])

            def _mx(e, wmix=wmix):
                for dt_ in range(4):
                    for ct in range(4):
                        ins = e.matmul(banks[2][:, dt_ * 128:(dt_ + 1) * 128], lhsT=wmix[:, ct, dt_ * 128:(dt_ + 1) * 128], rhs=plT[:, ct, :],
                                       start=(ct == 0), stop=(ct == 3))
                return ins
            S.add("pe", _mx, r=T_wm + [T_plT], w=[pM])

            def _zt(e, win=win, tok=tok):
                for dt_ in range(4):
                    for k in range(8):
                        ins = e.matmul(banks[3][:, dt_ * 128:(dt_ + 1) * 128], lhsT=win[:, k, 512 + dt_ * 128: 512 + (dt_ + 1) * 128], rhs=hnT_t[:, k, tok],
                                       start=(k == 0), stop=(k == 7))
                return ins
            S.add("pe", _zt, r=T_wi + [hnT[c]], w=[pZ])
            S.add("act", lambda e: e.activation(out=szT, in_=v3(banks[3][:], 4), func=AF.Silu), r=[pZ], w=[T_szT])

            def _m2(e, g=g):
                for dt_ in range(4):
                    ins = e.scalar_tensor_tensor(out=m2T[:, dt_, :], in0=banks[2][:, dt_ * 128:(dt_ + 1) * 128], scalar=pscT[:, g * 4 + dt_: g * 4 + dt_ + 1],
                                                 in1=szT[:, dt_, :], op0=ALU.mult, op1=ALU.mult)
                return ins
            S.add("dve", _m2, r=[pM, T_szT, T_lvec], w=[T_m2T])

            def _out(e, wout=wout):
                for half in range(2):
                    for dt_ in range(4):
                        ins = e.matmul(bankGH[:, half * 512:(half + 1) * 512], lhsT=m2T[:, dt_, :], rhs=wout[:, dt_, half * 512:(half + 1) * 512],
                                       start=(dt_ == 0), stop=(dt_ == 3))
                return ins
            S.add("pe", _out, r=[T_m2T] + T_wo, w=[pO])
            if samp:
                S.add("dve", lambda e: e.tensor_tensor(out=bankGH[:], in0=bankGH[:], in1=gt1s[:], op=ALU.mult), r=[pO, T_gt1s], w=[pO])
                S.add("dve", lambda e, c=c: e.tensor_tensor(out=res_t[:, c, :], in0=res_t[:, c, :], in1=bankGH[:], op=ALU.add),
                      r=[res[c], pO], w=[res[c]])
                S.add("pool", lambda e, wout=wout: e.tensor_tensor(out=wout, in0=wout, in1=gt1p[:].unsqueeze(1).to_broadcast([128, 4, 1024]), op=ALU.mult),
                      r=T_wo + [T_gt1p], w=T_wo)
            else:
                S.add("dve", lambda e, c=c: e.tensor_tensor(out=res_t[:, c, :], in0=res_t[:, c, :], in1=bankGH[:], op=ALU.add),
                      r=[res[c], pO], w=[res[c]])
            if g == 0 and c == SC:
                ckpt("Q1")
            if g == 0 and c == 0:
                ckpt("Q2")
            if g == 0 and c == 1:
                ckpt("Q3")


def _prep_shared(inp):
    f = lambda a: np.ascontiguousarray(a, dtype=np.float32)
    sh = {}
    sh["ada_w"] = f(inp["ada_w"])
    sh["ada_b"] = f(inp["ada_b"])
    sh["normg_bc"] = f(np.broadcast_to(inp["norm_g"][:, None, :], (4, 128, 1024)))
    sh["fng_bc"] = f(np.broadcast_to(inp["final_norm_g"][None, :], (128, 1024)))
    w_in = inp["ssd_w_in"]
    wsi = np.empty((2, 8, 128, 8, 768), np.float32)
    for g in range(8):
        cols = np.concatenate([2048 + 256 * g + np.arange(256), 4096 + 128 * g + np.arange(128),
                               5120 + 128 * g + np.arange(128), 256 * g + np.arange(256)])
        blk = w_in[:, :, cols].reshape(2, 8, 128, 768)
        wsi[:, g] = blk.transpose(0, 2, 1, 3)
    sh["w_ssd_in"] = wsi.reshape(2, 8, 128, 8 * 768)
    sh["w_ssd_dt"] = f(w_in[:, :, 6144:6176].reshape(2, 8, 128, 32).transpose(0, 2, 1, 3)).reshape(2, 128, 256)
    wo = inp["ssd_w_out"].reshape(2, 8, 2, 128, 1024)
    sh["w_ssd_out"] = f(wo.transpose(0, 1, 3, 2, 4)).reshape(2, 8, 128, 2048)
    cwv = inp["ssd_conv_w"]
    tiles = []
    for g in range(8):
        tiles += [2 * g, 2 * g + 1, 16 + g, 24 + g]
    tiles = np.array(tiles)
    cw = cwv.reshape(2, 4, 32, 128)[:, :, tiles, :]
    sh["convw"] = f(cw.transpose(0, 3, 2, 1)).reshape(2, 128, 128)
    cb = inp["ssd_conv_b"].reshape(2, 32, 128)[:, tiles, :]
    sh["convb"] = f(cb.transpose(0, 2, 1))
    v = np.concatenate([inp["ssd_dt_bias"], inp["ssd_a_log"], inp["ssd_d"]], axis=1)
    sh["vec32"] = f(np.broadcast_to(v[:, None, :], (2, 128, 96)))
    sh["ssd_normg_bc"] = f(np.broadcast_to(inp["ssd_norm_g"][:, None, :], (2, 128, 2048)))
    pw = inp["pool_w_in"]
    wpi = np.empty((2, 4, 128, 8, 1024), np.float32)
    for g in range(4):
        cols = np.concatenate([512 * g + np.arange(512), 2048 + 512 * g + np.arange(512)])
        wpi[:, g] = pw[:, :, cols].reshape(2, 8, 128, 1024).transpose(0, 2, 1, 3)
    sh["w_pool_in"] = wpi.reshape(2, 4, 128, 8192)
    sh["w_pool_mix"] = f(inp["pool_w_group"].reshape(2, 4, 4, 128, 512).transpose(0, 1, 3, 2, 4)).reshape(2, 4, 128, 2048)
    sh["w_pool_out"] = f(inp["pool_w_out"].reshape(2, 4, 4, 128, 1024).transpose(0, 1, 3, 2, 4)).reshape(2, 4, 128, 4096)
    sh["pscaleT"] = f(inp["pool_scale"].reshape(2, 16, 128).transpose(0, 2, 1))
    cb_, cn_, cc_ = _consts()
    sh["cbase"], sh["cneg"], sh["ccol"] = cb_, cn_, cc_
    sh["pmats"] = _pool_mats()
    return sh


def _prep_core(inp, i):
    f = lambda a: np.ascontiguousarray(a, dtype=np.float32)
    d = {}
    xs = inp["x_sample"][16 * i:16 * i + 16].reshape(128, D)
    d["xin"] = f(np.concatenate([inp["x_prompt"][i], xs], axis=0))
    c = np.concatenate([inp["c_prompt"][i:i + 1], inp["c_sample"][16 * i:16 * i + 16]], axis=0)
    d["cT"] = f(c.reshape(17, 8, 128).transpose(2, 1, 0)).reshape(128, 136)
    ss = inp["state_ssm"][:, 16 * i:16 * i + 16]
    ss = ss.reshape(2, 16, 8, 256, 128).transpose(0, 2, 4, 1, 3)
    d["st_ssm"] = f(ss)
    sc = inp["state_conv"][:, 16 * i:16 * i + 16]
    sc = sc.reshape(2, 16, 3, 32, 128)
    tiles = []
    for g in range(8):
        tiles += [2 * g, 2 * g + 1, 16 + g, 24 + g]
    sc = sc[:, :, :, np.array(tiles), :].reshape(2, 16, 3, 8, 4, 128)
    d["st_conv"] = f(sc.transpose(0, 3, 5, 4, 1, 2)).reshape(2, 8, 128, 192)
    d["st_pool"] = f(inp["state_pool"][:, 16 * i:16 * i + 16].reshape(2, 240, 2048))
    return d


_TILES = None


def _conv_tiles():
    t = []
    for g in range(8):
        t += [2 * g, 2 * g + 1, 16 + g, 24 + g]
    return np.array(t)


def _assemble(results, NL=4):
    nssd, npool = (NL + 1) // 2, NL // 2
    y_p = np.stack([r["yout"][:2048] for r in results]).astype(np.float32)
    y_s = np.concatenate([r["yout"][2048:].reshape(16, 8, D) for r in results]).astype(np.float32)
    sp = np.stack([r["o_ssm_p"] for r in results], axis=1)
    ssm_p = sp.reshape(2, 8, 8, 128, 4, 64).transpose(0, 1, 2, 4, 5, 3).reshape(2, 8, 32, 64, 128)
    ss = np.stack([r["o_ssm_s"] for r in results], axis=1)
    ssm_s = ss.reshape(2, 8, 8, 128, 16, 4, 64).transpose(0, 1, 4, 2, 5, 6, 3).reshape(2, 128, 32, 64, 128)
    tiles = _conv_tiles()
    inv = np.argsort(tiles)
    cp = np.stack([r["o_conv_p"] for r in results], axis=1)
    cp = cp.reshape(2, 8, 8, 128, 4, 3).transpose(0, 1, 5, 2, 4, 3).reshape(2, 8, 3, 32, 128)[:, :, :, inv, :]
    conv_p = cp.reshape(2, 8, 3, 4096)
    cs = np.stack([r["o_conv_s"] for r in results], axis=1)
    cs = cs.reshape(2, 8, 8, 128, 4, 16, 3).transpose(0, 1, 5, 6, 2, 4, 3).reshape(2, 128, 3, 32, 128)[:, :, :, inv, :]
    conv_s = cs.reshape(2, 128, 3, 4096)
    pool_p = np.stack([r["o_pool_p"] for r in results], axis=1)
    pool_s = np.concatenate([r["o_pool_s"] for r in results], axis=1)
    c = lambda a: np.ascontiguousarray(a, dtype=np.float32)
    return (c(y_p), c(y_s), c(ssm_p[:nssd]), c(conv_p[:nssd]), c(pool_p[:npool]), c(ssm_s[:nssd]), c(conv_s[:nssd]), c(pool_s[:npool]))


_NC_CACHE = {}


def kernel(_NL=4, **inputs):
    inputs = {k: np.asarray(v) for k, v in inputs.items()}
    if _NL not in _NC_CACHE:
        _NC_CACHE[_NL] = build_nc(_NL)
    nc = _NC_CACHE[_NL]
    sh = _prep_shared(inputs)
    in_maps = []
    for i in range(NCORES):
        d = dict(sh)
        d.update(_prep_core(inputs, i))
        in_maps.append(d)
    res = run_bass_kernel_spmd(nc, in_maps, core_ids=list(range(NCORES)))
    return _assemble(res.results, _NL)
```

```python
import numpy as np
import concourse.bass as bass
import concourse.mybir as mybir
from concourse.bass_utils import run_bass_kernel_spmd

F32 = mybir.dt.float32
BF16 = mybir.dt.bfloat16
AF = mybir.ActivationFunctionType
ALU = mybir.AluOpType

NCORES = 8
D = 1024
NCH = 17
SC = 16
NTOK = NCH * 128
EPS = 1e-6
POOL_W = (2, 4, 8, 16)
NEG = -30000.0
SAME_SYNC = True
SEG = 4000
ND = 8
STOP = None
EACF = AF.Exp
EVAR = 0


class StopBuild(Exception):
    pass


def ckpt(name):
    if STOP == name:
        raise StopBuild()


class Tl:
    __slots__ = ("ap", "key", "lw", "rd", "bank")

    def __init__(self, ap, key=None, bank=None):
        self.ap = ap
        self.key = key if key is not None else self
        self.lw = None
        self.rd = []
        self.bank = bank


class Op:
    __slots__ = ("eng", "fn", "deps", "dma", "idx", "cost")


LIST_SCHED = True
_DEF_COST = {"pe": 0.6, "act": 0.35, "dve": 0.35, "pool": 0.5, "sp": 0.1}


class Sched:
    ENG = ("pe", "act", "dve", "pool", "sp")

    def __init__(self):
        self.ops = []
        self.bar = {}
        self.lastop = {}
        self.dma_since = []
        self.out_dmas = []
        self.bank_last = {}
        self.cur_bar = None
        self.bar_fn = None

    def add(self, eng, fn, r=(), w=(), dma=0, nobar=False, out=False, cost=None):
        o = Op()
        o.cost = cost if cost is not None else (2.5 if dma else _DEF_COST[eng])
        o.idx = len(self.ops)
        o.eng = eng
        o.fn = fn
        o.dma = dma
        deps = set()
        for t in r:
            k = t.key
            if k.lw is not None:
                deps.add(k.lw)
        for t in w:
            k = t.key
            if k.lw is not None:
                deps.add(k.lw)
            deps.update(k.rd)
        for t in r:
            t.key.rd.append(o.idx)
        for t in w:
            k = t.key
            k.lw = o.idx
            k.rd = []
        for t in list(r) + list(w):
            if t.bank is not None:
                bl = self.bank_last.setdefault(t.bank, {})
                for e2, i2 in bl.items():
                    if e2 != eng:
                        deps.add(i2)
        for t in list(r) + list(w):
            if t.bank is not None:
                self.bank_last[t.bank][eng] = o.idx
        if not nobar and self.cur_bar is not None:
            deps.add(self.cur_bar)
        deps.discard(o.idx)
        o.deps = deps
        self.ops.append(o)
        self.lastop[eng] = o.idx
        if dma:
            self.dma_since.append(o.idx)
            if out:
                self.out_dmas.append(o.idx)
        return o.idx

    def barrier(self):
        deps = set(self.lastop.values()) | set(self.dma_since)
        self.dma_since = []
        idx = self.add("dve", self.bar_fn, nobar=True, cost=0.1)
        self.ops[idx].deps |= deps
        self.ops[idx].deps.discard(idx)
        self.cur_bar = idx

    def list_schedule(self):
        import heapq
        ops = self.ops
        n = len(ops)
        succ = [[] for _ in range(n)]
        indeg = [0] * n
        for o in ops:
            for d in o.deps:
                succ[d].append(o.idx)
            indeg[o.idx] = len(o.deps)
        ready_t = [0.0] * n
        fin = [0.0] * n
        free = {e: 0.0 for e in self.ENG}
        heaps = {e: [] for e in self.ENG}
        for o in ops:
            if indeg[o.idx] == 0:
                heapq.heappush(heaps[o.eng], (0.0, o.idx))
        order = []
        while len(order) < n:
            best = None
            for e in self.ENG:
                h = heaps[e]
                if not h:
                    continue
                rt, idx = h[0]
                st = max(rt, free[e])
                if best is None or (st, idx) < (best[0], best[1]):
                    best = (st, idx, e)
            st, idx, e = best
            heapq.heappop(heaps[e])
            o = ops[idx]
            issue = 0.06 if o.dma else o.cost
            free[e] = st + issue
            fin[idx] = st + o.cost
            order.append(idx)
            for s_ in succ[idx]:
                lat = 0.3 if (ops[s_].eng != e or o.dma) else 0.05
                ready_t[s_] = max(ready_t[s_], fin[idx] + lat)
                indeg[s_] -= 1
                if indeg[s_] == 0:
                    heapq.heappush(heaps[ops[s_].eng], (ready_t[s_], s_))
        return order

    def emit(self, nc):
        ops = self.ops
        order = self.list_schedule() if LIST_SCHED else list(range(len(ops)))
        ops_sorted = [ops[i] for i in order]
        needs = [False] * len(ops)
        for o in ops:
            for d in o.deps:
                po = ops[d]
                if po.dma:
                    continue
                if po.eng != o.eng or SAME_SYNC:
                    needs[d] = True
        cnt = {e: 0 for e in self.ENG}
        val = {}
        for o in ops_sorted:
            if o.dma:
                continue
            if needs[o.idx]:
                cnt[o.eng] += 1
                val[o.idx] = cnt[o.eng]
        esems = {e: [nc.alloc_semaphore(f"c_{e}_{i}") for i in range(cnt[e] // SEG + 1)] for e in self.ENG}
        dpool = {e: [nc.alloc_semaphore(f"d_{e}_{i}") for i in range(ND)] for e in ("sp", "pool", "act")}
        dtot = {e: [0] * ND for e in dpool}
        dcount = {e: 0 for e in dpool}
        dinfo = {}
        for o in ops_sorted:
            if o.dma:
                j = dcount[o.eng] % ND
                dcount[o.eng] += 1
                prev = dtot[o.eng][j]
                dtot[o.eng][j] = prev + 16 * o.dma
                dinfo[o.idx] = (dpool[o.eng][j], prev, dtot[o.eng][j])
        by_eng = {e: [o for o in ops_sorted if o.eng == e] for e in self.ENG}

        def run(eng_name, e):
            waited_c = {x: 0 for x in self.ENG}
            waited_d = {}
            for o in by_eng[eng_name]:
                for d in sorted(o.deps):
                    po = ops[d]
                    if po.dma:
                        sem, _, tot = dinfo[d]
                        if waited_d.get(sem.num if hasattr(sem, "num") else id(sem), 0) < tot:
                            e.wait_ge(sem, tot)
                            waited_d[sem.num if hasattr(sem, "num") else id(sem)] = tot
                    elif po.eng != eng_name or SAME_SYNC:
                        n = val[d]
                        if waited_c[po.eng] < n:
                            e.wait_ge(esems[po.eng][(n - 1) // SEG], (n - 1) % SEG + 1)
                            waited_c[po.eng] = n
                if o.dma:
                    sem, prev, tot = dinfo[o.idx]
                    key = sem.num if hasattr(sem, "num") else id(sem)
                    if prev > 0 and waited_d.get(key, 0) < prev:
                        e.wait_ge(sem, prev)
                        waited_d[key] = prev
                    ins = o.fn(e)
                    if not isinstance(ins, (list, tuple)):
                        ins = [ins]
                    assert len(ins) == o.dma, (len(ins), o.dma)
                    for i_ in ins:
                        i_.then_inc(sem, 16)
                else:
                    ins = o.fn(e)
                    if isinstance(ins, (list, tuple)):
                        ins = ins[-1]
                    if needs[o.idx] and ins is not None:
                        n = val[o.idx]
                        ins.then_inc(esems[eng_name][(n - 1) // SEG], 1)

        with nc.Block() as block:
            @block.tensor
            def _(e):
                run("pe", e)

            @block.scalar
            def _(e):
                run("act", e)

            @block.vector
            def _(e):
                run("dve", e)

            @block.gpsimd
            def _(e):
                run("pool", e)

            @block.sync
            def _(e):
                run("sp", e)


def _consts():
    k = np.arange(128)
    seq = k // 8
    ident = np.eye(128, dtype=np.float32)
    Lp = (k[:, None] <= k[None, :]).astype(np.float32)
    same = (seq[:, None] == seq[None, :])
    Ls = (same & (k[:, None] <= k[None, :])).astype(np.float32)
    Ss = same.astype(np.float32)
    NEGp = np.where(k[:, None] <= k[None, :], 0.0, NEG).astype(np.float32)
    NEGs = np.where(same & (k[:, None] <= k[None, :]), 0.0, NEG).astype(np.float32)
    u = np.arange(224)
    colmask = ((u[None, :] >= 96 + 8 * np.arange(4)[:, None]) & (u[None, :] < 104 + 8 * np.arange(4)[:, None])).astype(np.float32)
    colmask = np.broadcast_to(colmask.reshape(1, 4 * 224), (128, 4 * 224))
    rowmask = (seq[:, None] == np.arange(16)[None, :]).astype(np.float32)
    base = np.concatenate([ident, Lp, Ls, Ss, rowmask], axis=1)
    negs = np.concatenate([np.tile(NEGp, (1, 4)), np.tile(NEGs, (1, 4))], axis=1)
    return base.astype(np.float32), negs.astype(np.float32), np.ascontiguousarray(colmask, dtype=np.float32)


def _pool_mats():
    import ml_dtypes
    s = np.arange(128)[:, None]
    t = np.arange(128)[None, :]
    seq_s, ts = s // 8, s % 8
    seq_t, tt = t // 8, t % 8
    mats = []
    for w in POOL_W:
        band = ((t - s >= 0) & (t - s < w)).astype(np.float64)
        pcur = band / w - np.eye(128)
        cnt0 = np.minimum(w, np.arange(128) + 1)[None, :]
        p0 = band / cnt0 - np.eye(128)
        p0hi = p0.astype(np.float32).astype(ml_dtypes.bfloat16).astype(np.float64)
        p0lo = p0 - p0hi
        pprev = (s > 128 + t - w).astype(np.float64) / w
        pcs = ((seq_s == seq_t) & (tt - ts >= 0) & (tt - ts < w)).astype(np.float64) / w - np.eye(128)
        pps = []
        for hf in range(2):
            r = np.arange(128)[:, None]
            sl, j = r // 15, r % 15
            m = ((r < 120) & (seq_t == 8 * hf + sl) & (j > 15 + tt - w)).astype(np.float64) / w
            pps.append(m)
        mats += [pcur, p0hi, p0lo, pprev, pcs, pps[0], pps[1]]
    return np.concatenate(mats, axis=1).astype(np.float32)


def build_nc(NL=4):
    nc = bass.Bass("TRN2", target_bir_lowering=False)
    S = Sched()
    NSSD = (NL + 1) // 2
    NPOOL = NL // 2

    def din(name, shape):
        return nc.dram_tensor(name, list(shape), F32, kind="ExternalInput").ap()

    def dout(name, shape):
        return nc.dram_tensor(name, list(shape), F32, kind="ExternalOutput").ap()

    xin = din("xin", [NTOK, D])
    cTd = din("cT", [128, 8 * 17])
    st_ssm = din("st_ssm", [2, 8, 128, 16, 256])
    st_conv = din("st_conv", [2, 8, 128, 4 * 16 * 3])
    st_pool = din("st_pool", [2, 240, 2048])
    ada_w = din("ada_w", [4, 1024, 3072])
    ada_b = din("ada_b", [4, 3072])
    normg_d = din("normg_bc", [4, 128, 1024])
    fng_d = din("fng_bc", [128, 1024])
    wsi = din("w_ssd_in", [2, 8, 128, 8 * 768])
    wsd = din("w_ssd_dt", [2, 128, 8 * 32])
    wso = din("w_ssd_out", [2, 8, 128, 2 * 1024])
    convw_d = din("convw", [2, 128, 32 * 4])
    convb_d = din("convb", [2, 128, 32])
    vec32_d = din("vec32", [2, 128, 96])
    sng_d = din("ssd_normg_bc", [2, 128, 2048])
    wpi = din("w_pool_in", [2, 4, 128, 8 * 1024])
    wpm = din("w_pool_mix", [2, 4, 128, 4 * 512])
    wpo = din("w_pool_out", [2, 4, 128, 4 * 1024])
    psc_d = din("pscaleT", [2, 128, 16])
    cbase_d = din("cbase", [128, 528])
    cneg_d = din("cneg", [128, 1024])
    ccol_d = din("ccol", [128, 896])
    pmat_d = din("pmats", [128, 28 * 128])

    yout = dout("yout", [NTOK, D])
    o_ssm_p = dout("o_ssm_p", [2, 8, 128, 256])
    o_ssm_s = dout("o_ssm_s", [2, 8, 128, 16, 256])
    o_conv_p = dout("o_conv_p", [2, 8, 128, 12])
    o_conv_s = dout("o_conv_s", [2, 8, 128, 192])
    o_pool_p = dout("o_pool_p", [2, 15, 2048])
    o_pool_s = dout("o_pool_s", [2, 16, 15, 2048])

    def sb(name, shape, dt=F32):
        return nc.alloc_sbuf_tensor("s_" + name, list(shape), dt)

    res_t = sb("res", [128, NCH, D])
    res = [Tl(res_t[:, c, :]) for c in range(NCH)]
    hnT_t = sb("hnT", [128, 8, NTOK], BF16)
    hnT = [Tl(hnT_t[:, :, c * 128:(c + 1) * 128]) for c in range(NCH)]
    wslot_t = [sb(f"wslot{i}", [128, 8192], BF16) for i in range(2)]
    wslot = [[Tl(t[:, q * 2048:(q + 1) * 2048]) for q in range(4)] for t in wslot_t]
    cbase = sb("cbase", [128, 528])
    identf = cbase[:, 0:128]
    Lp_ap, Ls_ap, Ss_ap = cbase[:, 128:256], cbase[:, 256:384], cbase[:, 384:512]
    rowmask = cbase[:, 512:528]
    T_cbase = Tl(cbase[:])
    identb_t = sb("identb", [128, 128], BF16)
    T_identb = Tl(identb_t[:])
    cneg = sb("cneg", [128, 1024], BF16)
    T_cneg = Tl(cneg[:])
    ccol = sb("ccol", [128, 4, 224], BF16)
    T_ccol = Tl(ccol[:])
    ones_t = sb("ones", [128, 256])
    onesf, negonesf = ones_t[:, 0:128], ones_t[:, 128:256]
    T_ones = Tl(ones_t[:])
    cT = sb("cTs", [128, 8, 17])
    T_cT = Tl(cT[:])
    gt1p = sb("gt1p", [128, D]); T_gt1p = Tl(gt1p[:])
    gt1s = sb("gt1s", [128, D]); T_gt1s = Tl(gt1s[:])
    tmp = None; T_tmp = None
    big8 = sb("big8", [128, 2048])
    T_big8 = Tl(big8[:])
    pmat = big8[:].bitcast(BF16)
    lvec = sb("lvec", [128, 128 + 32 + 96 + 16])
    convw, convb, vec32, pscT = lvec[:, 0:128], lvec[:, 128:160], lvec[:, 160:256], lvec[:, 256:272]
    T_lvec = Tl(lvec[:])
    wdt_t = sb("wdt", [128, 8, 32], BF16); T_wdt = Tl(wdt_t[:])
    pp = sb("pp", [128, 5, NCH * 32])
    dt_a, dtd_a, nac_a, eac_a, dat_a = (pp[:, i, :].rearrange("p (c h) -> p c h", c=NCH) for i in range(5))
    T_pp = [Tl(pp[:, i, :]) for i in range(5)] + [None]
    dats = sb("dats", [128, 16, 32]); T_dats = Tl(dats[:])
    small = sb("small", [128, 16])
    T_ssq, T_rt, T_rstd = Tl(small[:, 0:1]), Tl(small[:, 1:2]), Tl(small[:, 2:3])
    Acol = sb("Acol", [128, 32]); T_Acol = Tl(Acol[:])

    WK = sb("wk", [128, 35 * 1024 // 4])
    wk_off = [0]

    def wk_reset():
        wk_off[0] = 0

    def wk(shape, dt=F32):
        n = int(np.prod(shape[1:]))
        nb = n * (4 if dt == F32 else 2)
        nb4 = (nb + 31) // 32 * 8
        a = WK[:, wk_off[0]:wk_off[0] + nb4]
        wk_off[0] += nb4
        assert wk_off[0] <= 35 * 1024 // 4, wk_off[0]
        if dt == BF16:
            a = a.bitcast(BF16)[:, 0:n]
        else:
            a = a[:, 0:n]
        if len(shape) == 3:
            a = a.rearrange("p (a b) -> p a b", a=shape[1])
        elif len(shape) == 4:
            a = a.rearrange("p (a b c) -> p a b c", a=shape[1], b=shape[2])
        return a

    banks = [nc.alloc_psum_tensor(f"bank{i}", [128, 512], F32) for i in range(6)]
    bankGH = nc.alloc_psum_tensor("bankGH", [128, 1024], F32)

    TB = [Tl(b_[:], bank=i_) for i_, b_ in enumerate(banks)]

    def v3(ap, a):
        return ap.rearrange("p (a b) -> p a b", a=a)

    def bc_last(ap2, n):
        return ap2.unsqueeze(2).to_broadcast([128, ap2.shape[1], n])

    S.add("sp", lambda e: e.dma_start(out=cbase[:], in_=cbase_d), w=[T_cbase], dma=1)
    S.add("pool", lambda e: e.dma_start(out=cneg[:], in_=cneg_d), w=[T_cneg], dma=1)
    S.add("pool", lambda e: e.dma_start(out=ccol[:], in_=ccol_d.rearrange("p (a b) -> p a b", a=4)), w=[T_ccol], dma=1)
    S.add("sp", lambda e: e.dma_start(out=cT[:].rearrange("p a b -> p (a b)"), in_=cTd), w=[T_cT], dma=1)

    def _ones(e):
        e.memset(onesf, 1.0)
        return e.memset(negonesf, -1.0)
    S.add("dve", _ones, w=[T_ones])
    S.add("act", lambda e: e.activation(out=identb_t[:], in_=identf, func=AF.Identity), r=[T_cbase], w=[T_identb])
    for c in range(NCH):
        S.add("sp", lambda e, c=c: e.dma_start(out=res_t[:, c, :], in_=xin[c * 128:(c + 1) * 128, :]), w=[res[c]], dma=1)

    def rms_stats(src_tl, src_ap, junk_tl, junk_ap):
        S.add("act", lambda e: e.activation(out=junk_ap, in_=src_ap, func=AF.Square, accum_out=small[:, 0:1]),
              r=[src_tl], w=[junk_tl, T_ssq])
        S.add("act", lambda e: e.activation(out=small[:, 1:2], in_=small[:, 0:1], func=AF.Sqrt, scale=1.0 / D, bias=small[:, 8:9]),
              r=[T_ssq, T_eps], w=[T_rt])
        S.add("dve", lambda e: e.reciprocal(out=small[:, 2:3], in_=small[:, 1:2]), r=[T_rt], w=[T_rstd])

    T_eps = Tl(small[:, 8:10])
    S.bar_fn = lambda e: e.memset(small[:, 12:13], 0.0)

    def _eps(e):
        e.memset(small[:, 8:9], EPS)
        return e.memset(small[:, 9:10], 1.0)
    S.add("dve", _eps, w=[T_eps])

    wctr = [0]

    def next_slot():
        i = wctr[0] % 2
        wctr[0] += 1
        return i

    try:
        _layers(locals())
    except StopBuild:
        S.barrier()
    _final(locals())
    return nc


def _layers(L_):
    globals().update({k: v for k, v in L_.items() if not k.startswith("__")})
    L_ = dict(L_)
    for li in range(NL):
        is_ssd = (li % 2 == 0)
        lj = li // 2
        wk_reset()
        G_p, SH_p, G_s, SH_s = (wk([128, D]) for _ in range(4))
        T_G_p, T_SH_p, T_G_s, T_SH_s = Tl(G_p), Tl(SH_p), Tl(G_s), Tl(SH_s)
        scTp = wk([128, 8, 128], BF16); T_scTp = Tl(scTp)
        scTs = wk([128, 8, 128], BF16); T_scTs = Tl(scTs)
        hnb = wk([128, D], BF16); T_hnb = Tl(hnb)
        junk, T_junk = hnb, T_hnb
        adab1 = wk([128, 512]); adab = [adab1, adab1]
        T_adab1 = Tl(adab1); T_adab = [T_adab1, T_adab1]
        scr2 = wk([128, NCH * 32]); scr_a = scr2.rearrange("p (c h) -> p c h", c=NCH)
        T_pp[5] = Tl(scr2)
        Dg = wk([128, 16, 32]); T_Dg = Tl(Dg)
        tmpA = wk([128, D]); T_tmpA = Tl(tmpA)
        normg = big8[:, 0:1024]
        S.add("sp", lambda e, li=li: e.dma_start(out=normg, in_=normg_d[li]), w=[T_big8], dma=1)
        S.add("act", lambda e: e.activation(out=scTp, in_=cT[:, :, 0:1].to_broadcast([128, 8, 128]), func=AF.Silu),
              r=[T_cT], w=[T_scTp])
        S.add("act", lambda e: e.activation(out=scTs.rearrange("p k (s t) -> p k s t", s=16),
                                            in_=cT[:, :, 1:17].unsqueeze(3).to_broadcast([128, 8, 16, 8]), func=AF.Silu),
              r=[T_cT], w=[T_scTs])
        psA = [TB[0], TB[1]]
        psB = [TB[2], TB[3]]
        for nb in range(6):
            si = next_slot()
            wv = wslot_t[si][:, 0:4096].rearrange("p (k n) -> p k n", k=8)
            S.add("pool", lambda e, wv=wv, li=li, nb=nb: e.dma_start(
                out=wv, in_=ada_w[li].rearrange("(k p) n -> p k n", p=128)[:, :, nb * 512:(nb + 1) * 512]),
                w=wslot[si][0:2], dma=1, nobar=True)
            ab = nb % 2
            S.add("sp", lambda e, ab=ab, li=li, nb=nb: e.dma_start(out=adab[ab][0:1, :], in_=ada_b[li:li + 1, nb * 512:(nb + 1) * 512]),
                  w=[T_adab[ab]], dma=1)
            for which, (lhs, T_lhs, pst, pbank) in enumerate(((scTp, T_scTp, psA[nb % 2], banks[nb % 2]),
                                                               (scTs, T_scTs, psB[nb % 2], banks[2 + nb % 2]))):
                def _mm(e, lhs=lhs, pbank=pbank, wv=wv, ab=ab):
                    e.matmul(pbank[:], lhsT=onesf[0:1, :], rhs=adab[ab][0:1, :], start=True, stop=False)
                    for k in range(8):
                        ins = e.matmul(pbank[:], lhsT=lhs[:, k, :], rhs=wv[:, k, :], start=False, stop=(k == 7))
                    return ins
                S.add("pe", _mm, cost=2.0, r=[T_lhs, T_adab[ab], T_ones] + wslot[si][0:2], w=[pst])
                blk = slice((nb % 2) * 512, (nb % 2) * 512 + 512)
                if nb < 2:
                    dst, T_dst = (SH_p, T_SH_p) if which == 0 else (SH_s, T_SH_s)
                    S.add("act", lambda e, dst=dst, pbank=pbank, blk=blk: e.activation(out=dst[:, blk], in_=pbank[:], func=AF.Identity),
                          r=[pst], w=[T_dst])
                elif nb < 4:
                    dst, T_dst = (G_p, T_G_p) if which == 0 else (G_s, T_G_s)
                    S.add("dve", lambda e, dst=dst, pbank=pbank, blk=blk: e.scalar_tensor_tensor(
                        out=dst[:, blk], in0=pbank[:], scalar=1.0, in1=normg[:, blk], op0=ALU.add, op1=ALU.mult),
                        r=[pst, T_big8], w=[T_dst])
                else:
                    dst, T_dst = (gt1p, T_gt1p) if which == 0 else (gt1s, T_gt1s)
                    S.add("dve", lambda e, dst=dst, pbank=pbank, blk=blk: e.tensor_scalar(
                        out=dst[:, blk], in0=pbank[:], scalar1=1.0, scalar2=None, op0=ALU.add),
                        r=[pst], w=[T_dst])
        ckpt("MOD%d" % li)
        psT = [TB[4], TB[5]]
        for c in range(NCH):
            G, T_G, SH, T_SH = (G_p, T_G_p, SH_p, T_SH_p) if c < SC else (G_s, T_G_s, SH_s, T_SH_s)
            rms_stats(res[c], res_t[:, c, :], T_junk, junk)
            S.add("dve", lambda e, c=c, G=G: e.scalar_tensor_tensor(out=tmpA, in0=res_t[:, c, :], scalar=small[:, 2:3], in1=G,
                                                                   op0=ALU.mult, op1=ALU.mult),
                  r=[res[c], T_rstd, T_G], w=[T_tmpA])
            S.add("pool", lambda e, SH=SH: e.tensor_tensor(out=hnb, in0=tmpA, in1=SH, op=ALU.add), r=[T_tmpA, T_SH], w=[T_hnb])
            pb = banks[4 + c % 2]
            pbv = pb[:].bitcast(BF16)

            def _tr(e, pbv=pbv):
                for k in range(8):
                    ins = e.transpose(out=pbv[:, k * 128:(k + 1) * 128], in_=hnb[:, k * 128:(k + 1) * 128], identity=identb_t[:])
                return ins
            S.add("pe", _tr, cost=0.9, r=[T_hnb, T_identb], w=[psT[c % 2]])
            S.add("act", lambda e, c=c, pbv=pbv: e.activation(out=hnT_t[:, :, c * 128:(c + 1) * 128],
                                                             in_=pbv[:, 0:1024].rearrange("p (k t) -> p k t", k=8), func=AF.Identity),
                  r=[psT[c % 2]], w=[hnT[c]])

        ckpt("A%d" % li)
        if is_ssd:
            S.add("sp", lambda e, lj=lj: e.dma_start(out=convw, in_=convw_d[lj]), w=[T_lvec], dma=1)
            S.add("sp", lambda e, lj=lj: e.dma_start(out=convb, in_=convb_d[lj]), w=[T_lvec], dma=1)
            S.add("sp", lambda e, lj=lj: e.dma_start(out=vec32, in_=vec32_d[lj]), w=[T_lvec], dma=1)
            S.add("pool", lambda e, lj=lj: e.dma_start(out=wdt_t[:].rearrange("p a b -> p (a b)"), in_=wsd[lj]), w=[T_wdt], dma=1)
            dtb, alog, Dcol = vec32[:, 0:32], vec32[:, 32:64], vec32[:, 64:96]
            S.add("act", lambda e: e.activation(out=Acol[:], in_=alog, func=AF.Exp), r=[T_lvec], w=[T_Acol])
            S.add("dve", lambda e: e.tensor_scalar(out=Acol[:], in0=Acol[:], scalar1=-1.0, scalar2=None, op0=ALU.mult),
                  r=[T_Acol], w=[T_Acol])
            T_psd = [TB[0], TB[1]]
            pd0 = banks[0][:, 0:512].rearrange("p (c h) -> p c h", c=16)
            pd1 = banks[1][:, 0:32]

            def _dtmm(e):
                for c in range(NCH):
                    o = pd0[:, c, :] if c < 16 else pd1
                    for k in range(8):
                        ins = e.matmul(o, lhsT=hnT_t[:, k, c * 128:(c + 1) * 128], rhs=wdt_t[:, k, :], start=(k == 0), stop=(k == 7))
                return ins
            S.add("pe", _dtmm, cost=8.0, r=hnT + [T_wdt], w=T_psd)
            ckpt("P1")
            def _v(e):
                e.tensor_tensor(out=scr_a[:, 0:16, :], in0=pd0, in1=dtb.unsqueeze(1).to_broadcast([128, 16, 32]), op=ALU.add)
                return e.tensor_tensor(out=scr_a[:, 16, :], in0=pd1, in1=dtb, op=ALU.add)
            S.add("dve", _v, r=T_psd + [T_lvec], w=[T_pp[5]])
            S.add("act", lambda e: e.activation(out=scr2, in_=scr2, func=AF.Exp), r=[T_pp[5]], w=[T_pp[5]])
            S.add("act", lambda e: e.activation(out=pp[:, 0, :], in_=scr2, func=AF.Ln, bias=small[:, 9:10]),
                  r=[T_pp[5], T_eps], w=[T_pp[0]])
            S.add("dve", lambda e: e.tensor_tensor(out=scr_a, in0=dt_a, in1=Acol[:].unsqueeze(1).to_broadcast([128, NCH, 32]), op=ALU.mult),
                  r=[T_pp[0], T_Acol], w=[T_pp[5]])
            ckpt("P2")
            T_pac = [TB[2], TB[3]]
            T_pat = [TB[4], TB[5]]
            pa0 = banks[2][:, 0:512].rearrange("p (c h) -> p c h", c=16); pa1 = banks[3][:, 0:32]
            pt0 = banks[4][:, 0:512].rearrange("p (c h) -> p c h", c=16); pt1 = banks[5][:, 0:32]

            def _acmm(e):
                for c in range(NCH):
                    oa = pa0[:, c, :] if c < 16 else pa1
                    ot = pt0[:, c, :] if c < 16 else pt1
                    e.matmul(oa, lhsT=(Lp_ap if c < 16 else Ls_ap), rhs=scr_a[:, c, :], start=True, stop=True)
                    ins = e.matmul(ot, lhsT=(onesf if c < 16 else Ss_ap), rhs=scr_a[:, c, :], start=True, stop=True)
                return ins
            S.add("pe", _acmm, cost=3.0, r=[T_pp[5], T_cbase, T_ones], w=T_pac + T_pat)
            ckpt("P3")

            def _nac(e):
                e.tensor_scalar(out=nac_a[:, 0:16, :], in0=pa0, scalar1=-1.0, scalar2=None, op0=ALU.mult)
                return e.tensor_scalar(out=nac_a[:, 16, :], in0=pa1, scalar1=-1.0, scalar2=None, op0=ALU.mult)
            S.add("dve", _nac, r=T_pac, w=[T_pp[2]])
            ckpt("P3a")

            def _eac(e):
                if EVAR == 1:
                    return e.activation(out=eac_a[:, 0:16, :], in_=pa0, func=EACF)
                if EVAR == 2:
                    return e.activation(out=eac_a[:, 16, :], in_=pa1, func=EACF)
                if EVAR == 4:
                    return e.activation(out=pp[:, 3, 0:512], in_=pp[:, 0, 0:512], func=EACF)
                if EVAR in (6, 7):
                    return e.activation(out=pp[:, 3, 0:512], in_=banks[2][:, 0:512], func=AF.Copy)
                if EVAR == 3:
                    return e.activation(out=pp[:, 3, 0:512], in_=banks[2][:, 0:512], func=EACF)
                e.activation(out=eac_a[:, 0:16, :], in_=pa0, func=EACF)
                e.activation(out=eac_a[:, 16, :], in_=pa1, func=EACF)
                e.activation(out=dat_a[:, 0:16, :], in_=pt0, func=EACF)
                return e.activation(out=dat_a[:, 16, :], in_=pt1, func=EACF)
            S.add("act", _eac, r=T_pac + T_pat + ([T_pp[2]] if EVAR == 7 else []), w=[T_pp[3], T_pp[4]])
            ckpt("P3b")

            def _dd(e):
                e.tensor_tensor(out=dtd_a[:, 0:16, :], in0=pt0, in1=nac_a[:, 0:16, :], op=ALU.add)
                return e.tensor_tensor(out=dtd_a[:, 16, :], in0=pt1, in1=nac_a[:, 16, :], op=ALU.add)
            S.add("dve", _dd, r=T_pat + [T_pp[2]], w=[T_pp[1]])
            ckpt("P3c")
            S.add("act", lambda e: e.activation(out=pp[:, 1, :], in_=pp[:, 1, :], func=AF.Exp), r=[T_pp[1]], w=[T_pp[1]])
            S.add("dve", lambda e: e.tensor_tensor(out=pp[:, 1, :], in0=pp[:, 1, :], in1=pp[:, 0, :], op=ALU.mult),
                  r=[T_pp[1], T_pp[0]], w=[T_pp[1]])
            ckpt("P4")
            S.add("sp", lambda e, lj=lj: e.dma_start(out=big8[:], in_=sng_d[lj]), w=[T_big8], dma=1)
            S.add("dve", lambda e: e.tensor_tensor(out=Dg, in0=scr_a[:, 16:17, :].to_broadcast([128, 16, 32]),
                                                   in1=rowmask.unsqueeze(2).to_broadcast([128, 16, 32]), op=ALU.mult),
                  r=[T_pp[5], T_cbase], w=[T_Dg])
            T_pds = TB[0]
            S.add("pe", lambda e: e.matmul(banks[0][:], lhsT=onesf, rhs=Dg.rearrange("p a b -> p (a b)"), start=True, stop=True),
                  r=[T_Dg, T_ones], w=[T_pds])
            S.add("act", lambda e: e.activation(out=dats[:].rearrange("p a b -> p (a b)"), in_=banks[0][:], func=AF.Exp),
                  r=[T_pds], w=[T_dats])
            S.barrier()
            ckpt("PRE%d" % li)
            L2 = dict(L_); L2.update(locals())
            ssd_layer(nc, S, L2, lj)
        else:
            S.add("sp", lambda e, lj=lj: e.dma_start(out=pscT, in_=psc_d[lj]), w=[T_lvec], dma=1)
            S.add("pool", lambda e: e.dma_start(out=pmat[:, 0:3584].rearrange("p (a b) -> p a b", a=28), in_=pmat_d.rearrange("p (a b) -> p a b", a=28)), w=[T_big8], dma=1)
            S.barrier()
            ckpt("PRE%d" % li)
            L2 = dict(L_); L2.update(locals())
            pool_layer(nc, S, L2, lj)
        S.barrier()
        ckpt("L%d" % li)


def _final(L_):
    globals().update({k: v for k, v in L_.items() if not k.startswith("__")})
    wk_reset()
    junk = wk([128, D], BF16); T_junk = Tl(junk)
    yb = [wk([128, D]) for _ in range(2)]; T_yb = [Tl(a) for a in yb]
    S.add("sp", lambda e: e.dma_start(out=big8[:, 0:1024], in_=fng_d), w=[T_big8], dma=1)
    for c in range(NCH):
        rms_stats(res[c], res_t[:, c, :], T_junk, junk)
        S.add("dve", lambda e, c=c: e.scalar_tensor_tensor(out=yb[c % 2], in0=res_t[:, c, :], scalar=small[:, 2:3], in1=big8[:, 0:1024],
                                                           op0=ALU.mult, op1=ALU.mult),
              r=[res[c], T_rstd, T_big8], w=[T_yb[c % 2]])
        S.add("sp", lambda e, c=c: e.dma_start(out=yout[c * 128:(c + 1) * 128, :], in_=yb[c % 2]), r=[T_yb[c % 2]], dma=1, out=True)
    o = Op(); o.idx = len(S.ops); o.eng = "sp"; o.fn = lambda e: None
    o.dma = 0; o.deps = set(S.out_dmas); o.cost = 0.1
    S.ops.append(o)
    S.emit(nc)


def _cat_lvec(lj):
    return None


def ssd_layer(nc, S, L, lj):
    g_ = L
    (banks, bankGH, wk, wk_reset, wslot, wslot_t, next_slot, hnT, hnT_t, res, res_t, tmp, T_tmp) = (
        g_[k] for k in ("banks", "bankGH", "wk", "wk_reset", "wslot", "wslot_t", "next_slot", "hnT", "hnT_t", "res", "res_t", "tmp", "T_tmp"))
    identf, identb_t, T_identb, T_cbase, rowmask = g_["identf"], g_["identb_t"], g_["T_identb"], g_["T_cbase"], g_["rowmask"]
    cneg, T_cneg, ccol, T_ccol = g_["cneg"], g_["T_cneg"], g_["ccol"], g_["T_ccol"]
    negonesf, T_ones = g_["negonesf"], g_["T_ones"]
    convw, convb, vec32, T_lvec = g_["convw"], g_["convb"], g_["vec32"], g_["T_lvec"]
    dt_a, dtd_a, nac_a, eac_a, dat_a = g_["dt_a"], g_["dtd_a"], g_["nac_a"], g_["eac_a"], g_["dat_a"]
    T_pp, dats, T_dats = g_["T_pp"], g_["dats"], g_["T_dats"]
    gt1p, T_gt1p, gt1s, T_gt1s = g_["gt1p"], g_["T_gt1p"], g_["gt1s"], g_["T_gt1s"]
    big8, T_big8, small, T_eps = g_["big8"], g_["T_big8"], g_["small"], g_["T_eps"]
    st_ssm, st_conv, wsi, wso = g_["st_ssm"], g_["st_conv"], g_["wsi"], g_["wso"]
    o_ssm_p, o_ssm_s, o_conv_p, o_conv_s = g_["o_ssm_p"], g_["o_ssm_s"], g_["o_conv_p"], g_["o_conv_s"]
    Dcol = vec32[:, 64:96]

    def v3(ap, a):
        return ap.rearrange("p (a b) -> p a b", a=a)

    wk_reset()
    xps = wk([128, 4, 16, 11]); T_xp = Tl(xps)
    xp = xps.rearrange("p a b c -> p (a b c)")[:, 0:4 * 131].rearrange("p (a b) -> p a b", a=4)
    acc = wk([128, 4, 128]); T_acc = Tl(acc)
    xa = wk([128, 4, 128], BF16); T_xa = Tl(xa)
    xs = wk([128, 256], BF16); T_xs = Tl(xs)
    xdt = wk([128, 256], BF16); T_xdt = Tl(xdt)
    xdtd = wk([128, 256], BF16); T_xdtd = Tl(xdtd)
    Bsb = wk([128, 128], BF16); T_Bsb = Tl(Bsb)
    Dexp = wk([128, 4, 128]); T_Dexp = Tl(Dexp)
    Eb = wk([128, 4, 128]); T_E = Tl(Eb)
    MT = wk([128, 4, 128], BF16); T_MT = Tl(MT)
    y1 = wk([128, 256]); T_y1 = Tl(y1)
    sz = wk([128, 256]); T_sz = Tl(sz)
    junk = sz; T_junk = T_sz
    gn = wk([128, 256], BF16); T_gn = Tl(gn)
    gT = wk([128, 2, 128], BF16); T_gT = Tl(gT)
    hT = wk([128, 256]); T_hT = Tl(hT)
    hTb = wk([128, 256], BF16); T_hTb = Tl(hTb)
    h0f = [wk([128, 4, 256]) for _ in range(2)]; T_h0f = [Tl(a) for a in h0f]
    h0b = [wk([128, 4, 256], BF16) for _ in range(2)]; T_h0b = [Tl(a) for a in h0b]
    Bm = [wk([128, 4, 128], BF16) for _ in range(2)]; T_Bm = [Tl(a) for a in Bm]
    CTm = [wk([128, 4, 128], BF16) for _ in range(2)]; T_CTm = [Tl(a) for a in CTm]
    cvs = wk([128, 192]); T_cvs = Tl(cvs)
    cvp = wk([128, 12]); T_cvp = Tl(cvp)
    sq = wk([128, 4]); T_sq = Tl(sq)

    pA = Tl(banks[0][:], bank=0)
    pZ = Tl(banks[1][:, 0:256], bank=1); pCB = Tl(banks[1][:, 256:384], bank=1)
    pC = banks[2][:].bitcast(BF16)
    pTx = Tl(pC[:, 0:384], bank=2); pTg = Tl(pC[:, 512:768], bank=2)
    pD = Tl(banks[3][:], bank=3)
    pYa = Tl(banks[4][:, 0:256], bank=4); pYb = Tl(banks[4][:, 256:512], bank=4)
    pF = Tl(banks[5][:], bank=5)
    pO = Tl(bankGH[:], bank=6)
    hcnt = [0]

    sz2 = [sz, wk([128, 256])]; T_sz2 = [T_sz, Tl(sz2[1])]
    W = {}; SZ = {}; acnt = [0]
    cw = v3(convw, 32)

    def load_w(g):
        si = next_slot()
        win = wslot_t[si][:, 0:6144].rearrange("p (k n) -> p k n", k=8)
        wout = wslot_t[si][:, 6144:8192].rearrange("p (j n) -> p j n", j=2)
        T_wi = wslot[si][0:3]
        T_wo = wslot[si][3:4]
        S.add("pool", lambda e, win=win, g=g: e.dma_start(out=win, in_=wsi[lj, g].rearrange("p (k n) -> p k n", k=8)),
              w=T_wi, dma=1, nobar=True)
        S.add("pool", lambda e, wout=wout, g=g: e.dma_start(out=wout, in_=wso[lj, g].rearrange("p (j n) -> p j n", j=2)),
              w=T_wo, dma=1, nobar=True)
        W[g] = (win, wout, T_wi, T_wo)

    def emitA(g, c):
        g4 = g * 4
        win, wout, T_wi, T_wo = W[g]
        samp = (c == SC)
        tok = slice(c * 128, (c + 1) * 128)
        si_ = acnt[0] % 2; acnt[0] += 1; SZ[(g, c)] = si_
        sz = sz2[si_]; T_sz = T_sz2[si_]
        def _xbc(e, win=win, tok=tok):
            for j in range(4):
                for k in range(8):
                    ins = e.matmul(banks[0][:, j * 128:(j + 1) * 128], lhsT=win[:, k, j * 128:(j + 1) * 128], rhs=hnT_t[:, k, tok],
                                   start=(k == 0), stop=(k == 7))
            return ins
        S.add("pe", _xbc, cost=3.5, r=T_wi + [hnT[c]], w=[pA])
        def _z(e, win=win, tok=tok):
            for k in range(8):
                ins = e.matmul(banks[1][:, 0:256], lhsT=hnT_t[:, k, tok], rhs=win[:, k, 512:768], start=(k == 0), stop=(k == 7))
            return ins
        S.add("pe", _z, cost=1.2, r=T_wi + [hnT[c]], w=[pZ])
        if samp:
            S.add("sp", lambda e, g=g: e.dma_start(out=cvs, in_=st_conv[lj, g]), w=[T_cvs], dma=1)
            S.add("pool", lambda e: e.tensor_copy(out=xps[:, :, :, 0:3], in_=cvs.rearrange("p (a b c) -> p a b c", a=4, b=16)),
                  r=[T_cvs], w=[T_xp])
            S.add("act", lambda e: e.activation(out=xps[:, :, :, 3:11], in_=banks[0][:].rearrange("p (a b c) -> p a b c", a=4, b=16),
                                                func=AF.Identity), r=[pA], w=[T_xp])
            S.add("pool", lambda e: e.tensor_copy(out=cvs.rearrange("p (a b c) -> p a b c", a=4, b=16), in_=xps[:, :, :, 8:11]),
                  r=[T_xp], w=[T_cvs])
            S.add("sp", lambda e, g=g: e.dma_start(out=o_conv_s[lj, g], in_=cvs), r=[T_cvs], dma=1, out=True)
            src = lambda j, k: xps[:, j, :, k:k + 8]
            accv = lambda j: acc[:, j, :].rearrange("p (a b) -> p a b", a=16)
        else:
            if c == 0:
                S.add("pool", lambda e: e.memset(xp[:, :, 0:3], 0.0), w=[T_xp])
            S.add("act", lambda e: e.activation(out=xp[:, :, 3:131], in_=v3(banks[0][:], 4), func=AF.Identity), r=[pA], w=[T_xp])
            src = lambda j, k: xp[:, j, k:k + 128]
            accv = lambda j: acc[:, j, :]

        for k in range(4):
            def _conv(e, src=src, accv=accv, g4=g4, k=k):
                for j in range(4):
                    ti = g4 + j
                    if k == 0:
                        ins = e.tensor_scalar(out=accv(j), in0=src(j, 0), scalar1=cw[:, ti, 0:1], scalar2=convb[:, ti:ti + 1], op0=ALU.mult, op1=ALU.add)
                    else:
                        ins = e.scalar_tensor_tensor(out=accv(j), in0=src(j, k), scalar=cw[:, ti, k:k + 1], in1=accv(j), op0=ALU.mult, op1=ALU.add)
                return ins
            S.add("dve", _conv, cost=0.6, r=[T_xp, T_lvec] + ([T_acc] if k else []), w=[T_acc])
        if not samp:
            if c == 15:
                S.add("pool", lambda e: e.tensor_copy(out=v3(cvp, 4), in_=xp[:, :, 128:131]), r=[T_xp], w=[T_cvp])
                S.add("sp", lambda e, g=g: e.dma_start(out=o_conv_p[lj, g], in_=cvp), r=[T_cvp], dma=1, out=True)
            else:
                S.add("pool", lambda e: e.tensor_copy(out=xp[:, :, 0:3], in_=xp[:, :, 128:131]), r=[T_xp], w=[T_xp])
        S.add("act", lambda e: e.activation(out=xa, in_=acc, func=AF.Silu), r=[T_acc], w=[T_xa])
        S.add("act", lambda e: e.activation(out=sz, in_=banks[1][:, 0:256], func=AF.Silu), r=[pZ], w=[T_sz])

    def emitM(g, c):
        g4 = g * 4
        win, wout, T_wi, T_wo = W[g]
        samp = (c == SC)
        tok = slice(c * 128, (c + 1) * 128)
        def _trx(e):
            e.transpose(out=pC[:, 0:128], in_=xa[:, 0, :], identity=identb_t[:])
            e.transpose(out=pC[:, 128:256], in_=xa[:, 1, :], identity=identb_t[:])
            return e.transpose(out=pC[:, 256:384], in_=xa[:, 2, :], identity=identb_t[:])
        S.add("pe", _trx, r=[T_xa, T_identb], w=[pTx])
        S.add("act", lambda e: e.activation(out=xs, in_=pC[:, 0:256], func=AF.Identity), r=[pTx], w=[T_xs])
        S.add("act", lambda e: e.activation(out=Bsb, in_=pC[:, 256:384], func=AF.Identity), r=[pTx], w=[T_Bsb])
        S.add("dve", lambda e, c=c, g4=g4: e.tensor_tensor(out=v3(xdt, 4), in0=v3(pC[:, 0:256], 4),
                                                          in1=dt_a[:, c, g4:g4 + 4].unsqueeze(2).to_broadcast([128, 4, 64]), op=ALU.mult),
              r=[pTx, T_pp[0]], w=[T_xdt])
        S.add("dve", lambda e, c=c, g4=g4: e.tensor_tensor(out=v3(xdtd, 4), in0=v3(pC[:, 0:256], 4),
                                                          in1=dtd_a[:, c, g4:g4 + 4].unsqueeze(2).to_broadcast([128, 4, 64]), op=ALU.mult),
              r=[pTx, T_pp[1]], w=[T_xdtd])
        S.add("pe", lambda e: e.matmul(banks[1][:, 256:384], lhsT=xa[:, 2, :], rhs=xa[:, 3, :], start=True, stop=True),
              r=[T_xa], w=[pCB])
        S.add("dve", lambda e, c=c, g4=g4: e.tensor_tensor(out=Dexp, in0=identf.unsqueeze(1).to_broadcast([128, 4, 128]),
                                                          in1=nac_a[:, c, g4:g4 + 4].unsqueeze(2).to_broadcast([128, 4, 128]), op=ALU.mult),
              r=[T_cbase, T_pp[2]], w=[T_Dexp])
        ncol = slice(512, 1024) if samp else slice(0, 512)

        def _seg(e, ncol=ncol):
            e.matmul(banks[3][:], lhsT=negonesf, rhs=Dexp.rearrange("p a b -> p (a b)"), start=True, stop=False)
            return e.matmul(banks[3][:], lhsT=identb_t[:], rhs=cneg[:, ncol], start=False, stop=True)
        S.add("pe", _seg, cost=1.8, r=[T_Dexp, T_ones, T_identb, T_cneg], w=[pD])

        def _E(e, c=c, g4=g4):
            for h in range(4):
                ins = e.activation(out=Eb[:, h, :], in_=banks[3][:, h * 128:(h + 1) * 128], func=AF.Exp, bias=nac_a[:, c, g4 + h:g4 + h + 1])
            return ins
        S.add("act", _E, cost=0.9, r=[pD, T_pp[2]], w=[T_E])
        S.add("dve", lambda e: e.tensor_tensor(out=MT, in0=Eb, in1=banks[1][:, 256:384].unsqueeze(1).to_broadcast([128, 4, 128]), op=ALU.mult),
              r=[T_E, pCB], w=[T_MT])
        def _yi(e):
            for h in range(4):
                ins = e.matmul(banks[4][:, h * 64:(h + 1) * 64], lhsT=MT[:, h, :], rhs=xdt[:, h * 64:(h + 1) * 64], start=True, stop=True)
            return ins
        S.add("pe", _yi, r=[T_MT, T_xdt], w=[pYa])
        if not samp:
            if c == 0:
                S.add("pool", lambda e: e.memset(hT, 0.0), w=[T_hT])
                S.add("pool", lambda e: e.memset(hTb, 0.0), w=[T_hTb])
            S.add("pe", lambda e: e.matmul(banks[4][:, 256:512], lhsT=xa[:, 3, :], rhs=hTb, start=True, stop=True), r=[T_xa, T_hTb], w=[pYb])
            S.add("pe", lambda e: e.matmul(banks[5][:, 0:256], lhsT=Bsb, rhs=xdtd, start=True, stop=True), r=[T_Bsb, T_xdtd], w=[pF])
            S.add("dve", lambda e, c=c, g4=g4: e.tensor_tensor(out=v3(hT, 4), in0=v3(hT, 4),
                                                              in1=dat_a[:, c, g4:g4 + 4].unsqueeze(2).to_broadcast([128, 4, 64]), op=ALU.mult),
                  r=[T_hT, T_pp[4], pYb], w=[T_hT])
            S.add("dve", lambda e: e.tensor_tensor(out=hT, in0=hT, in1=banks[5][:, 0:256], op=ALU.add), r=[T_hT, pF], w=[T_hT])
            if c == 15:
                S.add("sp", lambda e, g=g: e.dma_start(out=o_ssm_p[lj, g], in_=hT), r=[T_hT], dma=1, out=True)
            else:
                S.add("act", lambda e: e.activation(out=hTb, in_=hT, func=AF.Identity), r=[T_hT], w=[T_hTb])
        else:
            for pc in range(4):
                b = hcnt[0] % 2
                hcnt[0] += 1
                S.add("sp", lambda e, b=b, g=g, pc=pc: e.dma_start(out=h0f[b], in_=st_ssm[lj, g, :, pc * 4:(pc + 1) * 4, :]),
                      w=[T_h0f[b]], dma=1)
                S.add("pool", lambda e, b=b: e.tensor_copy(out=h0b[b], in_=h0f[b]), r=[T_h0f[b]], w=[T_h0b[b]])
                S.add("pool", lambda e, b=b, pc=pc: e.tensor_tensor(out=CTm[b], in0=xa[:, 3, :].unsqueeze(1).to_broadcast([128, 4, 128]),
                                                                   in1=ccol[:, :, 96 - 32 * pc: 224 - 32 * pc], op=ALU.mult),
                      r=[T_xa, T_ccol], w=[T_CTm[b]])
                S.add("pool", lambda e, b=b, pc=pc: e.tensor_tensor(out=Bm[b], in0=Bsb.unsqueeze(1).to_broadcast([128, 4, 128]),
                                                                   in1=rowmask[:, pc * 4:(pc + 1) * 4].unsqueeze(2).to_broadcast([128, 4, 128]), op=ALU.mult),
                      r=[T_Bsb, T_cbase], w=[T_Bm[b]])

                def _yis(e, b=b, pc=pc):
                    for s_ in range(4):
                        ins = e.matmul(banks[4][:, 256:512], lhsT=CTm[b][:, s_, :], rhs=h0b[b][:, s_, :],
                                       start=(pc == 0 and s_ == 0), stop=(pc == 3 and s_ == 3))
                    return ins
                S.add("pe", _yis, cost=0.8, r=[T_CTm[b], T_h0b[b]] + ([pYa] if pc == 0 else []), w=[pYb])
                for hp in range(2):
                    def _sts(e, b=b, hp=hp):
                        for s_ in range(2):
                            ins = e.matmul(banks[5][:, s_ * 256:(s_ + 1) * 256], lhsT=Bm[b][:, hp * 2 + s_, :], rhs=xdtd, start=True, stop=True)
                        return ins
                    S.add("pe", _sts, r=[T_Bm[b], T_xdtd], w=[pF])
                    seq0 = pc * 4 + hp * 2
                    hv = h0f[b][:, hp * 2:hp * 2 + 2, :].rearrange("p s (h q) -> p s h q", h=4)
                    S.add("dve", lambda e, hv=hv, seq0=seq0, g4=g4: e.tensor_tensor(
                        out=hv, in0=hv, in1=dats[:, seq0:seq0 + 2, g4:g4 + 4].unsqueeze(3).to_broadcast([128, 2, 4, 64]), op=ALU.mult),
                        r=[T_h0f[b], T_dats, T_h0b[b]], w=[T_h0f[b]])
                    hv2 = h0f[b][:, hp * 2:hp * 2 + 2, :]
                    S.add("dve", lambda e, hv2=hv2: e.tensor_tensor(out=hv2, in0=hv2, in1=banks[5][:].rearrange("p (s q) -> p s q", s=2), op=ALU.add),
                          r=[T_h0f[b], pF], w=[T_h0f[b]])
                S.add("sp", lambda e, b=b, g=g, pc=pc: e.dma_start(out=o_ssm_s[lj, g, :, pc * 4:(pc + 1) * 4, :], in_=h0f[b]),
                      r=[T_h0f[b]], dma=1, out=True)

    def emitT(g, c):
        g4 = g * 4
        win, wout, T_wi, T_wo = W[g]
        samp = (c == SC)
        tok = slice(c * 128, (c + 1) * 128)
        si_ = SZ[(g, c)]
        sz = sz2[si_]; T_sz = T_sz2[si_]; junk = sz; T_junk = T_sz
        S.add("dve", lambda e, c=c, g4=g4: e.tensor_tensor(out=v3(y1, 4), in0=v3(banks[4][:, 256:512], 4),
                                                          in1=eac_a[:, c, g4:g4 + 4].unsqueeze(2).to_broadcast([128, 4, 64]), op=ALU.mult),
              r=[pYb, T_pp[3]], w=[T_y1])
        S.add("dve", lambda e: e.tensor_tensor(out=y1, in0=y1, in1=banks[4][:, 0:256], op=ALU.add), r=[T_y1, pYa], w=[T_y1])

        def _dsk(e, g4=g4):
            for h in range(4):
                hs = slice(h * 64, (h + 1) * 64)
                ins = e.scalar_tensor_tensor(out=y1[:, hs], in0=xs[:, hs], scalar=Dcol[:, g4 + h:g4 + h + 1], in1=y1[:, hs], op0=ALU.mult, op1=ALU.add)
            return ins
        S.add("dve", _dsk, cost=0.5, r=[T_xs, T_y1, T_lvec], w=[T_y1])
        S.add("dve", lambda e: e.tensor_tensor(out=y1, in0=y1, in1=sz, op=ALU.mult), r=[T_y1, T_sz], w=[T_y1])
        S.add("act", lambda e: e.activation(out=junk, in_=y1, func=AF.Square, accum_out=sq[:, 0:1]), r=[T_y1], w=[T_junk, T_sq])
        S.add("act", lambda e: e.activation(out=sq[:, 1:2], in_=sq[:, 0:1], func=AF.Sqrt, scale=1.0 / 256, bias=small[:, 8:9]),
              r=[T_sq, T_eps], w=[T_sq])
        S.add("dve", lambda e: e.reciprocal(out=sq[:, 2:3], in_=sq[:, 1:2]), r=[T_sq], w=[T_sq])
        S.add("dve", lambda e, g=g: e.scalar_tensor_tensor(out=gn, in0=y1, scalar=sq[:, 2:3], in1=big8[:, g * 256:(g + 1) * 256],
                                                          op0=ALU.mult, op1=ALU.mult), r=[T_y1, T_sq, T_big8], w=[T_gn])

        def _trg(e):
            e.transpose(out=pC[:, 512:640], in_=gn[:, 0:128], identity=identb_t[:])
            return e.transpose(out=pC[:, 640:768], in_=gn[:, 128:256], identity=identb_t[:])
        S.add("pe", _trg, r=[T_gn, T_identb], w=[pTg])
        S.add("act", lambda e: e.activation(out=gT, in_=v3(pC[:, 512:768], 2), func=AF.Identity), r=[pTg], w=[T_gT])

        def _out(e, wout=wout):
            for half in range(2):
                for j in range(2):
                    ins = e.matmul(bankGH[:, half * 512:(half + 1) * 512], lhsT=gT[:, j, :], rhs=wout[:, j, half * 512:(half + 1) * 512],
                                   start=(j == 0), stop=(j == 1))
            return ins
        S.add("pe", _out, cost=1.2, r=[T_gT] + T_wo, w=[pO])
        if samp:
            S.add("dve", lambda e: e.tensor_tensor(out=bankGH[:], in0=bankGH[:], in1=gt1s[:], op=ALU.mult), r=[pO, T_gt1s], w=[pO])
            S.add("dve", lambda e, c=c: e.tensor_tensor(out=res_t[:, c, :], in0=res_t[:, c, :], in1=bankGH[:], op=ALU.add),
                  r=[res[c], pO], w=[res[c]])
            S.add("pool", lambda e, wout=wout: e.tensor_tensor(out=wout, in0=wout, in1=gt1p[:].unsqueeze(1).to_broadcast([128, 2, 1024]), op=ALU.mult),
                  r=T_wo + [T_gt1p], w=T_wo)
        else:
            S.add("dve", lambda e, c=c: e.tensor_tensor(out=res_t[:, c, :], in0=res_t[:, c, :], in1=bankGH[:], op=ALU.add),
                  r=[res[c], pO], w=[res[c]])

    units = [(g, c) for g in range(8) for c in [SC] + list(range(16))]
    load_w(0)
    emitA(*units[0])
    for i_, (g, c) in enumerate(units):
        if c == SC and g + 1 < 8:
            load_w(g + 1)
        emitM(g, c)
        if i_ + 1 < len(units):
            emitA(*units[i_ + 1])
        emitT(g, c)


def pool_layer(nc, S, L, lj):
    g_ = L
    (banks, bankGH, wk, wk_reset, wslot, wslot_t, next_slot, hnT, hnT_t, res, res_t, tmp, T_tmp) = (
        g_[k] for k in ("banks", "bankGH", "wk", "wk_reset", "wslot", "wslot_t", "next_slot", "hnT", "hnT_t", "res", "res_t", "tmp", "T_tmp"))
    pmat, T_big8, pscT, T_lvec = g_["pmat"], g_["T_big8"], g_["pscT"], g_["T_lvec"]
    gt1p, T_gt1p, gt1s, T_gt1s = g_["gt1p"], g_["T_gt1p"], g_["gt1s"], g_["T_gt1s"]
    st_pool, wpi, wpm, wpo, o_pool_p, o_pool_s = g_["st_pool"], g_["wpi"], g_["wpm"], g_["wpo"], g_["o_pool_p"], g_["o_pool_s"]

    def v3(ap, a):
        return ap.rearrange("p (a b) -> p a b", a=a)

    wk_reset()
    ub = [wk([128, 512], BF16) for _ in range(2)]; T_ub = [Tl(a) for a in ub]
    uf = wk([128, 512]); T_uf = Tl(uf)
    plT = wk([128, 4, 128], BF16); T_plT = Tl(plT)
    szT = wk([128, 4, 128]); T_szT = Tl(szT)
    m2T = wk([128, 4, 128], BF16); T_m2T = Tl(m2T)
    prevS = wk([128, 2, 512], BF16); T_prevS = Tl(prevS)

    pU = Tl(banks[0][:], bank=0); pP = Tl(banks[1][:], bank=1); pM = Tl(banks[2][:], bank=2); pZ = Tl(banks[3][:], bank=3); pO = Tl(bankGH[:], bank=6)
    S.add("sp", lambda e: e.dma_start(out=o_pool_s[lj, :, 0:7, :], in_=st_pool[lj].rearrange("(s j) c -> s j c", j=15)[:, 8:15, :]),
          dma=1, out=True)
    ucnt = [0]
    ckpt("Q0")
    for g in range(4):
        if g == 1:
            ckpt("Q4")
        si = next_slot()
        win = wslot_t[si][:].rearrange("p (k n) -> p k n", k=8)
        T_wi = wslot[si][0:4]
        S.add("pool", lambda e, win=win, g=g: e.dma_start(out=win, in_=wpi[lj, g].rearrange("p (k n) -> p k n", k=8)),
              w=T_wi, dma=1, nobar=True)
        si2 = next_slot()
        wmix = wslot_t[si2][:, 0:2048].rearrange("p (k n) -> p k n", k=4)
        wout = wslot_t[si2][:, 2048:6144].rearrange("p (k n) -> p k n", k=4)
        T_wm = wslot[si2][0:1]
        T_wo = wslot[si2][1:3]
        S.add("pool", lambda e, wmix=wmix, g=g: e.dma_start(out=wmix, in_=wpm[lj, g].rearrange("p (k n) -> p k n", k=4)),
              w=T_wm, dma=1, nobar=True)
        S.add("pool", lambda e, wout=wout, g=g: e.dma_start(out=wout, in_=wpo[lj, g].rearrange("p (k n) -> p k n", k=4)),
              w=T_wo, dma=1, nobar=True)
        S.add("pool", lambda e, g=g: e.dma_start(out=prevS[0:120, :, :],
                                                 in_=st_pool[lj].rearrange("(h r) c -> r h c", h=2)[:, :, g * 512:(g + 1) * 512]),
              w=[T_prevS], dma=1)
        mb = g * 7 * 128
        Pcur, P0hi, P0lo, Pprev, PcS, PpS0, PpS1 = (pmat[:, mb + i * 128: mb + (i + 1) * 128] for i in range(7))
        prev_u = None
        for c in [SC] + list(range(16)):
            samp = (c == SC)
            tok = slice(c * 128, (c + 1) * 128)
            bi = ucnt[0] % 2
            ucnt[0] += 1
            cur = ub[bi]; T_cur = T_ub[bi]

            def _u(e, win=win, tok=tok):
                for k in range(8):
                    ins = e.matmul(banks[0][:], lhsT=hnT_t[:, k, tok], rhs=win[:, k, 0:512], start=(k == 0), stop=(k == 7))
                return ins
            S.add("pe", _u, cost=1.6, r=T_wi + [hnT[c]], w=[pU])
            S.add("act", lambda e, cur=cur: e.activation(out=cur, in_=banks[0][:], func=AF.Identity), r=[pU], w=[T_cur])
            if samp or c == 15:
                S.add("dve", lambda e: e.tensor_copy(out=uf, in_=banks[0][:]), r=[pU], w=[T_uf])
                if samp:
                    def _us(e, g=g):
                        return [e.dma_start(out=o_pool_s[lj, s_, 7:15, g * 512:(g + 1) * 512], in_=uf[s_ * 8:(s_ + 1) * 8, :]) for s_ in range(16)]
                    S.add("sp", _us, r=[T_uf], dma=16, out=True)
                else:
                    S.add("sp", lambda e, g=g: e.dma_start(out=o_pool_p[lj, :, g * 512:(g + 1) * 512], in_=uf[113:128, :]), r=[T_uf], dma=1, out=True)
            if samp:
                def _pl(e, cur=cur, PcS=PcS, PpS0=PpS0, PpS1=PpS1):
                    for ct in range(4):
                        cs = slice(ct * 128, (ct + 1) * 128)
                        o = banks[1][:, cs]
                        e.matmul(o, lhsT=cur[:, cs], rhs=PcS, start=True, stop=False)
                        e.matmul(o, lhsT=prevS[0:120, 0, cs], rhs=PpS0[0:120, :], start=False, stop=False)
                        ins = e.matmul(o, lhsT=prevS[0:120, 1, cs], rhs=PpS1[0:120, :], start=False, stop=True)
                    return ins
                S.add("pe", _pl, cost=1.0, r=[T_cur, T_prevS, T_big8], w=[pP])
            elif c == 0:
                def _pl(e, cur=cur, P0hi=P0hi, P0lo=P0lo):
                    for ct in range(4):
                        cs = slice(ct * 128, (ct + 1) * 128)
                        o = banks[1][:, cs]
                        e.matmul(o, lhsT=cur[:, cs], rhs=P0hi, start=True, stop=False)
                        ins = e.matmul(o, lhsT=cur[:, cs], rhs=P0lo, start=False, stop=True)
                    return ins
                S.add("pe", _pl, cost=1.0, r=[T_cur, T_big8], w=[pP])
            else:
                pu, T_pu = prev_u

                def _pl(e, cur=cur, pu=pu, Pcur=Pcur, Pprev=Pprev):
                    for ct in range(4):
                        cs = slice(ct * 128, (ct + 1) * 128)
                        o = banks[1][:, cs]
                        e.matmul(o, lhsT=cur[:, cs], rhs=Pcur, start=True, stop=False)
                        ins = e.matmul(o, lhsT=pu[:, cs], rhs=Pprev, start=False, stop=True)
                    return ins
                S.add("pe", _pl, cost=1.0, r=[T_cur, T_pu, T_big8], w=[pP])
            prev_u = (cur, T_cur)
            S.add("act", lambda e: e.activation(out=plT, in_=v3(banks[1][:], 4), func=AF.Identity), r=[pP], w=[T_plT])

            def _mx(e, wmix=wmix):
                for dt_ in range(4):
                    for ct in range(4):
                        ins = e.matmul(banks[2][:, dt_ * 128:(dt_ + 1) * 128], lhsT=wmix[:, ct, dt_ * 128:(dt_ + 1) * 128], rhs=plT[:, ct, :],
                                       start=(ct == 0), stop=(ct == 3))
                return ins
            S.add("pe", _mx, cost=1.6, r=T_wm + [T_plT], w=[pM])

            def _zt(e, win=win, tok=tok):
                for dt_ in range(4):
                    for k in range(8):
                        ins = e.matmul(banks[3][:, dt_ * 128:(dt_ + 1) * 128], lhsT=win[:, k, 512 + dt_ * 128: 512 + (dt_ + 1) * 128], rhs=hnT_t[:, k, tok],
                                       start=(k == 0), stop=(k == 7))
                return ins
            S.add("pe", _zt, cost=3.5, r=T_wi + [hnT[c]], w=[pZ])
            S.add("act", lambda e: e.activation(out=szT, in_=v3(banks[3][:], 4), func=AF.Silu), r=[pZ], w=[T_szT])

            def _m2(e, g=g):
                for dt_ in range(4):
                    ins = e.scalar_tensor_tensor(out=m2T[:, dt_, :], in0=banks[2][:, dt_ * 128:(dt_ + 1) * 128], scalar=pscT[:, g * 4 + dt_: g * 4 + dt_ + 1],
                                                 in1=szT[:, dt_, :], op0=ALU.mult, op1=ALU.mult)
                return ins
            S.add("dve", _m2, cost=0.7, r=[pM, T_szT, T_lvec], w=[T_m2T])

            def _out(e, wout=wout):
                for half in range(2):
                    for dt_ in range(4):
                        ins = e.matmul(bankGH[:, half * 512:(half + 1) * 512], lhsT=m2T[:, dt_, :], rhs=wout[:, dt_, half * 512:(half + 1) * 512],
                                       start=(dt_ == 0), stop=(dt_ == 3))
                return ins
            S.add("pe", _out, cost=1.2, r=[T_m2T] + T_wo, w=[pO])
            if samp:
                S.add("dve", lambda e: e.tensor_tensor(out=bankGH[:], in0=bankGH[:], in1=gt1s[:], op=ALU.mult), r=[pO, T_gt1s], w=[pO])
                S.add("dve", lambda e, c=c: e.tensor_tensor(out=res_t[:, c, :], in0=res_t[:, c, :], in1=bankGH[:], op=ALU.add),
                      r=[res[c], pO], w=[res[c]], cost=1.2)
                S.add("pool", lambda e, wout=wout: e.tensor_tensor(out=wout, in0=wout, in1=gt1p[:].unsqueeze(1).to_broadcast([128, 4, 1024]), op=ALU.mult),
                      r=T_wo + [T_gt1p], w=T_wo)
            else:
                S.add("dve", lambda e, c=c: e.tensor_tensor(out=res_t[:, c, :], in0=res_t[:, c, :], in1=bankGH[:], op=ALU.add),
                      r=[res[c], pO], w=[res[c]], cost=1.2)
            if g == 0 and c == SC:
                ckpt("Q1")
            if g == 0 and c == 0:
                ckpt("Q2")
            if g == 0 and c == 1:
                ckpt("Q3")


def _prep_shared(inp):
    f = lambda a: np.ascontiguousarray(a, dtype=np.float32)
    sh = {}
    sh["ada_w"] = f(inp["ada_w"])
    sh["ada_b"] = f(inp["ada_b"])
    sh["normg_bc"] = f(np.broadcast_to(inp["norm_g"][:, None, :], (4, 128, 1024)))
    sh["fng_bc"] = f(np.broadcast_to(inp["final_norm_g"][None, :], (128, 1024)))
    w_in = inp["ssd_w_in"]
    wsi = np.empty((2, 8, 128, 8, 768), np.float32)
    for g in range(8):
        cols = np.concatenate([2048 + 256 * g + np.arange(256), 4096 + 128 * g + np.arange(128),
                               5120 + 128 * g + np.arange(128), 256 * g + np.arange(256)])
        blk = w_in[:, :, cols].reshape(2, 8, 128, 768)
        wsi[:, g] = blk.transpose(0, 2, 1, 3)
    sh["w_ssd_in"] = wsi.reshape(2, 8, 128, 8 * 768)
    sh["w_ssd_dt"] = f(w_in[:, :, 6144:6176].reshape(2, 8, 128, 32).transpose(0, 2, 1, 3)).reshape(2, 128, 256)
    wo = inp["ssd_w_out"].reshape(2, 8, 2, 128, 1024)
    sh["w_ssd_out"] = f(wo.transpose(0, 1, 3, 2, 4)).reshape(2, 8, 128, 2048)
    cwv = inp["ssd_conv_w"]
    tiles = []
    for g in range(8):
        tiles += [2 * g, 2 * g + 1, 16 + g, 24 + g]
    tiles = np.array(tiles)
    cw = cwv.reshape(2, 4, 32, 128)[:, :, tiles, :]
    sh["convw"] = f(cw.transpose(0, 3, 2, 1)).reshape(2, 128, 128)
    cb = inp["ssd_conv_b"].reshape(2, 32, 128)[:, tiles, :]
    sh["convb"] = f(cb.transpose(0, 2, 1))
    v = np.concatenate([inp["ssd_dt_bias"], inp["ssd_a_log"], inp["ssd_d"]], axis=1)
    sh["vec32"] = f(np.broadcast_to(v[:, None, :], (2, 128, 96)))
    sh["ssd_normg_bc"] = f(np.broadcast_to(inp["ssd_norm_g"][:, None, :], (2, 128, 2048)))
    pw = inp["pool_w_in"]
    wpi = np.empty((2, 4, 128, 8, 1024), np.float32)
    for g in range(4):
        cols = np.concatenate([512 * g + np.arange(512), 2048 + 512 * g + np.arange(512)])
        wpi[:, g] = pw[:, :, cols].reshape(2, 8, 128, 1024).transpose(0, 2, 1, 3)
    sh["w_pool_in"] = wpi.reshape(2, 4, 128, 8192)
    sh["w_pool_mix"] = f(inp["pool_w_group"].reshape(2, 4, 4, 128, 512).transpose(0, 1, 3, 2, 4)).reshape(2, 4, 128, 2048)
    sh["w_pool_out"] = f(inp["pool_w_out"].reshape(2, 4, 4, 128, 1024).transpose(0, 1, 3, 2, 4)).reshape(2, 4, 128, 4096)
    sh["pscaleT"] = f(inp["pool_scale"].reshape(2, 16, 128).transpose(0, 2, 1))
    cb_, cn_, cc_ = _consts()
    sh["cbase"], sh["cneg"], sh["ccol"] = cb_, cn_, cc_
    sh["pmats"] = _pool_mats()
    return sh


def _prep_core(inp, i):
    f = lambda a: np.ascontiguousarray(a, dtype=np.float32)
    d = {}
    xs = inp["x_sample"][16 * i:16 * i + 16].reshape(128, D)
    d["xin"] = f(np.concatenate([inp["x_prompt"][i], xs], axis=0))
    c = np.concatenate([inp["c_prompt"][i:i + 1], inp["c_sample"][16 * i:16 * i + 16]], axis=0)
    d["cT"] = f(c.reshape(17, 8, 128).transpose(2, 1, 0)).reshape(128, 136)
    ss = inp["state_ssm"][:, 16 * i:16 * i + 16]
    ss = ss.reshape(2, 16, 8, 256, 128).transpose(0, 2, 4, 1, 3)
    d["st_ssm"] = f(ss)
    sc = inp["state_conv"][:, 16 * i:16 * i + 16]
    sc = sc.reshape(2, 16, 3, 32, 128)
    tiles = []
    for g in range(8):
        tiles += [2 * g, 2 * g + 1, 16 + g, 24 + g]
    sc = sc[:, :, :, np.array(tiles), :].reshape(2, 16, 3, 8, 4, 128)
    d["st_conv"] = f(sc.transpose(0, 3, 5, 4, 1, 2)).reshape(2, 8, 128, 192)
    d["st_pool"] = f(inp["state_pool"][:, 16 * i:16 * i + 16].reshape(2, 240, 2048))
    return d


_TILES = None


def _conv_tiles():
    t = []
    for g in range(8):
        t += [2 * g, 2 * g + 1, 16 + g, 24 + g]
    return np.array(t)


def _assemble(results, NL=4):
    nssd, npool = (NL + 1) // 2, NL // 2
    y_p = np.stack([r["yout"][:2048] for r in results]).astype(np.float32)
    y_s = np.concatenate([r["yout"][2048:].reshape(16, 8, D) for r in results]).astype(np.float32)
    sp = np.stack([r["o_ssm_p"] for r in results], axis=1)
    ssm_p = sp.reshape(2, 8, 8, 128, 4, 64).transpose(0, 1, 2, 4, 5, 3).reshape(2, 8, 32, 64, 128)
    ss = np.stack([r["o_ssm_s"] for r in results], axis=1)
    ssm_s = ss.reshape(2, 8, 8, 128, 16, 4, 64).transpose(0, 1, 4, 2, 5, 6, 3).reshape(2, 128, 32, 64, 128)
    tiles = _conv_tiles()
    inv = np.argsort(tiles)
    cp = np.stack([r["o_conv_p"] for r in results], axis=1)
    cp = cp.reshape(2, 8, 8, 128, 4, 3).transpose(0, 1, 5, 2, 4, 3).reshape(2, 8, 3, 32, 128)[:, :, :, inv, :]
    conv_p = cp.reshape(2, 8, 3, 4096)
    cs = np.stack([r["o_conv_s"] for r in results], axis=1)
    cs = cs.reshape(2, 8, 8, 128, 4, 16, 3).transpose(0, 1, 5, 6, 2, 4, 3).reshape(2, 128, 3, 32, 128)[:, :, :, inv, :]
    conv_s = cs.reshape(2, 128, 3, 4096)
    pool_p = np.stack([r["o_pool_p"] for r in results], axis=1)
    pool_s = np.concatenate([r["o_pool_s"] for r in results], axis=1)
    c = lambda a: np.ascontiguousarray(a, dtype=np.float32)
    return (c(y_p), c(y_s), c(ssm_p[:nssd]), c(conv_p[:nssd]), c(pool_p[:npool]), c(ssm_s[:nssd]), c(conv_s[:nssd]), c(pool_s[:npool]))


_NC_CACHE = {}


def kernel(_NL=4, **inputs):
    inputs = {k: np.asarray(v) for k, v in inputs.items()}
    if _NL not in _NC_CACHE:
        _NC_CACHE[_NL] = build_nc(_NL)
    nc = _NC_CACHE[_NL]
    sh = _prep_shared(inputs)
    in_maps = []
    for i in range(NCORES):
        d = dict(sh)
        d.update(_prep_core(inputs, i))
        in_maps.append(d)
    res = run_bass_kernel_spmd(nc, in_maps, core_ids=list(range(NCORES)))
    return _assemble(res.results, _NL)
```

```python
import numpy as np
import concourse.bass as bass
import concourse.mybir as mybir
from concourse.bass_utils import run_bass_kernel_spmd

F32 = mybir.dt.float32
BF16 = mybir.dt.bfloat16
AF = mybir.ActivationFunctionType
ALU = mybir.AluOpType

NCORES = 8
D = 1024
NCH = 17
SC = 16
NTOK = NCH * 128
EPS = 1e-6
POOL_W = (2, 4, 8, 16)
NEG = -30000.0
SAME_SYNC = True
SEG = 4000
ND = 8
STOP = None


class StopBuild(Exception):
    pass


def ckpt(name):
    if STOP == name:
        raise StopBuild()


class Tl:
    __slots__ = ("ap", "key", "lw", "rd", "bank")

    def __init__(self, ap, key=None, bank=None):
        self.ap = ap
        self.key = key if key is not None else self
        self.lw = None
        self.rd = []
        self.bank = bank


class Op:
    __slots__ = ("eng", "fn", "deps", "dma", "idx", "cost")


LIST_SCHED = True
LAT = 1.0
EXCL_ALL = True
_DEF_COST = {"pe": 0.6, "act": 0.35, "dve": 0.35, "pool": 0.5, "sp": 0.1}


class Sched:
    ENG = ("pe", "act", "dve", "pool", "sp")

    def __init__(self):
        self.ops = []
        self.bar = {}
        self.lastop = {}
        self.dma_since = []
        self.out_dmas = []
        self.bank_last = {}
        self.cur_bar = None
        self.bar_fn = None

    def add(self, eng, fn, r=(), w=(), dma=0, nobar=False, out=False, cost=None):
        o = Op()
        o.cost = cost if cost is not None else (2.5 if dma else _DEF_COST[eng])
        o.idx = len(self.ops)
        o.eng = eng
        o.fn = fn
        o.dma = dma
        deps = set()
        for t in r:
            k = t.key
            if k.lw is not None:
                deps.add(k.lw)
        for t in w:
            k = t.key
            if k.lw is not None:
                deps.add(k.lw)
            deps.update(k.rd)
        for t in r:
            t.key.rd.append(o.idx)
        for t in w:
            k = t.key
            k.lw = o.idx
            k.rd = []
        for t in list(r) + list(w):
            if t.bank is not None:
                bl = self.bank_last.setdefault(t.bank, {})
                for e2, i2 in bl.items():
                    if e2 != eng and (EXCL_ALL or (e2 != "pe" and eng != "pe")):
                        deps.add(i2)
        for t in list(r) + list(w):
            if t.bank is not None:
                self.bank_last[t.bank][eng] = o.idx
        if not nobar and self.cur_bar is not None:
            deps.add(self.cur_bar)
        deps.discard(o.idx)
        o.deps = deps
        self.ops.append(o)
        self.lastop[eng] = o.idx
        if dma:
            self.dma_since.append(o.idx)
            if out:
                self.out_dmas.append(o.idx)
        return o.idx

    def barrier(self):
        deps = set(self.lastop.values()) | set(self.dma_since)
        self.dma_since = []
        idx = self.add("dve", self.bar_fn, nobar=True, cost=0.1)
        self.ops[idx].deps |= deps
        self.ops[idx].deps.discard(idx)
        self.cur_bar = idx

    def list_schedule(self):
        import heapq
        ops = self.ops
        n = len(ops)
        succ = [[] for _ in range(n)]
        indeg = [0] * n
        for o in ops:
            for d in o.deps:
                succ[d].append(o.idx)
            indeg[o.idx] = len(o.deps)
        ready_t = [0.0] * n
        fin = [0.0] * n
        free = {e: 0.0 for e in self.ENG}
        heaps = {e: [] for e in self.ENG}
        for o in ops:
            if indeg[o.idx] == 0:
                heapq.heappush(heaps[o.eng], (0.0, o.idx))
        order = []
        while len(order) < n:
            best = None
            for e in self.ENG:
                h = heaps[e]
                if not h:
                    continue
                rt, idx = h[0]
                st = max(rt, free[e])
                if best is None or (st, idx) < (best[0], best[1]):
                    best = (st, idx, e)
            st, idx, e = best
            heapq.heappop(heaps[e])
            o = ops[idx]
            issue = 0.06 if o.dma else o.cost
            free[e] = st + issue
            fin[idx] = st + o.cost
            order.append(idx)
            for s_ in succ[idx]:
                lat = LAT if (ops[s_].eng != e or o.dma) else 0.05
                ready_t[s_] = max(ready_t[s_], fin[idx] + lat)
                indeg[s_] -= 1
                if indeg[s_] == 0:
                    heapq.heappush(heaps[ops[s_].eng], (ready_t[s_], s_))
        return order

    def emit(self, nc):
        ops = self.ops
        order = self.list_schedule() if LIST_SCHED else list(range(len(ops)))
        ops_sorted = [ops[i] for i in order]
        needs = [False] * len(ops)
        for o in ops:
            for d in o.deps:
                po = ops[d]
                if po.dma:
                    continue
                if po.eng != o.eng or SAME_SYNC:
                    needs[d] = True
        cnt = {e: 0 for e in self.ENG}
        val = {}
        for o in ops_sorted:
            if o.dma:
                continue
            if needs[o.idx]:
                cnt[o.eng] += 1
                val[o.idx] = cnt[o.eng]
        esems = {e: [nc.alloc_semaphore(f"c_{e}_{i}") for i in range(cnt[e] // SEG + 1)] for e in self.ENG}
        dpool = {e: [nc.alloc_semaphore(f"d_{e}_{i}") for i in range(ND)] for e in ("sp", "pool", "act")}
        dtot = {e: [0] * ND for e in dpool}
        dcount = {e: 0 for e in dpool}
        dinfo = {}
        for o in ops_sorted:
            if o.dma:
                j = dcount[o.eng] % ND
                dcount[o.eng] += 1
                prev = dtot[o.eng][j]
                dtot[o.eng][j] = prev + 16 * o.dma
                dinfo[o.idx] = (dpool[o.eng][j], prev, dtot[o.eng][j])
        by_eng = {e: [o for o in ops_sorted if o.eng == e] for e in self.ENG}

        def run(eng_name, e):
            waited_c = {x: 0 for x in self.ENG}
            waited_d = {}
            for o in by_eng[eng_name]:
                for d in sorted(o.deps):
                    po = ops[d]
                    if po.dma:
                        sem, _, tot = dinfo[d]
                        if waited_d.get(sem.num if hasattr(sem, "num") else id(sem), 0) < tot:
                            e.wait_ge(sem, tot)
                            waited_d[sem.num if hasattr(sem, "num") else id(sem)] = tot
                    elif po.eng != eng_name or SAME_SYNC:
                        n = val[d]
                        if waited_c[po.eng] < n:
                            e.wait_ge(esems[po.eng][(n - 1) // SEG], (n - 1) % SEG + 1)
                            waited_c[po.eng] = n
                if o.dma:
                    sem, prev, tot = dinfo[o.idx]
                    key = sem.num if hasattr(sem, "num") else id(sem)
                    if prev > 0 and waited_d.get(key, 0) < prev:
                        e.wait_ge(sem, prev)
                        waited_d[key] = prev
                    ins = o.fn(e)
                    if not isinstance(ins, (list, tuple)):
                        ins = [ins]
                    assert len(ins) == o.dma, (len(ins), o.dma)
                    for i_ in ins:
                        i_.then_inc(sem, 16)
                else:
                    ins = o.fn(e)
                    if isinstance(ins, (list, tuple)):
                        ins = ins[-1]
                    if needs[o.idx] and ins is not None:
                        n = val[o.idx]
                        ins.then_inc(esems[eng_name][(n - 1) // SEG], 1)

        with nc.Block() as block:
            @block.tensor
            def _(e):
                run("pe", e)

            @block.scalar
            def _(e):
                run("act", e)

            @block.vector
            def _(e):
                run("dve", e)

            @block.gpsimd
            def _(e):
                run("pool", e)

            @block.sync
            def _(e):
                run("sp", e)


def _consts():
    k = np.arange(128)
    seq = k // 8
    ident = np.eye(128, dtype=np.float32)
    Lp = (k[:, None] <= k[None, :]).astype(np.float32)
    same = (seq[:, None] == seq[None, :])
    Ls = (same & (k[:, None] <= k[None, :])).astype(np.float32)
    Ss = same.astype(np.float32)
    NEGp = np.where(k[:, None] <= k[None, :], 0.0, NEG).astype(np.float32)
    NEGs = np.where(same & (k[:, None] <= k[None, :]), 0.0, NEG).astype(np.float32)
    u = np.arange(224)
    colmask = ((u[None, :] >= 96 + 8 * np.arange(4)[:, None]) & (u[None, :] < 104 + 8 * np.arange(4)[:, None])).astype(np.float32)
    colmask = np.broadcast_to(colmask.reshape(1, 4 * 224), (128, 4 * 224))
    rowmask = (seq[:, None] == np.arange(16)[None, :]).astype(np.float32)
    base = np.concatenate([ident, Lp, Ls, Ss, rowmask], axis=1)
    negs = np.concatenate([np.tile(NEGp, (1, 4)), np.tile(NEGs, (1, 4))], axis=1)
    return base.astype(np.float32), negs.astype(np.float32), np.ascontiguousarray(colmask, dtype=np.float32)


def _pool_mats():
    import ml_dtypes
    s = np.arange(128)[:, None]
    t = np.arange(128)[None, :]
    seq_s, ts = s // 8, s % 8
    seq_t, tt = t // 8, t % 8
    mats = []
    for w in POOL_W:
        band = ((t - s >= 0) & (t - s < w)).astype(np.float64)
        pcur = band / w - np.eye(128)
        cnt0 = np.minimum(w, np.arange(128) + 1)[None, :]
        p0 = band / cnt0 - np.eye(128)
        p0hi = p0.astype(np.float32).astype(ml_dtypes.bfloat16).astype(np.float64)
        p0lo = p0 - p0hi
        pprev = (s > 128 + t - w).astype(np.float64) / w
        pcs = ((seq_s == seq_t) & (tt - ts >= 0) & (tt - ts < w)).astype(np.float64) / w - np.eye(128)
        pps = []
        for hf in range(2):
            r = np.arange(128)[:, None]
            sl, j = r // 15, r % 15
            m = ((r < 120) & (seq_t == 8 * hf + sl) & (j > 15 + tt - w)).astype(np.float64) / w
            pps.append(m)
        mats += [pcur, p0hi, p0lo, pprev, pcs, pps[0], pps[1]]
    return np.concatenate(mats, axis=1).astype(np.float32)


def build_nc(NL=4):
    nc = bass.Bass("TRN2", target_bir_lowering=False)
    S = Sched()
    NSSD = (NL + 1) // 2
    NPOOL = NL // 2

    def din(name, shape):
        return nc.dram_tensor(name, list(shape), F32, kind="ExternalInput").ap()

    def dout(name, shape):
        return nc.dram_tensor(name, list(shape), F32, kind="ExternalOutput").ap()

    xin = din("xin", [NTOK, D])
    cTd = din("cT", [128, 8 * 17])
    st_ssm = din("st_ssm", [2, 8, 128, 16, 256])
    st_conv = din("st_conv", [2, 8, 128, 4 * 16 * 3])
    st_pool = din("st_pool", [2, 240, 2048])
    ada_w = din("ada_w", [4, 1024, 3072])
    ada_b = din("ada_b", [4, 3072])
    normg_d = din("normg_bc", [4, 128, 1024])
    fng_d = din("fng_bc", [128, 1024])
    wsi = din("w_ssd_in", [2, 8, 128, 8 * 768])
    wsd = din("w_ssd_dt", [2, 128, 8 * 32])
    wso = din("w_ssd_out", [2, 8, 128, 2 * 1024])
    convw_d = din("convw", [2, 128, 32 * 4])
    convb_d = din("convb", [2, 128, 32])
    vec32_d = din("vec32", [2, 128, 96])
    sng_d = din("ssd_normg_bc", [2, 128, 2048])
    wpi = din("w_pool_in", [2, 4, 128, 8 * 1024])
    wpm = din("w_pool_mix", [2, 4, 128, 4 * 512])
    wpo = din("w_pool_out", [2, 4, 128, 4 * 1024])
    psc_d = din("pscaleT", [2, 128, 16])
    cbase_d = din("cbase", [128, 528])
    cneg_d = din("cneg", [128, 1024])
    ccol_d = din("ccol", [128, 896])
    pmat_d = din("pmats", [128, 28 * 128])

    yout = dout("yout", [NTOK, D])
    o_ssm_p = dout("o_ssm_p", [2, 8, 128, 256])
    o_ssm_s = dout("o_ssm_s", [2, 8, 128, 16, 256])
    o_conv_p = dout("o_conv_p", [2, 8, 128, 12])
    o_conv_s = dout("o_conv_s", [2, 8, 128, 192])
    o_pool_p = dout("o_pool_p", [2, 15, 2048])
    o_pool_s = dout("o_pool_s", [2, 16, 15, 2048])

    def sb(name, shape, dt=F32):
        return nc.alloc_sbuf_tensor("s_" + name, list(shape), dt)

    res_t = sb("res", [128, NCH, D])
    res = [Tl(res_t[:, c, :]) for c in range(NCH)]
    hnT_t = sb("hnT", [128, 8, NTOK], BF16)
    hnT = [Tl(hnT_t[:, :, c * 128:(c + 1) * 128]) for c in range(NCH)]
    wslot_t = [sb(f"wslot{i}", [128, 8192], BF16) for i in range(2)]
    wslot = [[Tl(t[:, q * 2048:(q + 1) * 2048]) for q in range(4)] for t in wslot_t]
    cbase = sb("cbase", [128, 528])
    identf = cbase[:, 0:128]
    Lp_ap, Ls_ap, Ss_ap = cbase[:, 128:256], cbase[:, 256:384], cbase[:, 384:512]
    rowmask = cbase[:, 512:528]
    T_cbase = Tl(cbase[:])
    identb_t = sb("identb", [128, 128], BF16)
    T_identb = Tl(identb_t[:])
    cneg = sb("cneg", [128, 1024], BF16)
    T_cneg = Tl(cneg[:])
    ccol = sb("ccol", [128, 4, 224], BF16)
    T_ccol = Tl(ccol[:])
    ones_t = sb("ones", [128, 256])
    onesf, negonesf = ones_t[:, 0:128], ones_t[:, 128:256]
    T_ones = Tl(ones_t[:])
    cT = sb("cTs", [128, 8, 17])
    T_cT = Tl(cT[:])
    gt1p = sb("gt1p", [128, D]); T_gt1p = Tl(gt1p[:])
    gt1s = sb("gt1s", [128, D]); T_gt1s = Tl(gt1s[:])
    tmp = None; T_tmp = None
    big8 = sb("big8", [128, 2048])
    T_big8 = Tl(big8[:])
    pmat = big8[:].bitcast(BF16)
    lvec = sb("lvec", [128, 128 + 32 + 96 + 16])
    convw, convb, vec32, pscT = lvec[:, 0:128], lvec[:, 128:160], lvec[:, 160:256], lvec[:, 256:272]
    T_lvec = Tl(lvec[:])
    wdt_t = sb("wdt", [128, 8, 32], BF16); T_wdt = Tl(wdt_t[:])
    pp = sb("pp", [128, 5, NCH * 32])
    dt_a, dtd_a, nac_a, eac_a, dat_a = (pp[:, i, :].rearrange("p (c h) -> p c h", c=NCH) for i in range(5))
    T_pp = [Tl(pp[:, i, :]) for i in range(5)] + [None]
    dats = sb("dats", [128, 16, 32]); T_dats = Tl(dats[:])
    small = sb("small", [128, 16])
    T_ssq, T_rt, T_rstd = Tl(small[:, 0:1]), Tl(small[:, 1:2]), Tl(small[:, 2:3])
    Acol = sb("Acol", [128, 32]); T_Acol = Tl(Acol[:])

    WK = sb("wk", [128, 35 * 1024 // 4])
    wk_off = [0]

    def wk_reset():
        wk_off[0] = 0

    def wk(shape, dt=F32):
        n = int(np.prod(shape[1:]))
        nb = n * (4 if dt == F32 else 2)
        nb4 = (nb + 31) // 32 * 8
        a = WK[:, wk_off[0]:wk_off[0] + nb4]
        wk_off[0] += nb4
        assert wk_off[0] <= 35 * 1024 // 4, wk_off[0]
        if dt == BF16:
            a = a.bitcast(BF16)[:, 0:n]
        else:
            a = a[:, 0:n]
        if len(shape) == 3:
            a = a.rearrange("p (a b) -> p a b", a=shape[1])
        elif len(shape) == 4:
            a = a.rearrange("p (a b c) -> p a b c", a=shape[1], b=shape[2])
        return a

    banks = [nc.alloc_psum_tensor(f"bank{i}", [128, 512], F32) for i in range(6)]
    bankGH = nc.alloc_psum_tensor("bankGH", [128, 1024], F32)

    TB = [Tl(b_[:], bank=i_) for i_, b_ in enumerate(banks)]

    def v3(ap, a):
        return ap.rearrange("p (a b) -> p a b", a=a)

    def bc_last(ap2, n):
        return ap2.unsqueeze(2).to_broadcast([128, ap2.shape[1], n])

    S.add("sp", lambda e: e.dma_start(out=cbase[:], in_=cbase_d), w=[T_cbase], dma=1)
    S.add("pool", lambda e: e.dma_start(out=cneg[:], in_=cneg_d), w=[T_cneg], dma=1)
    S.add("pool", lambda e: e.dma_start(out=ccol[:], in_=ccol_d.rearrange("p (a b) -> p a b", a=4)), w=[T_ccol], dma=1)
    S.add("sp", lambda e: e.dma_start(out=cT[:].rearrange("p a b -> p (a b)"), in_=cTd), w=[T_cT], dma=1)

    def _ones(e):
        e.memset(onesf, 1.0)
        return e.memset(negonesf, -1.0)
    S.add("dve", _ones, w=[T_ones])
    S.add("act", lambda e: e.activation(out=identb_t[:], in_=identf, func=AF.Identity), r=[T_cbase], w=[T_identb])
    for c in range(NCH):
        S.add("sp", lambda e, c=c: e.dma_start(out=res_t[:, c, :], in_=xin[c * 128:(c + 1) * 128, :]), w=[res[c]], dma=1)

    def rms_stats(src_tl, src_ap, junk_tl, junk_ap):
        S.add("act", lambda e: e.activation(out=junk_ap, in_=src_ap, func=AF.Square, accum_out=small[:, 0:1]),
              r=[src_tl], w=[junk_tl, T_ssq])
        S.add("pool", lambda e: e.tensor_scalar(out=small[:, 1:2], in0=small[:, 0:1], scalar1=1.0 / D, scalar2=EPS, op0=ALU.mult, op1=ALU.add),
              r=[T_ssq], w=[T_rt])
        S.add("pool", lambda e: e.tensor_tensor(out=small[:, 2:3], in0=small[:, 1:2], in1=small[:, 10:11], op=ALU.pow), r=[T_rt, T_eps], w=[T_rstd])

    T_eps = Tl(small[:, 8:11])
    S.bar_fn = lambda e: e.memset(small[:, 12:13], 0.0)

    def _eps(e):
        e.memset(small[:, 8:9], EPS)
        e.memset(small[:, 10:11], -0.5)
        return e.memset(small[:, 9:10], 1.0)
    S.add("dve", _eps, w=[T_eps])

    wctr = [0]

    def next_slot():
        i = wctr[0] % 2
        wctr[0] += 1
        return i

    try:
        _layers(locals())
    except StopBuild:
        S.barrier()
    _final(locals())
    return nc


def _layers(L_):
    globals().update({k: v for k, v in L_.items() if not k.startswith("__")})
    L_ = dict(L_)
    for li in range(NL):
        is_ssd = (li % 2 == 0)
        lj = li // 2
        wk_reset()
        G_p, SH_p, G_s, SH_s = (wk([128, D]) for _ in range(4))
        T_G_p, T_SH_p, T_G_s, T_SH_s = Tl(G_p), Tl(SH_p), Tl(G_s), Tl(SH_s)
        scTp = wk([128, 8, 128], BF16); T_scTp = Tl(scTp)
        scTs = wk([128, 8, 128], BF16); T_scTs = Tl(scTs)
        hnb = wk([128, D], BF16); T_hnb = Tl(hnb)
        junk, T_junk = hnb, T_hnb
        adab1 = wk([128, 512]); adab = [adab1, adab1]
        T_adab1 = Tl(adab1); T_adab = [T_adab1, T_adab1]
        scr2 = wk([128, NCH * 32]); scr_a = scr2.rearrange("p (c h) -> p c h", c=NCH)
        T_pp[5] = Tl(scr2)
        Dg = wk([128, 16, 32]); T_Dg = Tl(Dg)
        tmpA = wk([128, D]); T_tmpA = Tl(tmpA)
        normg = big8[:, 0:1024]
        S.add("sp", lambda e, li=li: e.dma_start(out=normg, in_=normg_d[li]), w=[T_big8], dma=1)
        S.add("act", lambda e: e.activation(out=scTp, in_=cT[:, :, 0:1].to_broadcast([128, 8, 128]), func=AF.Silu),
              r=[T_cT], w=[T_scTp])
        S.add("act", lambda e: e.activation(out=scTs.rearrange("p k (s t) -> p k s t", s=16),
                                            in_=cT[:, :, 1:17].unsqueeze(3).to_broadcast([128, 8, 16, 8]), func=AF.Silu),
              r=[T_cT], w=[T_scTs])
        psA = [TB[0], TB[1]]
        psB = [TB[2], TB[3]]
        for nb in range(6):
            si = next_slot()
            wv = wslot_t[si][:, 0:4096].rearrange("p (k n) -> p k n", k=8)
            S.add("pool", lambda e, wv=wv, li=li, nb=nb: e.dma_start(
                out=wv, in_=ada_w[li].rearrange("(k p) n -> p k n", p=128)[:, :, nb * 512:(nb + 1) * 512]),
                w=wslot[si][0:2], dma=1, nobar=True)
            ab = nb % 2
            S.add("sp", lambda e, ab=ab, li=li, nb=nb: e.dma_start(out=adab[ab][0:1, :], in_=ada_b[li:li + 1, nb * 512:(nb + 1) * 512]),
                  w=[T_adab[ab]], dma=1)
            for which, (lhs, T_lhs, pst, pbank) in enumerate(((scTp, T_scTp, psA[nb % 2], banks[nb % 2]),
                                                               (scTs, T_scTs, psB[nb % 2], banks[2 + nb % 2]))):
                def _mm(e, lhs=lhs, pbank=pbank, wv=wv, ab=ab):
                    e.matmul(pbank[:], lhsT=onesf[0:1, :], rhs=adab[ab][0:1, :], start=True, stop=False)
                    for k in range(8):
                        ins = e.matmul(pbank[:], lhsT=lhs[:, k, :], rhs=wv[:, k, :], start=False, stop=(k == 7))
                    return ins
                S.add("pe", _mm, cost=2.0, r=[T_lhs, T_adab[ab], T_ones] + wslot[si][0:2], w=[pst])
                blk = slice((nb % 2) * 512, (nb % 2) * 512 + 512)
                if nb < 2:
                    dst, T_dst = (SH_p, T_SH_p) if which == 0 else (SH_s, T_SH_s)
                    S.add("act", lambda e, dst=dst, pbank=pbank, blk=blk: e.activation(out=dst[:, blk], in_=pbank[:], func=AF.Identity),
                          r=[pst], w=[T_dst])
                elif nb < 4:
                    dst, T_dst = (G_p, T_G_p) if which == 0 else (G_s, T_G_s)
                    S.add("dve", lambda e, dst=dst, pbank=pbank, blk=blk: e.scalar_tensor_tensor(
                        out=dst[:, blk], in0=pbank[:], scalar=1.0, in1=normg[:, blk], op0=ALU.add, op1=ALU.mult),
                        r=[pst, T_big8], w=[T_dst])
                else:
                    dst, T_dst = (gt1p, T_gt1p) if which == 0 else (gt1s, T_gt1s)
                    S.add("dve", lambda e, dst=dst, pbank=pbank, blk=blk: e.tensor_scalar(
                        out=dst[:, blk], in0=pbank[:], scalar1=1.0, scalar2=None, op0=ALU.add),
                        r=[pst], w=[T_dst])
        ckpt("MOD%d" % li)
        psT = [TB[4], TB[5]]
        for c in range(NCH):
            G, T_G, SH, T_SH = (G_p, T_G_p, SH_p, T_SH_p) if c < SC else (G_s, T_G_s, SH_s, T_SH_s)
            rms_stats(res[c], res_t[:, c, :], T_junk, junk)
            S.add("dve", lambda e, c=c, G=G: e.scalar_tensor_tensor(out=tmpA, in0=res_t[:, c, :], scalar=small[:, 2:3], in1=G,
                                                                   op0=ALU.mult, op1=ALU.mult),
                  r=[res[c], T_rstd, T_G], w=[T_tmpA])
            S.add("pool", lambda e, SH=SH: e.tensor_tensor(out=hnb, in0=tmpA, in1=SH, op=ALU.add), r=[T_tmpA, T_SH], w=[T_hnb])
            pb = banks[4 + c % 2]
            pbv = pb[:].bitcast(BF16)

            def _tr(e, pbv=pbv):
                for k in range(8):
                    ins = e.transpose(out=pbv[:, k * 128:(k + 1) * 128], in_=hnb[:, k * 128:(k + 1) * 128], identity=identb_t[:])
                return ins
            S.add("pe", _tr, cost=0.9, r=[T_hnb, T_identb], w=[psT[c % 2]])
            S.add("act", lambda e, c=c, pbv=pbv: e.activation(out=hnT_t[:, :, c * 128:(c + 1) * 128],
                                                             in_=pbv[:, 0:1024].rearrange("p (k t) -> p k t", k=8), func=AF.Identity),
                  r=[psT[c % 2]], w=[hnT[c]])

        ckpt("A%d" % li)
        if is_ssd:
            S.add("sp", lambda e, lj=lj: e.dma_start(out=convw, in_=convw_d[lj]), w=[T_lvec], dma=1)
            S.add("sp", lambda e, lj=lj: e.dma_start(out=convb, in_=convb_d[lj]), w=[T_lvec], dma=1)
            S.add("sp", lambda e, lj=lj: e.dma_start(out=vec32, in_=vec32_d[lj]), w=[T_lvec], dma=1)
            S.add("pool", lambda e, lj=lj: e.dma_start(out=wdt_t[:].rearrange("p a b -> p (a b)"), in_=wsd[lj]), w=[T_wdt], dma=1)
            dtb, alog, Dcol = vec32[:, 0:32], vec32[:, 32:64], vec32[:, 64:96]
            S.add("act", lambda e: e.activation(out=Acol[:], in_=alog, func=AF.Exp), r=[T_lvec], w=[T_Acol])
            S.add("dve", lambda e: e.tensor_scalar(out=Acol[:], in0=Acol[:], scalar1=-1.0, scalar2=None, op0=ALU.mult),
                  r=[T_Acol], w=[T_Acol])
            T_psd = [TB[0], TB[1]]
            pd0 = banks[0][:, 0:512].rearrange("p (c h) -> p c h", c=16)
            pd1 = banks[1][:, 0:32]

            def _dtmm(e):
                for c in range(NCH):
                    o = pd0[:, c, :] if c < 16 else pd1
                    for k in range(8):
                        ins = e.matmul(o, lhsT=hnT_t[:, k, c * 128:(c + 1) * 128], rhs=wdt_t[:, k, :], start=(k == 0), stop=(k == 7))
                return ins
            S.add("pe", _dtmm, cost=8.0, r=hnT + [T_wdt], w=T_psd)
            ckpt("P1")
            def _v(e):
                e.tensor_tensor(out=scr_a[:, 0:16, :], in0=pd0, in1=dtb.unsqueeze(1).to_broadcast([128, 16, 32]), op=ALU.add)
                return e.tensor_tensor(out=scr_a[:, 16, :], in0=pd1, in1=dtb, op=ALU.add)
            S.add("dve", _v, r=T_psd + [T_lvec], w=[T_pp[5]])
            S.add("act", lambda e: e.activation(out=scr2, in_=scr2, func=AF.Exp), r=[T_pp[5]], w=[T_pp[5]])
            S.add("act", lambda e: e.activation(out=pp[:, 0, :], in_=scr2, func=AF.Ln, bias=small[:, 9:10]),
                  r=[T_pp[5], T_eps], w=[T_pp[0]])
            S.add("dve", lambda e: e.tensor_tensor(out=scr_a, in0=dt_a, in1=Acol[:].unsqueeze(1).to_broadcast([128, NCH, 32]), op=ALU.mult),
                  r=[T_pp[0], T_Acol], w=[T_pp[5]])
            ckpt("P2")
            T_pac = [TB[2], TB[3]]
            T_pat = [TB[4], TB[5]]
            pa0 = banks[2][:, 0:512].rearrange("p (c h) -> p c h", c=16); pa1 = banks[3][:, 0:32]
            pt0 = banks[4][:, 0:512].rearrange("p (c h) -> p c h", c=16); pt1 = banks[5][:, 0:32]

            def _acmm(e):
                for c in range(NCH):
                    oa = pa0[:, c, :] if c < 16 else pa1
                    ot = pt0[:, c, :] if c < 16 else pt1
                    e.matmul(oa, lhsT=(Lp_ap if c < 16 else Ls_ap), rhs=scr_a[:, c, :], start=True, stop=True)
                    ins = e.matmul(ot, lhsT=(onesf if c < 16 else Ss_ap), rhs=scr_a[:, c, :], start=True, stop=True)
                return ins
            S.add("pe", _acmm, cost=3.0, r=[T_pp[5], T_cbase, T_ones], w=T_pac + T_pat)
            ckpt("P3")

            def _nac(e):
                e.tensor_scalar(out=nac_a[:, 0:16, :], in0=pa0, scalar1=-1.0, scalar2=None, op0=ALU.mult)
                return e.tensor_scalar(out=nac_a[:, 16, :], in0=pa1, scalar1=-1.0, scalar2=None, op0=ALU.mult)
            S.add("dve", _nac, r=T_pac, w=[T_pp[2]])
            ckpt("P3a")

            def _eac(e):
                e.activation(out=eac_a[:, 0:16, :], in_=pa0, func=AF.Exp)
                e.activation(out=eac_a[:, 16, :], in_=pa1, func=AF.Exp)
                e.activation(out=dat_a[:, 0:16, :], in_=pt0, func=AF.Exp)
                return e.activation(out=dat_a[:, 16, :], in_=pt1, func=AF.Exp)
            S.add("act", _eac, r=T_pac + T_pat, w=[T_pp[3], T_pp[4]])
            ckpt("P3b")

            def _dd(e):
                e.tensor_tensor(out=dtd_a[:, 0:16, :], in0=pt0, in1=nac_a[:, 0:16, :], op=ALU.add)
                return e.tensor_tensor(out=dtd_a[:, 16, :], in0=pt1, in1=nac_a[:, 16, :], op=ALU.add)
            S.add("dve", _dd, r=T_pat + [T_pp[2]], w=[T_pp[1]])
            ckpt("P3c")
            S.add("act", lambda e: e.activation(out=pp[:, 1, :], in_=pp[:, 1, :], func=AF.Exp), r=[T_pp[1]], w=[T_pp[1]])
            S.add("dve", lambda e: e.tensor_tensor(out=pp[:, 1, :], in0=pp[:, 1, :], in1=pp[:, 0, :], op=ALU.mult),
                  r=[T_pp[1], T_pp[0]], w=[T_pp[1]])
            ckpt("P4")
            S.add("sp", lambda e, lj=lj: e.dma_start(out=big8[:], in_=sng_d[lj]), w=[T_big8], dma=1)
            S.add("dve", lambda e: e.tensor_tensor(out=Dg, in0=scr_a[:, 16:17, :].to_broadcast([128, 16, 32]),
                                                   in1=rowmask.unsqueeze(2).to_broadcast([128, 16, 32]), op=ALU.mult),
                  r=[T_pp[5], T_cbase], w=[T_Dg])
            T_pds = TB[0]
            S.add("pe", lambda e: e.matmul(banks[0][:], lhsT=onesf, rhs=Dg.rearrange("p a b -> p (a b)"), start=True, stop=True),
                  r=[T_Dg, T_ones], w=[T_pds])
            S.add("act", lambda e: e.activation(out=dats[:].rearrange("p a b -> p (a b)"), in_=banks[0][:], func=AF.Exp),
                  r=[T_pds], w=[T_dats])
            S.barrier()
            ckpt("PRE%d" % li)
            L2 = dict(L_); L2.update(locals())
            ssd_layer(nc, S, L2, lj)
        else:
            S.add("sp", lambda e, lj=lj: e.dma_start(out=pscT, in_=psc_d[lj]), w=[T_lvec], dma=1)
            S.add("pool", lambda e: e.dma_start(out=pmat[:, 0:3584].rearrange("p (a b) -> p a b", a=28), in_=pmat_d.rearrange("p (a b) -> p a b", a=28)), w=[T_big8], dma=1)
            S.barrier()
            ckpt("PRE%d" % li)
            L2 = dict(L_); L2.update(locals())
            pool_layer(nc, S, L2, lj)
        S.barrier()
        ckpt("L%d" % li)


def _final(L_):
    globals().update({k: v for k, v in L_.items() if not k.startswith("__")})
    wk_reset()
    junk = wk([128, D], BF16); T_junk = Tl(junk)
    yb = [wk([128, D]) for _ in range(2)]; T_yb = [Tl(a) for a in yb]
    S.add("sp", lambda e: e.dma_start(out=big8[:, 0:1024], in_=fng_d), w=[T_big8], dma=1)
    for c in range(NCH):
        rms_stats(res[c], res_t[:, c, :], T_junk, junk)
        S.add("dve", lambda e, c=c: e.scalar_tensor_tensor(out=yb[c % 2], in0=res_t[:, c, :], scalar=small[:, 2:3], in1=big8[:, 0:1024],
                                                           op0=ALU.mult, op1=ALU.mult),
              r=[res[c], T_rstd, T_big8], w=[T_yb[c % 2]])
        S.add("sp", lambda e, c=c: e.dma_start(out=yout[c * 128:(c + 1) * 128, :], in_=yb[c % 2]), r=[T_yb[c % 2]], dma=1, out=True)
    o = Op(); o.idx = len(S.ops); o.eng = "sp"; o.fn = lambda e: None
    o.dma = 0; o.deps = set(S.out_dmas); o.cost = 0.1
    S.ops.append(o)
    S.emit(nc)


def _cat_lvec(lj):
    return None


def ssd_layer(nc, S, L, lj):
    g_ = L
    (banks, bankGH, wk, wk_reset, wslot, wslot_t, next_slot, hnT, hnT_t, res, res_t, tmp, T_tmp) = (
        g_[k] for k in ("banks", "bankGH", "wk", "wk_reset", "wslot", "wslot_t", "next_slot", "hnT", "hnT_t", "res", "res_t", "tmp", "T_tmp"))
    identf, identb_t, T_identb, T_cbase, rowmask = g_["identf"], g_["identb_t"], g_["T_identb"], g_["T_cbase"], g_["rowmask"]
    cneg, T_cneg, ccol, T_ccol = g_["cneg"], g_["T_cneg"], g_["ccol"], g_["T_ccol"]
    negonesf, T_ones = g_["negonesf"], g_["T_ones"]
    convw, convb, vec32, T_lvec = g_["convw"], g_["convb"], g_["vec32"], g_["T_lvec"]
    dt_a, dtd_a, nac_a, eac_a, dat_a = g_["dt_a"], g_["dtd_a"], g_["nac_a"], g_["eac_a"], g_["dat_a"]
    T_pp, dats, T_dats = g_["T_pp"], g_["dats"], g_["T_dats"]
    gt1p, T_gt1p, gt1s, T_gt1s = g_["gt1p"], g_["T_gt1p"], g_["gt1s"], g_["T_gt1s"]
    big8, T_big8, small, T_eps = g_["big8"], g_["T_big8"], g_["small"], g_["T_eps"]
    st_ssm, st_conv, wsi, wso = g_["st_ssm"], g_["st_conv"], g_["wsi"], g_["wso"]
    o_ssm_p, o_ssm_s, o_conv_p, o_conv_s = g_["o_ssm_p"], g_["o_ssm_s"], g_["o_conv_p"], g_["o_conv_s"]
    Dcol = vec32[:, 64:96]

    def v3(ap, a):
        return ap.rearrange("p (a b) -> p a b", a=a)

    wk_reset()
    xps = wk([128, 4, 16, 11]); T_xp = Tl(xps)
    xp = xps.rearrange("p a b c -> p (a b c)")[:, 0:4 * 131].rearrange("p (a b) -> p a b", a=4)
    acc = wk([128, 4, 128]); T_acc = Tl(acc)
    xa = wk([128, 4, 128], BF16); T_xa = Tl(xa)
    xs = wk([128, 256], BF16); T_xs = Tl(xs)
    xdt = wk([128, 256], BF16); T_xdt = Tl(xdt)
    xdtd = wk([128, 256], BF16); T_xdtd = Tl(xdtd)
    Bsb = wk([128, 128], BF16); T_Bsb = Tl(Bsb)
    Dexp = wk([128, 4, 128]); T_Dexp = Tl(Dexp)
    Eb = wk([128, 4, 128]); T_E = Tl(Eb)
    MT = wk([128, 4, 128], BF16); T_MT = Tl(MT)
    y1 = wk([128, 256]); T_y1 = Tl(y1)
    sz = wk([128, 256]); T_sz = Tl(sz)
    junk = sz; T_junk = T_sz
    gn = wk([128, 256], BF16); T_gn = Tl(gn)
    gT = wk([128, 2, 128], BF16); T_gT = Tl(gT)
    hT = wk([128, 256]); T_hT = Tl(hT)
    hTb = wk([128, 256], BF16); T_hTb = Tl(hTb)
    h0f = [wk([128, 4, 256]) for _ in range(2)]; T_h0f = [Tl(a) for a in h0f]
    h0b = [wk([128, 4, 256], BF16) for _ in range(2)]; T_h0b = [Tl(a) for a in h0b]
    Bm = [wk([128, 4, 128], BF16) for _ in range(2)]; T_Bm = [Tl(a) for a in Bm]
    CTm = [wk([128, 4, 128], BF16) for _ in range(2)]; T_CTm = [Tl(a) for a in CTm]
    cvs = wk([128, 192]); T_cvs = Tl(cvs)
    cvp = wk([128, 12]); T_cvp = Tl(cvp)
    sq = wk([128, 4]); T_sq = Tl(sq)

    pA = Tl(banks[0][:], bank=0)
    pZ = Tl(banks[1][:, 0:256], bank=1); pCB = Tl(banks[1][:, 256:384], bank=1)
    pC = banks[2][:].bitcast(BF16)
    pTx = Tl(pC[:, 0:384], bank=2); pTg = Tl(pC[:, 512:768], bank=2)
    pD = Tl(banks[3][:], bank=3)
    pYa = Tl(banks[4][:, 0:256], bank=4); pYb = Tl(banks[4][:, 256:512], bank=4)
    pF = Tl(banks[5][:], bank=5)
    pO = Tl(bankGH[:], bank=6)
    hcnt = [0]

    sz2 = [sz, wk([128, 256])]; T_sz2 = [T_sz, Tl(sz2[1])]
    W = {}; SZ = {}; acnt = [0]
    cw = v3(convw, 32)

    def load_w(g):
        si = next_slot()
        win = wslot_t[si][:, 0:6144].rearrange("p (k n) -> p k n", k=8)
        wout = wslot_t[si][:, 6144:8192].rearrange("p (j n) -> p j n", j=2)
        T_wi = wslot[si][0:3]
        T_wo = wslot[si][3:4]
        S.add("pool", lambda e, win=win, g=g: e.dma_start(out=win, in_=wsi[lj, g].rearrange("p (k n) -> p k n", k=8)),
              w=T_wi, dma=1, nobar=True)
        S.add("pool", lambda e, wout=wout, g=g: e.dma_start(out=wout, in_=wso[lj, g].rearrange("p (j n) -> p j n", j=2)),
              w=T_wo, dma=1, nobar=True)
        W[g] = (win, wout, T_wi, T_wo)

    def emitA(g, c):
        g4 = g * 4
        win, wout, T_wi, T_wo = W[g]
        samp = (c == SC)
        tok = slice(c * 128, (c + 1) * 128)
        si_ = acnt[0] % 2; acnt[0] += 1; SZ[(g, c)] = si_
        sz = sz2[si_]; T_sz = T_sz2[si_]
        def _xbc(e, win=win, tok=tok):
            for j in range(4):
                for k in range(8):
                    ins = e.matmul(banks[0][:, j * 128:(j + 1) * 128], lhsT=win[:, k, j * 128:(j + 1) * 128], rhs=hnT_t[:, k, tok],
                                   start=(k == 0), stop=(k == 7))
            return ins
        S.add("pe", _xbc, cost=2.4, r=T_wi + [hnT[c]], w=[pA])
        def _z(e, win=win, tok=tok):
            for k in range(8):
                ins = e.matmul(banks[1][:, 0:256], lhsT=hnT_t[:, k, tok], rhs=win[:, k, 512:768], start=(k == 0), stop=(k == 7))
            return ins
        S.add("pe", _z, cost=1.9, r=T_wi + [hnT[c]], w=[pZ])
        if samp:
            S.add("sp", lambda e, g=g: e.dma_start(out=cvs, in_=st_conv[lj, g]), w=[T_cvs], dma=1)
            S.add("pool", lambda e: e.tensor_copy(out=xps[:, :, :, 0:3], in_=cvs.rearrange("p (a b c) -> p a b c", a=4, b=16)),
                  r=[T_cvs], w=[T_xp])
            S.add("act", lambda e: e.activation(out=xps[:, :, :, 3:11], in_=banks[0][:].rearrange("p (a b c) -> p a b c", a=4, b=16),
                                                func=AF.Identity), r=[pA], w=[T_xp])
            S.add("pool", lambda e: e.tensor_copy(out=cvs.rearrange("p (a b c) -> p a b c", a=4, b=16), in_=xps[:, :, :, 8:11]),
                  r=[T_xp], w=[T_cvs])
            S.add("sp", lambda e, g=g: e.dma_start(out=o_conv_s[lj, g], in_=cvs), r=[T_cvs], dma=1, out=True)
            src = lambda j, k: xps[:, j, :, k:k + 8]
            accv = lambda j: acc[:, j, :].rearrange("p (a b) -> p a b", a=16)
        else:
            if c == 0:
                S.add("pool", lambda e: e.memset(xp[:, :, 0:3], 0.0), w=[T_xp])
            S.add("act", lambda e: e.activation(out=xp[:, :, 3:131], in_=v3(banks[0][:], 4), func=AF.Identity), r=[pA], w=[T_xp])
            src = lambda j, k: xp[:, j, k:k + 128]
            accv = lambda j: acc[:, j, :]

        for k in range(4):
            def _conv(e, src=src, accv=accv, g4=g4, k=k):
                for j in range(4):
                    ti = g4 + j
                    if k == 0:
                        ins = e.tensor_scalar(out=accv(j), in0=src(j, 0), scalar1=cw[:, ti, 0:1], scalar2=convb[:, ti:ti + 1], op0=ALU.mult, op1=ALU.add)
                    else:
                        ins = e.scalar_tensor_tensor(out=accv(j), in0=src(j, k), scalar=cw[:, ti, k:k + 1], in1=accv(j), op0=ALU.mult, op1=ALU.add)
                return ins
            S.add("dve", _conv, cost=1.15, r=[T_xp, T_lvec] + ([T_acc] if k else []), w=[T_acc])
        if not samp:
            if c == 15:
                S.add("pool", lambda e: e.tensor_copy(out=v3(cvp, 4), in_=xp[:, :, 128:131]), r=[T_xp], w=[T_cvp])
                S.add("sp", lambda e, g=g: e.dma_start(out=o_conv_p[lj, g], in_=cvp), r=[T_cvp], dma=1, out=True)
            else:
                S.add("pool", lambda e: e.tensor_copy(out=xp[:, :, 0:3], in_=xp[:, :, 128:131]), r=[T_xp], w=[T_xp])
        S.add("act", lambda e: e.activation(out=xa, in_=acc, func=AF.Silu), r=[T_acc], w=[T_xa])
        S.add("act", lambda e: e.activation(out=sz, in_=banks[1][:, 0:256], func=AF.Silu), r=[pZ], w=[T_sz], cost=0.85)

    def emitM(g, c):
        g4 = g * 4
        win, wout, T_wi, T_wo = W[g]
        samp = (c == SC)
        tok = slice(c * 128, (c + 1) * 128)
        def _trx(e):
            e.transpose(out=pC[:, 0:128], in_=xa[:, 0, :], identity=identb_t[:])
            e.transpose(out=pC[:, 128:256], in_=xa[:, 1, :], identity=identb_t[:])
            return e.transpose(out=pC[:, 256:384], in_=xa[:, 2, :], identity=identb_t[:])
        S.add("pe", _trx, r=[T_xa, T_identb], w=[pTx])
        S.add("act", lambda e: e.activation(out=xs, in_=pC[:, 0:256], func=AF.Identity), r=[pTx], w=[T_xs])
        S.add("act", lambda e: e.activation(out=Bsb, in_=pC[:, 256:384], func=AF.Identity), r=[pTx], w=[T_Bsb])
        S.add("dve", lambda e, c=c, g4=g4: e.tensor_tensor(out=v3(xdt, 4), in0=v3(pC[:, 0:256], 4),
                                                          in1=dt_a[:, c, g4:g4 + 4].unsqueeze(2).to_broadcast([128, 4, 64]), op=ALU.mult),
              r=[pTx, T_pp[0]], w=[T_xdt])
        S.add("dve", lambda e, c=c, g4=g4: e.tensor_tensor(out=v3(xdtd, 4), in0=v3(pC[:, 0:256], 4),
                                                          in1=dtd_a[:, c, g4:g4 + 4].unsqueeze(2).to_broadcast([128, 4, 64]), op=ALU.mult),
              r=[pTx, T_pp[1]], w=[T_xdtd])
        S.add("pe", lambda e: e.matmul(banks[1][:, 256:384], lhsT=xa[:, 2, :], rhs=xa[:, 3, :], start=True, stop=True),
              r=[T_xa], w=[pCB])
        S.add("dve", lambda e, c=c, g4=g4: e.tensor_tensor(out=Dexp, in0=identf.unsqueeze(1).to_broadcast([128, 4, 128]),
                                                          in1=nac_a[:, c, g4:g4 + 4].unsqueeze(2).to_broadcast([128, 4, 128]), op=ALU.mult),
              r=[T_cbase, T_pp[2]], w=[T_Dexp], cost=0.7)
        ncol = slice(512, 1024) if samp else slice(0, 512)

        def _seg(e, ncol=ncol):
            e.matmul(banks[3][:], lhsT=negonesf, rhs=Dexp.rearrange("p a b -> p (a b)"), start=True, stop=False)
            return e.matmul(banks[3][:], lhsT=identb_t[:], rhs=cneg[:, ncol], start=False, stop=True)
        S.add("pe", _seg, cost=2.0, r=[T_Dexp, T_ones, T_identb, T_cneg], w=[pD])

        def _E(e, c=c, g4=g4):
            for h in range(4):
                ins = e.activation(out=Eb[:, h, :], in_=banks[3][:, h * 128:(h + 1) * 128], func=AF.Exp, bias=nac_a[:, c, g4 + h:g4 + h + 1])
            return ins
        S.add("act", _E, cost=2.2, r=[pD, T_pp[2]], w=[T_E])
        S.add("dve", lambda e: e.tensor_tensor(out=MT, in0=Eb, in1=banks[1][:, 256:384].unsqueeze(1).to_broadcast([128, 4, 128]), op=ALU.mult),
              r=[T_E, pCB], w=[T_MT], cost=0.6)
        def _yi(e):
            for h in range(4):
                ins = e.matmul(banks[4][:, h * 64:(h + 1) * 64], lhsT=MT[:, h, :], rhs=xdt[:, h * 64:(h + 1) * 64], start=True, stop=True)
            return ins
        S.add("pe", _yi, r=[T_MT, T_xdt], w=[pYa])
        if not samp:
            if c == 0:
                S.add("pool", lambda e: e.memset(hT, 0.0), w=[T_hT])
                S.add("pool", lambda e: e.memset(hTb, 0.0), w=[T_hTb])
            S.add("pe", lambda e: e.matmul(banks[4][:, 256:512], lhsT=xa[:, 3, :], rhs=hTb, start=True, stop=True), r=[T_xa, T_hTb], w=[pYb])
            S.add("pe", lambda e: e.matmul(banks[5][:, 0:256], lhsT=Bsb, rhs=xdtd, start=True, stop=True), r=[T_Bsb, T_xdtd], w=[pF])
            S.add("dve", lambda e, c=c, g4=g4: e.tensor_tensor(out=v3(hT, 4), in0=v3(hT, 4),
                                                              in1=dat_a[:, c, g4:g4 + 4].unsqueeze(2).to_broadcast([128, 4, 64]), op=ALU.mult),
                  r=[T_hT, T_pp[4], pYb], w=[T_hT])
            S.add("dve", lambda e: e.tensor_tensor(out=hT, in0=hT, in1=banks[5][:, 0:256], op=ALU.add), r=[T_hT, pF], w=[T_hT])
            if c == 15:
                S.add("sp", lambda e, g=g: e.dma_start(out=o_ssm_p[lj, g], in_=hT), r=[T_hT], dma=1, out=True)
            else:
                S.add("act", lambda e: e.activation(out=hTb, in_=hT, func=AF.Identity), r=[T_hT], w=[T_hTb])
        else:
            for pc in range(4):
                b = hcnt[0] % 2
                hcnt[0] += 1
                S.add("sp", lambda e, b=b, g=g, pc=pc: e.dma_start(out=h0f[b], in_=st_ssm[lj, g, :, pc * 4:(pc + 1) * 4, :]),
                      w=[T_h0f[b]], dma=1)
                S.add("pool", lambda e, b=b: e.tensor_copy(out=h0b[b], in_=h0f[b]), r=[T_h0f[b]], w=[T_h0b[b]])
                S.add("pool", lambda e, b=b, pc=pc: e.tensor_tensor(out=CTm[b], in0=xa[:, 3, :].unsqueeze(1).to_broadcast([128, 4, 128]),
                                                                   in1=ccol[:, :, 96 - 32 * pc: 224 - 32 * pc], op=ALU.mult),
                      r=[T_xa, T_ccol], w=[T_CTm[b]])
                S.add("pool", lambda e, b=b, pc=pc: e.tensor_tensor(out=Bm[b], in0=Bsb.unsqueeze(1).to_broadcast([128, 4, 128]),
                                                                   in1=rowmask[:, pc * 4:(pc + 1) * 4].unsqueeze(2).to_broadcast([128, 4, 128]), op=ALU.mult),
                      r=[T_Bsb, T_cbase], w=[T_Bm[b]])

                def _yis(e, b=b, pc=pc):
                    for s_ in range(4):
                        ins = e.matmul(banks[4][:, 256:512], lhsT=CTm[b][:, s_, :], rhs=h0b[b][:, s_, :],
                                       start=(pc == 0 and s_ == 0), stop=(pc == 3 and s_ == 3))
                    return ins
                S.add("pe", _yis, cost=0.8, r=[T_CTm[b], T_h0b[b]] + ([pYa] if pc == 0 else []), w=[pYb])
                for hp in range(2):
                    def _sts(e, b=b, hp=hp):
                        for s_ in range(2):
                            ins = e.matmul(banks[5][:, s_ * 256:(s_ + 1) * 256], lhsT=Bm[b][:, hp * 2 + s_, :], rhs=xdtd, start=True, stop=True)
                        return ins
                    S.add("pe", _sts, r=[T_Bm[b], T_xdtd], w=[pF])
                    seq0 = pc * 4 + hp * 2
                    hv = h0f[b][:, hp * 2:hp * 2 + 2, :].rearrange("p s (h q) -> p s h q", h=4)
                    S.add("dve", lambda e, hv=hv, seq0=seq0, g4=g4: e.tensor_tensor(
                        out=hv, in0=hv, in1=dats[:, seq0:seq0 + 2, g4:g4 + 4].unsqueeze(3).to_broadcast([128, 2, 4, 64]), op=ALU.mult),
                        r=[T_h0f[b], T_dats, T_h0b[b]], w=[T_h0f[b]])
                    hv2 = h0f[b][:, hp * 2:hp * 2 + 2, :]
                    S.add("dve", lambda e, hv2=hv2: e.tensor_tensor(out=hv2, in0=hv2, in1=banks[5][:].rearrange("p (s q) -> p s q", s=2), op=ALU.add),
                          r=[T_h0f[b], pF], w=[T_h0f[b]])
                S.add("sp", lambda e, b=b, g=g, pc=pc: e.dma_start(out=o_ssm_s[lj, g, :, pc * 4:(pc + 1) * 4, :], in_=h0f[b]),
                      r=[T_h0f[b]], dma=1, out=True)

    def emitT(g, c):
        g4 = g * 4
        win, wout, T_wi, T_wo = W[g]
        samp = (c == SC)
        tok = slice(c * 128, (c + 1) * 128)
        si_ = SZ[(g, c)]
        sz = sz2[si_]; T_sz = T_sz2[si_]; junk = sz; T_junk = T_sz
        S.add("dve", lambda e, c=c, g4=g4: e.tensor_tensor(out=v3(y1, 4), in0=v3(banks[4][:, 256:512], 4),
                                                          in1=eac_a[:, c, g4:g4 + 4].unsqueeze(2).to_broadcast([128, 4, 64]), op=ALU.mult),
              r=[pYb, T_pp[3]], w=[T_y1])
        S.add("dve", lambda e: e.tensor_tensor(out=y1, in0=y1, in1=banks[4][:, 0:256], op=ALU.add), r=[T_y1, pYa], w=[T_y1])

        def _dsk(e, g4=g4):
            for h in range(4):
                hs = slice(h * 64, (h + 1) * 64)
                ins = e.scalar_tensor_tensor(out=y1[:, hs], in0=xs[:, hs], scalar=Dcol[:, g4 + h:g4 + h + 1], in1=y1[:, hs], op0=ALU.mult, op1=ALU.add)
            return ins
        S.add("dve", _dsk, cost=0.8, r=[T_xs, T_y1, T_lvec], w=[T_y1])
        S.add("dve", lambda e: e.tensor_tensor(out=y1, in0=y1, in1=sz, op=ALU.mult), r=[T_y1, T_sz], w=[T_y1])
        S.add("act", lambda e: e.activation(out=junk, in_=y1, func=AF.Square, accum_out=sq[:, 0:1]), r=[T_y1], w=[T_junk, T_sq])
        S.add("pool", lambda e: e.tensor_scalar(out=sq[:, 1:2], in0=sq[:, 0:1], scalar1=1.0 / 256, scalar2=EPS, op0=ALU.mult, op1=ALU.add),
              r=[T_sq], w=[T_sq])
        S.add("pool", lambda e: e.tensor_tensor(out=sq[:, 2:3], in0=sq[:, 1:2], in1=small[:, 10:11], op=ALU.pow), r=[T_sq, T_eps], w=[T_sq])
        S.add("dve", lambda e, g=g: e.scalar_tensor_tensor(out=gn, in0=y1, scalar=sq[:, 2:3], in1=big8[:, g * 256:(g + 1) * 256],
                                                          op0=ALU.mult, op1=ALU.mult), r=[T_y1, T_sq, T_big8], w=[T_gn], cost=0.55)

        def _trg(e):
            e.transpose(out=pC[:, 512:640], in_=gn[:, 0:128], identity=identb_t[:])
            return e.transpose(out=pC[:, 640:768], in_=gn[:, 128:256], identity=identb_t[:])
        S.add("pe", _trg, r=[T_gn, T_identb], w=[pTg])
        S.add("act", lambda e: e.activation(out=gT, in_=v3(pC[:, 512:768], 2), func=AF.Identity), r=[pTg], w=[T_gT])

        def _out(e, wout=wout):
            for half in range(2):
                for j in range(2):
                    ins = e.matmul(bankGH[:, half * 512:(half + 1) * 512], lhsT=gT[:, j, :], rhs=wout[:, j, half * 512:(half + 1) * 512],
                                   start=(j == 0), stop=(j == 1))
            return ins
        S.add("pe", _out, cost=1.2, r=[T_gT] + T_wo, w=[pO])
        if samp:
            S.add("dve", lambda e: e.tensor_tensor(out=bankGH[:], in0=bankGH[:], in1=gt1s[:], op=ALU.mult), r=[pO, T_gt1s], w=[pO])
            S.add("dve", lambda e, c=c: e.tensor_tensor(out=res_t[:, c, :], in0=res_t[:, c, :], in1=bankGH[:], op=ALU.add),
                  r=[res[c], pO], w=[res[c]])
            S.add("pool", lambda e, wout=wout: e.tensor_tensor(out=wout, in0=wout, in1=gt1p[:].unsqueeze(1).to_broadcast([128, 2, 1024]), op=ALU.mult),
                  r=T_wo + [T_gt1p], w=T_wo)
        else:
            S.add("dve", lambda e, c=c: e.tensor_tensor(out=res_t[:, c, :], in0=res_t[:, c, :], in1=bankGH[:], op=ALU.add),
                  r=[res[c], pO], w=[res[c]])

    units = [(g, c) for g in range(8) for c in [SC] + list(range(16))]
    load_w(0)
    emitA(*units[0])
    for i_, (g, c) in enumerate(units):
        if c == SC and g + 1 < 8:
            load_w(g + 1)
        emitM(g, c)
        if i_ + 1 < len(units):
            emitA(*units[i_ + 1])
        emitT(g, c)


def pool_layer(nc, S, L, lj):
    g_ = L
    (banks, bankGH, wk, wk_reset, wslot, wslot_t, next_slot, hnT, hnT_t, res, res_t, tmp, T_tmp) = (
        g_[k] for k in ("banks", "bankGH", "wk", "wk_reset", "wslot", "wslot_t", "next_slot", "hnT", "hnT_t", "res", "res_t", "tmp", "T_tmp"))
    pmat, T_big8, pscT, T_lvec = g_["pmat"], g_["T_big8"], g_["pscT"], g_["T_lvec"]
    gt1p, T_gt1p, gt1s, T_gt1s = g_["gt1p"], g_["T_gt1p"], g_["gt1s"], g_["T_gt1s"]
    st_pool, wpi, wpm, wpo, o_pool_p, o_pool_s = g_["st_pool"], g_["wpi"], g_["wpm"], g_["wpo"], g_["o_pool_p"], g_["o_pool_s"]

    def v3(ap, a):
        return ap.rearrange("p (a b) -> p a b", a=a)

    wk_reset()
    ub = [wk([128, 512], BF16) for _ in range(2)]; T_ub = [Tl(a) for a in ub]
    uf = wk([128, 512]); T_uf = Tl(uf)
    plT = wk([128, 4, 128], BF16); T_plT = Tl(plT)
    szT = wk([128, 4, 128]); T_szT = Tl(szT)
    m2T = wk([128, 4, 128], BF16); T_m2T = Tl(m2T)
    prevS = wk([128, 2, 512], BF16); T_prevS = Tl(prevS)

    pU = Tl(banks[0][:], bank=0); pP = Tl(banks[1][:], bank=1); pM = Tl(banks[2][:], bank=2); pZ = Tl(banks[3][:], bank=3); pO = Tl(bankGH[:], bank=6)
    S.add("sp", lambda e: e.dma_start(out=o_pool_s[lj, :, 0:7, :], in_=st_pool[lj].rearrange("(s j) c -> s j c", j=15)[:, 8:15, :]),
          dma=1, out=True)
    ucnt = [0]
    ckpt("Q0")
    for g in range(4):
        if g == 1:
            ckpt("Q4")
        si = next_slot()
        win = wslot_t[si][:].rearrange("p (k n) -> p k n", k=8)
        T_wi = wslot[si][0:4]
        S.add("pool", lambda e, win=win, g=g: e.dma_start(out=win, in_=wpi[lj, g].rearrange("p (k n) -> p k n", k=8)),
              w=T_wi, dma=1, nobar=True)
        si2 = next_slot()
        wmix = wslot_t[si2][:, 0:2048].rearrange("p (k n) -> p k n", k=4)
        wout = wslot_t[si2][:, 2048:6144].rearrange("p (k n) -> p k n", k=4)
        T_wm = wslot[si2][0:1]
        T_wo = wslot[si2][1:3]
        S.add("pool", lambda e, wmix=wmix, g=g: e.dma_start(out=wmix, in_=wpm[lj, g].rearrange("p (k n) -> p k n", k=4)),
              w=T_wm, dma=1, nobar=True)
        S.add("pool", lambda e, wout=wout, g=g: e.dma_start(out=wout, in_=wpo[lj, g].rearrange("p (k n) -> p k n", k=4)),
              w=T_wo, dma=1, nobar=True)
        S.add("pool", lambda e, g=g: e.dma_start(out=prevS[0:120, :, :],
                                                 in_=st_pool[lj].rearrange("(h r) c -> r h c", h=2)[:, :, g * 512:(g + 1) * 512]),
              w=[T_prevS], dma=1)
        mb = g * 7 * 128
        Pcur, P0hi, P0lo, Pprev, PcS, PpS0, PpS1 = (pmat[:, mb + i * 128: mb + (i + 1) * 128] for i in range(7))
        prev_u = None
        for c in [SC] + list(range(16)):
            samp = (c == SC)
            tok = slice(c * 128, (c + 1) * 128)
            bi = ucnt[0] % 2
            ucnt[0] += 1
            cur = ub[bi]; T_cur = T_ub[bi]

            def _u(e, win=win, tok=tok):
                for k in range(8):
                    ins = e.matmul(banks[0][:], lhsT=hnT_t[:, k, tok], rhs=win[:, k, 0:512], start=(k == 0), stop=(k == 7))
                return ins
            S.add("pe", _u, cost=2.0, r=T_wi + [hnT[c]], w=[pU])
            S.add("act", lambda e, cur=cur: e.activation(out=cur, in_=banks[0][:], func=AF.Identity), r=[pU], w=[T_cur])
            if samp or c == 15:
                S.add("dve", lambda e: e.tensor_copy(out=uf, in_=banks[0][:]), r=[pU], w=[T_uf])
                if samp:
                    def _us(e, g=g):
                        return [e.dma_start(out=o_pool_s[lj, s_, 7:15, g * 512:(g + 1) * 512], in_=uf[s_ * 8:(s_ + 1) * 8, :]) for s_ in range(16)]
                    S.add("sp", _us, r=[T_uf], dma=16, out=True)
                else:
                    S.add("sp", lambda e, g=g: e.dma_start(out=o_pool_p[lj, :, g * 512:(g + 1) * 512], in_=uf[113:128, :]), r=[T_uf], dma=1, out=True)
            if samp:
                def _pl(e, cur=cur, PcS=PcS, PpS0=PpS0, PpS1=PpS1):
                    for ct in range(4):
                        cs = slice(ct * 128, (ct + 1) * 128)
                        o = banks[1][:, cs]
                        e.matmul(o, lhsT=cur[:, cs], rhs=PcS, start=True, stop=False)
                        e.matmul(o, lhsT=prevS[0:120, 0, cs], rhs=PpS0[0:120, :], start=False, stop=False)
                        ins = e.matmul(o, lhsT=prevS[0:120, 1, cs], rhs=PpS1[0:120, :], start=False, stop=True)
                    return ins
                S.add("pe", _pl, cost=1.0, r=[T_cur, T_prevS, T_big8], w=[pP])
            elif c == 0:
                def _pl(e, cur=cur, P0hi=P0hi, P0lo=P0lo):
                    for ct in range(4):
                        cs = slice(ct * 128, (ct + 1) * 128)
                        o = banks[1][:, cs]
                        e.matmul(o, lhsT=cur[:, cs], rhs=P0hi, start=True, stop=False)
                        ins = e.matmul(o, lhsT=cur[:, cs], rhs=P0lo, start=False, stop=True)
                    return ins
                S.add("pe", _pl, cost=1.0, r=[T_cur, T_big8], w=[pP])
            else:
                pu, T_pu = prev_u

                def _pl(e, cur=cur, pu=pu, Pcur=Pcur, Pprev=Pprev):
                    for ct in range(4):
                        cs = slice(ct * 128, (ct + 1) * 128)
                        o = banks[1][:, cs]
                        e.matmul(o, lhsT=cur[:, cs], rhs=Pcur, start=True, stop=False)
                        ins = e.matmul(o, lhsT=pu[:, cs], rhs=Pprev, start=False, stop=True)
                    return ins
                S.add("pe", _pl, cost=1.0, r=[T_cur, T_pu, T_big8], w=[pP])
            prev_u = (cur, T_cur)
            S.add("act", lambda e: e.activation(out=plT, in_=v3(banks[1][:], 4), func=AF.Identity), r=[pP], w=[T_plT])

            def _mx(e, wmix=wmix):
                for dt_ in range(4):
                    for ct in range(4):
                        ins = e.matmul(banks[2][:, dt_ * 128:(dt_ + 1) * 128], lhsT=wmix[:, ct, dt_ * 128:(dt_ + 1) * 128], rhs=plT[:, ct, :],
                                       start=(ct == 0), stop=(ct == 3))
                return ins
            S.add("pe", _mx, cost=0.9, r=T_wm + [T_plT], w=[pM])

            def _zt(e, win=win, tok=tok):
                for dt_ in range(4):
                    for k in range(8):
                        ins = e.matmul(banks[3][:, dt_ * 128:(dt_ + 1) * 128], lhsT=win[:, k, 512 + dt_ * 128: 512 + (dt_ + 1) * 128], rhs=hnT_t[:, k, tok],
                                       start=(k == 0), stop=(k == 7))
                return ins
            S.add("pe", _zt, cost=1.9, r=T_wi + [hnT[c]], w=[pZ])
            S.add("act", lambda e: e.activation(out=szT, in_=v3(banks[3][:], 4), func=AF.Silu), r=[pZ], w=[T_szT])

            def _m2(e, g=g):
                for dt_ in range(4):
                    ins = e.scalar_tensor_tensor(out=m2T[:, dt_, :], in0=banks[2][:, dt_ * 128:(dt_ + 1) * 128], scalar=pscT[:, g * 4 + dt_: g * 4 + dt_ + 1],
                                                 in1=szT[:, dt_, :], op0=ALU.mult, op1=ALU.mult)
                return ins
            S.add("dve", _m2, cost=1.1, r=[pM, T_szT, T_lvec], w=[T_m2T])

            def _out(e, wout=wout):
                for half in range(2):
                    for dt_ in range(4):
                        ins = e.matmul(bankGH[:, half * 512:(half + 1) * 512], lhsT=m2T[:, dt_, :], rhs=wout[:, dt_, half * 512:(half + 1) * 512],
                                       start=(dt_ == 0), stop=(dt_ == 3))
                return ins
            S.add("pe", _out, cost=1.2, r=[T_m2T] + T_wo, w=[pO])
            if samp:
                S.add("dve", lambda e: e.tensor_tensor(out=bankGH[:], in0=bankGH[:], in1=gt1s[:], op=ALU.mult), r=[pO, T_gt1s], w=[pO])
                S.add("dve", lambda e, c=c: e.tensor_tensor(out=res_t[:, c, :], in0=res_t[:, c, :], in1=bankGH[:], op=ALU.add),
                      r=[res[c], pO], w=[res[c]], cost=1.2)
                S.add("pool", lambda e, wout=wout: e.tensor_tensor(out=wout, in0=wout, in1=gt1p[:].unsqueeze(1).to_broadcast([128, 4, 1024]), op=ALU.mult),
                      r=T_wo + [T_gt1p], w=T_wo)
            else:
                S.add("dve", lambda e, c=c: e.tensor_tensor(out=res_t[:, c, :], in0=res_t[:, c, :], in1=bankGH[:], op=ALU.add),
                      r=[res[c], pO], w=[res[c]], cost=1.2)
            if g == 0 and c == SC:
                ckpt("Q1")
            if g == 0 and c == 0:
                ckpt("Q2")
            if g == 0 and c == 1:
                ckpt("Q3")


def _prep_shared(inp):
    f = lambda a: np.ascontiguousarray(a, dtype=np.float32)
    sh = {}
    sh["ada_w"] = f(inp["ada_w"])
    sh["ada_b"] = f(inp["ada_b"])
    sh["normg_bc"] = f(np.broadcast_to(inp["norm_g"][:, None, :], (4, 128, 1024)))
    sh["fng_bc"] = f(np.broadcast_to(inp["final_norm_g"][None, :], (128, 1024)))
    w_in = inp["ssd_w_in"]
    wsi = np.empty((2, 8, 128, 8, 768), np.float32)
    for g in range(8):
        cols = np.concatenate([2048 + 256 * g + np.arange(256), 4096 + 128 * g + np.arange(128),
                               5120 + 128 * g + np.arange(128), 256 * g + np.arange(256)])
        blk = w_in[:, :, cols].reshape(2, 8, 128, 768)
        wsi[:, g] = blk.transpose(0, 2, 1, 3)
    sh["w_ssd_in"] = wsi.reshape(2, 8, 128, 8 * 768)
    sh["w_ssd_dt"] = f(w_in[:, :, 6144:6176].reshape(2, 8, 128, 32).transpose(0, 2, 1, 3)).reshape(2, 128, 256)
    wo = inp["ssd_w_out"].reshape(2, 8, 2, 128, 1024)
    sh["w_ssd_out"] = f(wo.transpose(0, 1, 3, 2, 4)).reshape(2, 8, 128, 2048)
    cwv = inp["ssd_conv_w"]
    tiles = []
    for g in range(8):
        tiles += [2 * g, 2 * g + 1, 16 + g, 24 + g]
    tiles = np.array(tiles)
    cw = cwv.reshape(2, 4, 32, 128)[:, :, tiles, :]
    sh["convw"] = f(cw.transpose(0, 3, 2, 1)).reshape(2, 128, 128)
    cb = inp["ssd_conv_b"].reshape(2, 32, 128)[:, tiles, :]
    sh["convb"] = f(cb.transpose(0, 2, 1))
    v = np.concatenate([inp["ssd_dt_bias"], inp["ssd_a_log"], inp["ssd_d"]], axis=1)
    sh["vec32"] = f(np.broadcast_to(v[:, None, :], (2, 128, 96)))
    sh["ssd_normg_bc"] = f(np.broadcast_to(inp["ssd_norm_g"][:, None, :], (2, 128, 2048)))
    pw = inp["pool_w_in"]
    wpi = np.empty((2, 4, 128, 8, 1024), np.float32)
    for g in range(4):
        cols = np.concatenate([512 * g + np.arange(512), 2048 + 512 * g + np.arange(512)])
        wpi[:, g] = pw[:, :, cols].reshape(2, 8, 128, 1024).transpose(0, 2, 1, 3)
    sh["w_pool_in"] = wpi.reshape(2, 4, 128, 8192)
    sh["w_pool_mix"] = f(inp["pool_w_group"].reshape(2, 4, 4, 128, 512).transpose(0, 1, 3, 2, 4)).reshape(2, 4, 128, 2048)
    sh["w_pool_out"] = f(inp["pool_w_out"].reshape(2, 4, 4, 128, 1024).transpose(0, 1, 3, 2, 4)).reshape(2, 4, 128, 4096)
    sh["pscaleT"] = f(inp["pool_scale"].reshape(2, 16, 128).transpose(0, 2, 1))
    cb_, cn_, cc_ = _consts()
    sh["cbase"], sh["cneg"], sh["ccol"] = cb_, cn_, cc_
    sh["pmats"] = _pool_mats()
    return sh


def _prep_core(inp, i):
    f = lambda a: np.ascontiguousarray(a, dtype=np.float32)
    d = {}
    xs = inp["x_sample"][16 * i:16 * i + 16].reshape(128, D)
    d["xin"] = f(np.concatenate([inp["x_prompt"][i], xs], axis=0))
    c = np.concatenate([inp["c_prompt"][i:i + 1], inp["c_sample"][16 * i:16 * i + 16]], axis=0)
    d["cT"] = f(c.reshape(17, 8, 128).transpose(2, 1, 0)).reshape(128, 136)
    ss = inp["state_ssm"][:, 16 * i:16 * i + 16]
    ss = ss.reshape(2, 16, 8, 256, 128).transpose(0, 2, 4, 1, 3)
    d["st_ssm"] = f(ss)
    sc = inp["state_conv"][:, 16 * i:16 * i + 16]
    sc = sc.reshape(2, 16, 3, 32, 128)
    tiles = []
    for g in range(8):
        tiles += [2 * g, 2 * g + 1, 16 + g, 24 + g]
    sc = sc[:, :, :, np.array(tiles), :].reshape(2, 16, 3, 8, 4, 128)
    d["st_conv"] = f(sc.transpose(0, 3, 5, 4, 1, 2)).reshape(2, 8, 128, 192)
    d["st_pool"] = f(inp["state_pool"][:, 16 * i:16 * i + 16].reshape(2, 240, 2048))
    return d


_TILES = None


def _conv_tiles():
    t = []
    for g in range(8):
        t += [2 * g, 2 * g + 1, 16 + g, 24 + g]
    return np.array(t)


def _assemble(results, NL=4):
    nssd, npool = (NL + 1) // 2, NL // 2
    y_p = np.stack([r["yout"][:2048] for r in results]).astype(np.float32)
    y_s = np.concatenate([r["yout"][2048:].reshape(16, 8, D) for r in results]).astype(np.float32)
    sp = np.stack([r["o_ssm_p"] for r in results], axis=1)
    ssm_p = sp.reshape(2, 8, 8, 128, 4, 64).transpose(0, 1, 2, 4, 5, 3).reshape(2, 8, 32, 64, 128)
    ss = np.stack([r["o_ssm_s"] for r in results], axis=1)
    ssm_s = ss.reshape(2, 8, 8, 128, 16, 4, 64).transpose(0, 1, 4, 2, 5, 6, 3).reshape(2, 128, 32, 64, 128)
    tiles = _conv_tiles()
    inv = np.argsort(tiles)
    cp = np.stack([r["o_conv_p"] for r in results], axis=1)
    cp = cp.reshape(2, 8, 8, 128, 4, 3).transpose(0, 1, 5, 2, 4, 3).reshape(2, 8, 3, 32, 128)[:, :, :, inv, :]
    conv_p = cp.reshape(2, 8, 3, 4096)
    cs = np.stack([r["o_conv_s"] for r in results], axis=1)
    cs = cs.reshape(2, 8, 8, 128, 4, 16, 3).transpose(0, 1, 5, 6, 2, 4, 3).reshape(2, 128, 3, 32, 128)[:, :, :, inv, :]
    conv_s = cs.reshape(2, 128, 3, 4096)
    pool_p = np.stack([r["o_pool_p"] for r in results], axis=1)
    pool_s = np.concatenate([r["o_pool_s"] for r in results], axis=1)
    c = lambda a: np.ascontiguousarray(a, dtype=np.float32)
    return (c(y_p), c(y_s), c(ssm_p[:nssd]), c(conv_p[:nssd]), c(pool_p[:npool]), c(ssm_s[:nssd]), c(conv_s[:nssd]), c(pool_s[:npool]))


_NC_CACHE = {}


def kernel(_NL=4, **inputs):
    inputs = {k: np.asarray(v) for k, v in inputs.items()}
    if _NL not in _NC_CACHE:
        _NC_CACHE[_NL] = build_nc(_NL)
    nc = _NC_CACHE[_NL]
    sh = _prep_shared(inputs)
    in_maps = []
    for i in range(NCORES):
        d = dict(sh)
        d.update(_prep_core(inputs, i))
        in_maps.append(d)
    res = run_bass_kernel_spmd(nc, in_maps, core_ids=list(range(NCORES)))
    return _assemble(res.results, _NL)
```

```python
import numpy as np
import concourse.bass as bass
import concourse.mybir as mybir
from concourse.bass_utils import run_bass_kernel_spmd

F32 = mybir.dt.float32
BF16 = mybir.dt.bfloat16
AF = mybir.ActivationFunctionType
ALU = mybir.AluOpType

NCORES = 8
D = 1024
NCH = 17
SC = 16
NTOK = NCH * 128
EPS = 1e-6
POOL_W = (2, 4, 8, 16)
NEG = -30000.0
SAME_SYNC = True
SEG = 4000
ND = 8
STOP = None


class StopBuild(Exception):
    pass


def ckpt(name):
    if STOP == name:
        raise StopBuild()


class Tl:
    __slots__ = ("ap", "key", "lw", "rd", "bank")

    def __init__(self, ap, key=None, bank=None):
        self.ap = ap
        self.key = key if key is not None else self
        self.lw = None
        self.rd = []
        self.bank = bank


class Op:
    __slots__ = ("eng", "fn", "deps", "dma", "idx", "cost")


LIST_SCHED = True
LAT = 1.0
RANK_PRIO = True
EXCL_ALL = True
_DEF_COST = {"pe": 0.6, "act": 0.35, "dve": 0.35, "pool": 0.5, "sp": 0.1}


class Sched:
    ENG = ("pe", "act", "dve", "pool", "sp")

    def __init__(self):
        self.ops = []
        self.bar = {}
        self.lastop = {}
        self.dma_since = []
        self.out_dmas = []
        self.bank_last = {}
        self.cur_bar = None
        self.bar_fn = None

    def add(self, eng, fn, r=(), w=(), dma=0, nobar=False, out=False, cost=None):
        o = Op()
        o.cost = cost if cost is not None else (2.5 if dma else _DEF_COST[eng])
        o.idx = len(self.ops)
        o.eng = eng
        o.fn = fn
        o.dma = dma
        deps = set()
        for t in r:
            k = t.key
            if k.lw is not None:
                deps.add(k.lw)
        for t in w:
            k = t.key
            if k.lw is not None:
                deps.add(k.lw)
            deps.update(k.rd)
        for t in r:
            t.key.rd.append(o.idx)
        for t in w:
            k = t.key
            k.lw = o.idx
            k.rd = []
        for t in list(r) + list(w):
            if t.bank is not None:
                bl = self.bank_last.setdefault(t.bank, {})
                for e2, i2 in bl.items():
                    if e2 != eng and (EXCL_ALL or (e2 != "pe" and eng != "pe")):
                        deps.add(i2)
        for t in list(r) + list(w):
            if t.bank is not None:
                self.bank_last[t.bank][eng] = o.idx
        if not nobar and self.cur_bar is not None:
            deps.add(self.cur_bar)
        deps.discard(o.idx)
        o.deps = deps
        self.ops.append(o)
        self.lastop[eng] = o.idx
        if dma:
            self.dma_since.append(o.idx)
            if out:
                self.out_dmas.append(o.idx)
        return o.idx

    def barrier(self):
        deps = set(self.lastop.values()) | set(self.dma_since)
        self.dma_since = []
        idx = self.add("dve", self.bar_fn, nobar=True, cost=0.1)
        self.ops[idx].deps |= deps
        self.ops[idx].deps.discard(idx)
        self.cur_bar = idx

    def list_schedule(self):
        import heapq
        ops = self.ops
        n = len(ops)
        succ = [[] for _ in range(n)]
        indeg = [0] * n
        for o in ops:
            for d in o.deps:
                succ[d].append(o.idx)
            indeg[o.idx] = len(o.deps)
        rank = [0.0] * n
        for i in range(n - 1, -1, -1):
            r_ = 0.0
            for s_ in succ[i]:
                lat = LAT if (ops[s_].eng != ops[i].eng or ops[i].dma) else 0.05
                if rank[s_] + lat > r_:
                    r_ = rank[s_] + lat
            rank[i] = r_ + ops[i].cost
        ready_t = [0.0] * n
        fin = [0.0] * n
        free = {e: 0.0 for e in self.ENG}
        pend = {e: [] for e in self.ENG}
        avail = {e: [] for e in self.ENG}
        for o in ops:
            if indeg[o.idx] == 0:
                heapq.heappush(pend[o.eng], (0.0, o.idx))
        order = []
        while len(order) < n:
            best = None
            for e in self.ENG:
                while pend[e] and pend[e][0][0] <= free[e]:
                    rt, idx = heapq.heappop(pend[e])
                    heapq.heappush(avail[e], ((-rank[idx]) if RANK_PRIO else rt, idx))
                if avail[e]:
                    t_ = free[e]
                elif pend[e]:
                    t_ = pend[e][0][0]
                else:
                    continue
                if best is None or t_ < best[0]:
                    best = (t_, e)
            t_, e = best
            if not avail[e]:
                free[e] = t_
                continue
            _, idx = heapq.heappop(avail[e])
            o = ops[idx]
            st = max(free[e], ready_t[idx])
            issue = 0.06 if o.dma else o.cost
            free[e] = st + issue
            fin[idx] = st + o.cost
            order.append(idx)
            for s_ in succ[idx]:
                lat = LAT if (ops[s_].eng != e or o.dma) else 0.05
                ready_t[s_] = max(ready_t[s_], fin[idx] + lat)
                indeg[s_] -= 1
                if indeg[s_] == 0:
                    heapq.heappush(pend[ops[s_].eng], (ready_t[s_], s_))
        return order

    def emit(self, nc):
        ops = self.ops
        order = self.list_schedule() if LIST_SCHED else list(range(len(ops)))
        ops_sorted = [ops[i] for i in order]
        needs = [False] * len(ops)
        for o in ops:
            for d in o.deps:
                po = ops[d]
                if po.dma:
                    continue
                if po.eng != o.eng or SAME_SYNC:
                    needs[d] = True
        cnt = {e: 0 for e in self.ENG}
        val = {}
        for o in ops_sorted:
            if o.dma:
                continue
            if needs[o.idx]:
                cnt[o.eng] += 1
                val[o.idx] = cnt[o.eng]
        esems = {e: [nc.alloc_semaphore(f"c_{e}_{i}") for i in range(cnt[e] // SEG + 1)] for e in self.ENG}
        dpool = {e: [nc.alloc_semaphore(f"d_{e}_{i}") for i in range(ND)] for e in ("sp", "pool", "act")}
        dtot = {e: [0] * ND for e in dpool}
        dcount = {e: 0 for e in dpool}
        dinfo = {}
        for o in ops_sorted:
            if o.dma:
                j = dcount[o.eng] % ND
                dcount[o.eng] += 1
                prev = dtot[o.eng][j]
                dtot[o.eng][j] = prev + 16 * o.dma
                dinfo[o.idx] = (dpool[o.eng][j], prev, dtot[o.eng][j])
        by_eng = {e: [o for o in ops_sorted if o.eng == e] for e in self.ENG}

        def run(eng_name, e):
            waited_c = {x: 0 for x in self.ENG}
            waited_d = {}
            for o in by_eng[eng_name]:
                for d in sorted(o.deps):
                    po = ops[d]
                    if po.dma:
                        sem, _, tot = dinfo[d]
                        if waited_d.get(sem.num if hasattr(sem, "num") else id(sem), 0) < tot:
                            e.wait_ge(sem, tot)
                            waited_d[sem.num if hasattr(sem, "num") else id(sem)] = tot
                    elif po.eng != eng_name or SAME_SYNC:
                        n = val[d]
                        if waited_c[po.eng] < n:
                            e.wait_ge(esems[po.eng][(n - 1) // SEG], (n - 1) % SEG + 1)
                            waited_c[po.eng] = n
                if o.dma:
                    sem, prev, tot = dinfo[o.idx]
                    key = sem.num if hasattr(sem, "num") else id(sem)
                    if prev > 0 and waited_d.get(key, 0) < prev:
                        e.wait_ge(sem, prev)
                        waited_d[key] = prev
                    ins = o.fn(e)
                    if not isinstance(ins, (list, tuple)):
                        ins = [ins]
                    assert len(ins) == o.dma, (len(ins), o.dma)
                    for i_ in ins:
                        i_.then_inc(sem, 16)
                else:
                    ins = o.fn(e)
                    if isinstance(ins, (list, tuple)):
                        ins = ins[-1]
                    if needs[o.idx] and ins is not None:
                        n = val[o.idx]
                        ins.then_inc(esems[eng_name][(n - 1) // SEG], 1)

        with nc.Block() as block:
            @block.tensor
            def _(e):
                run("pe", e)

            @block.scalar
            def _(e):
                run("act", e)

            @block.vector
            def _(e):
                run("dve", e)

            @block.gpsimd
            def _(e):
                run("pool", e)

            @block.sync
            def _(e):
                run("sp", e)


def _consts():
    k = np.arange(128)
    seq = k // 8
    ident = np.eye(128, dtype=np.float32)
    Lp = (k[:, None] <= k[None, :]).astype(np.float32)
    same = (seq[:, None] == seq[None, :])
    Ls = (same & (k[:, None] <= k[None, :])).astype(np.float32)
    Ss = same.astype(np.float32)
    NEGp = np.where(k[:, None] <= k[None, :], 0.0, NEG).astype(np.float32)
    NEGs = np.where(same & (k[:, None] <= k[None, :]), 0.0, NEG).astype(np.float32)
    u = np.arange(224)
    colmask = ((u[None, :] >= 96 + 8 * np.arange(4)[:, None]) & (u[None, :] < 104 + 8 * np.arange(4)[:, None])).astype(np.float32)
    colmask = np.broadcast_to(colmask.reshape(1, 4 * 224), (128, 4 * 224))
    rowmask = (seq[:, None] == np.arange(16)[None, :]).astype(np.float32)
    base = np.concatenate([ident, Lp, Ls, Ss, rowmask], axis=1)
    negs = np.concatenate([np.tile(NEGp, (1, 4)), np.tile(NEGs, (1, 4))], axis=1)
    return base.astype(np.float32), negs.astype(np.float32), np.ascontiguousarray(colmask, dtype=np.float32)


def _pool_mats():
    import ml_dtypes
    s = np.arange(128)[:, None]
    t = np.arange(128)[None, :]
    seq_s, ts = s // 8, s % 8
    seq_t, tt = t // 8, t % 8
    mats = []
    for w in POOL_W:
        band = ((t - s >= 0) & (t - s < w)).astype(np.float64)
        pcur = band / w - np.eye(128)
        cnt0 = np.minimum(w, np.arange(128) + 1)[None, :]
        p0 = band / cnt0 - np.eye(128)
        p0hi = p0.astype(np.float32).astype(ml_dtypes.bfloat16).astype(np.float64)
        p0lo = p0 - p0hi
        pprev = (s > 128 + t - w).astype(np.float64) / w
        pcs = ((seq_s == seq_t) & (tt - ts >= 0) & (tt - ts < w)).astype(np.float64) / w - np.eye(128)
        pps = []
        for hf in range(2):
            r = np.arange(128)[:, None]
            sl, j = r // 15, r % 15
            m = ((r < 120) & (seq_t == 8 * hf + sl) & (j > 15 + tt - w)).astype(np.float64) / w
            pps.append(m)
        mats += [pcur, p0hi, p0lo, pprev, pcs, pps[0], pps[1]]
    return np.concatenate(mats, axis=1).astype(np.float32)


def build_nc(NL=4):
    nc = bass.Bass("TRN2", target_bir_lowering=False)
    S = Sched()
    NSSD = (NL + 1) // 2
    NPOOL = NL // 2

    def din(name, shape):
        return nc.dram_tensor(name, list(shape), F32, kind="ExternalInput").ap()

    def dout(name, shape):
        return nc.dram_tensor(name, list(shape), F32, kind="ExternalOutput").ap()

    xin = din("xin", [NTOK, D])
    cTd = din("cT", [128, 8 * 17])
    st_ssm = din("st_ssm", [2, 8, 128, 16, 256])
    st_conv = din("st_conv", [2, 8, 128, 4 * 16 * 3])
    st_pool = din("st_pool", [2, 240, 2048])
    ada_w = din("ada_w", [4, 1024, 3072])
    ada_b = din("ada_b", [4, 3072])
    normg_d = din("normg_bc", [4, 128, 1024])
    fng_d = din("fng_bc", [128, 1024])
    wsi = din("w_ssd_in", [2, 8, 128, 8 * 768])
    wsd = din("w_ssd_dt", [2, 128, 8 * 32])
    wso = din("w_ssd_out", [2, 8, 128, 2 * 1024])
    convw_d = din("convw", [2, 128, 32 * 4])
    convb_d = din("convb", [2, 128, 32])
    vec32_d = din("vec32", [2, 128, 96])
    sng_d = din("ssd_normg_bc", [2, 128, 2048])
    wpi = din("w_pool_in", [2, 4, 128, 8 * 1024])
    wpm = din("w_pool_mix", [2, 4, 128, 4 * 512])
    wpo = din("w_pool_out", [2, 4, 128, 4 * 1024])
    psc_d = din("pscaleT", [2, 128, 16])
    cbase_d = din("cbase", [128, 528])
    cneg_d = din("cneg", [128, 1024])
    ccol_d = din("ccol", [128, 896])
    pmat_d = din("pmats", [128, 28 * 128])

    yout = dout("yout", [NTOK, D])
    o_ssm_p = dout("o_ssm_p", [2, 8, 128, 256])
    o_ssm_s = dout("o_ssm_s", [2, 8, 128, 16, 256])
    o_conv_p = dout("o_conv_p", [2, 8, 128, 12])
    o_conv_s = dout("o_conv_s", [2, 8, 128, 192])
    o_pool_p = dout("o_pool_p", [2, 15, 2048])
    o_pool_s = dout("o_pool_s", [2, 16, 15, 2048])

    def sb(name, shape, dt=F32):
        return nc.alloc_sbuf_tensor("s_" + name, list(shape), dt)

    res_t = sb("res", [128, NCH, D])
    res = [Tl(res_t[:, c, :]) for c in range(NCH)]
    hnT_t = sb("hnT", [128, 8, NTOK], BF16)
    hnT = [Tl(hnT_t[:, :, c * 128:(c + 1) * 128]) for c in range(NCH)]
    wslot_t = [sb(f"wslot{i}", [128, 8192], BF16) for i in range(2)]
    wslot = [[Tl(t[:, q * 2048:(q + 1) * 2048]) for q in range(4)] for t in wslot_t]
    cbase = sb("cbase", [128, 528])
    identf = cbase[:, 0:128]
    Lp_ap, Ls_ap, Ss_ap = cbase[:, 128:256], cbase[:, 256:384], cbase[:, 384:512]
    rowmask = cbase[:, 512:528]
    T_cbase = Tl(cbase[:])
    identb_t = sb("identb", [128, 128], BF16)
    T_identb = Tl(identb_t[:])
    cneg = sb("cneg", [128, 1024], BF16)
    T_cneg = Tl(cneg[:])
    ccol = sb("ccol", [128, 4, 224], BF16)
    T_ccol = Tl(ccol[:])
    ones_t = sb("ones", [128, 256])
    onesf, negonesf = ones_t[:, 0:128], ones_t[:, 128:256]
    T_ones = Tl(ones_t[:])
    cT = sb("cTs", [128, 8, 17])
    T_cT = Tl(cT[:])
    gt1p = sb("gt1p", [128, D]); T_gt1p = Tl(gt1p[:])
    gt1s = sb("gt1s", [128, D]); T_gt1s = Tl(gt1s[:])
    tmp = None; T_tmp = None
    big8 = sb("big8", [128, 2048])
    T_big8 = Tl(big8[:])
    pmat = big8[:].bitcast(BF16)
    lvec = sb("lvec", [128, 128 + 32 + 96 + 16])
    convw, convb, vec32, pscT = lvec[:, 0:128], lvec[:, 128:160], lvec[:, 160:256], lvec[:, 256:272]
    T_lvec = Tl(lvec[:])
    wdt_t = sb("wdt", [128, 8, 32], BF16); T_wdt = Tl(wdt_t[:])
    pp = sb("pp", [128, 5, NCH * 32])
    dt_a, dtd_a, nac_a, eac_a, dat_a = (pp[:, i, :].rearrange("p (c h) -> p c h", c=NCH) for i in range(5))
    T_pp = [Tl(pp[:, i, :]) for i in range(5)] + [None]
    dats = sb("dats", [128, 16, 32]); T_dats = Tl(dats[:])
    small = sb("small", [128, 16])
    T_ssq, T_rt, T_rstd = Tl(small[:, 0:1]), Tl(small[:, 1:2]), Tl(small[:, 2:3])
    Acol = sb("Acol", [128, 32]); T_Acol = Tl(Acol[:])

    WK = sb("wk", [128, 35 * 1024 // 4])
    wk_off = [0]

    def wk_reset():
        wk_off[0] = 0

    def wk(shape, dt=F32):
        n = int(np.prod(shape[1:]))
        nb = n * (4 if dt == F32 else 2)
        nb4 = (nb + 31) // 32 * 8
        a = WK[:, wk_off[0]:wk_off[0] + nb4]
        wk_off[0] += nb4
        assert wk_off[0] <= 35 * 1024 // 4, wk_off[0]
        if dt == BF16:
            a = a.bitcast(BF16)[:, 0:n]
        else:
            a = a[:, 0:n]
        if len(shape) == 3:
            a = a.rearrange("p (a b) -> p a b", a=shape[1])
        elif len(shape) == 4:
            a = a.rearrange("p (a b c) -> p a b c", a=shape[1], b=shape[2])
        return a

    banks = [nc.alloc_psum_tensor(f"bank{i}", [128, 512], F32) for i in range(6)]
    bankGH = nc.alloc_psum_tensor("bankGH", [128, 1024], F32)

    TB = [Tl(b_[:], bank=i_) for i_, b_ in enumerate(banks)]

    def v3(ap, a):
        return ap.rearrange("p (a b) -> p a b", a=a)

    def bc_last(ap2, n):
        return ap2.unsqueeze(2).to_broadcast([128, ap2.shape[1], n])

    S.add("sp", lambda e: e.dma_start(out=cbase[:], in_=cbase_d), w=[T_cbase], dma=1)
    S.add("pool", lambda e: e.dma_start(out=cneg[:], in_=cneg_d), w=[T_cneg], dma=1)
    S.add("pool", lambda e: e.dma_start(out=ccol[:], in_=ccol_d.rearrange("p (a b) -> p a b", a=4)), w=[T_ccol], dma=1)
    S.add("sp", lambda e: e.dma_start(out=cT[:].rearrange("p a b -> p (a b)"), in_=cTd), w=[T_cT], dma=1)

    def _ones(e):
        e.memset(onesf, 1.0)
        return e.memset(negonesf, -1.0)
    S.add("dve", _ones, w=[T_ones])
    S.add("act", lambda e: e.activation(out=identb_t[:], in_=identf, func=AF.Identity), r=[T_cbase], w=[T_identb])
    for c in range(NCH):
        S.add("sp", lambda e, c=c: e.dma_start(out=res_t[:, c, :], in_=xin[c * 128:(c + 1) * 128, :]), w=[res[c]], dma=1)

    def rms_stats(src_tl, src_ap, junk_tl, junk_ap):
        S.add("act", lambda e: e.activation(out=junk_ap, in_=src_ap, func=AF.Square, accum_out=small[:, 0:1]),
              r=[src_tl], w=[junk_tl, T_ssq])
        S.add("pool", lambda e: e.tensor_scalar(out=small[:, 1:2], in0=small[:, 0:1], scalar1=1.0 / D, scalar2=EPS, op0=ALU.mult, op1=ALU.add),
              r=[T_ssq], w=[T_rt])
        S.add("pool", lambda e: e.tensor_tensor(out=small[:, 2:3], in0=small[:, 1:2], in1=small[:, 10:11], op=ALU.pow), r=[T_rt, T_eps], w=[T_rstd])

    T_eps = Tl(small[:, 8:11])
    S.bar_fn = lambda e: e.memset(small[:, 12:13], 0.0)

    def _eps(e):
        e.memset(small[:, 8:9], EPS)
        e.memset(small[:, 10:11], -0.5)
        return e.memset(small[:, 9:10], 1.0)
    S.add("dve", _eps, w=[T_eps])

    wctr = [0]

    def next_slot():
        i = wctr[0] % 2
        wctr[0] += 1
        return i

    try:
        _layers(locals())
    except StopBuild:
        S.barrier()
    _final(locals())
    return nc


def _layers(L_):
    globals().update({k: v for k, v in L_.items() if not k.startswith("__")})
    L_ = dict(L_)
    for li in range(NL):
        is_ssd = (li % 2 == 0)
        lj = li // 2
        wk_reset()
        G_p, SH_p, G_s, SH_s = (wk([128, D]) for _ in range(4))
        T_G_p, T_SH_p, T_G_s, T_SH_s = Tl(G_p), Tl(SH_p), Tl(G_s), Tl(SH_s)
        scTp = wk([128, 8, 128], BF16); T_scTp = Tl(scTp)
        scTs = wk([128, 8, 128], BF16); T_scTs = Tl(scTs)
        hnb = wk([128, D], BF16); T_hnb = Tl(hnb)
        junk, T_junk = hnb, T_hnb
        adab1 = wk([128, 512]); adab = [adab1, adab1]
        T_adab1 = Tl(adab1); T_adab = [T_adab1, T_adab1]
        scr2 = wk([128, NCH * 32]); scr_a = scr2.rearrange("p (c h) -> p c h", c=NCH)
        T_pp[5] = Tl(scr2)
        Dg = wk([128, 16, 32]); T_Dg = Tl(Dg)
        tmpA = wk([128, D]); T_tmpA = Tl(tmpA)
        normg = big8[:, 0:1024]
        S.add("sp", lambda e, li=li: e.dma_start(out=normg, in_=normg_d[li]), w=[T_big8], dma=1)
        S.add("act", lambda e: e.activation(out=scTp, in_=cT[:, :, 0:1].to_broadcast([128, 8, 128]), func=AF.Silu),
              r=[T_cT], w=[T_scTp])
        S.add("act", lambda e: e.activation(out=scTs.rearrange("p k (s t) -> p k s t", s=16),
                                            in_=cT[:, :, 1:17].unsqueeze(3).to_broadcast([128, 8, 16, 8]), func=AF.Silu),
              r=[T_cT], w=[T_scTs])
        psA = [TB[0], TB[1]]
        psB = [TB[2], TB[3]]
        for nb in range(6):
            si = next_slot()
            wv = wslot_t[si][:, 0:4096].rearrange("p (k n) -> p k n", k=8)
            S.add("pool", lambda e, wv=wv, li=li, nb=nb: e.dma_start(
                out=wv, in_=ada_w[li].rearrange("(k p) n -> p k n", p=128)[:, :, nb * 512:(nb + 1) * 512]),
                w=wslot[si][0:2], dma=1, nobar=True)
            ab = nb % 2
            S.add("sp", lambda e, ab=ab, li=li, nb=nb: e.dma_start(out=adab[ab][0:1, :], in_=ada_b[li:li + 1, nb * 512:(nb + 1) * 512]),
                  w=[T_adab[ab]], dma=1)
            for which, (lhs, T_lhs, pst, pbank) in enumerate(((scTp, T_scTp, psA[nb % 2], banks[nb % 2]),
                                                               (scTs, T_scTs, psB[nb % 2], banks[2 + nb % 2]))):
                def _mm(e, lhs=lhs, pbank=pbank, wv=wv, ab=ab):
                    e.matmul(pbank[:], lhsT=onesf[0:1, :], rhs=adab[ab][0:1, :], start=True, stop=False)
                    for k in range(8):
                        ins = e.matmul(pbank[:], lhsT=lhs[:, k, :], rhs=wv[:, k, :], start=False, stop=(k == 7))
                    return ins
                S.add("pe", _mm, cost=2.0, r=[T_lhs, T_adab[ab], T_ones] + wslot[si][0:2], w=[pst])
                blk = slice((nb % 2) * 512, (nb % 2) * 512 + 512)
                if nb < 2:
                    dst, T_dst = (SH_p, T_SH_p) if which == 0 else (SH_s, T_SH_s)
                    S.add("act", lambda e, dst=dst, pbank=pbank, blk=blk: e.activation(out=dst[:, blk], in_=pbank[:], func=AF.Identity),
                          r=[pst], w=[T_dst])
                elif nb < 4:
                    dst, T_dst = (G_p, T_G_p) if which == 0 else (G_s, T_G_s)
                    S.add("dve", lambda e, dst=dst, pbank=pbank, blk=blk: e.scalar_tensor_tensor(
                        out=dst[:, blk], in0=pbank[:], scalar=1.0, in1=normg[:, blk], op0=ALU.add, op1=ALU.mult),
                        r=[pst, T_big8], w=[T_dst])
                else:
                    dst, T_dst = (gt1p, T_gt1p) if which == 0 else (gt1s, T_gt1s)
                    S.add("dve", lambda e, dst=dst, pbank=pbank, blk=blk: e.tensor_scalar(
                        out=dst[:, blk], in0=pbank[:], scalar1=1.0, scalar2=None, op0=ALU.add),
                        r=[pst], w=[T_dst])
        ckpt("MOD%d" % li)
        psT = [TB[4], TB[5]]
        for c in range(NCH):
            G, T_G, SH, T_SH = (G_p, T_G_p, SH_p, T_SH_p) if c < SC else (G_s, T_G_s, SH_s, T_SH_s)
            rms_stats(res[c], res_t[:, c, :], T_junk, junk)
            S.add("dve", lambda e, c=c, G=G: e.scalar_tensor_tensor(out=tmpA, in0=res_t[:, c, :], scalar=small[:, 2:3], in1=G,
                                                                   op0=ALU.mult, op1=ALU.mult),
                  r=[res[c], T_rstd, T_G], w=[T_tmpA])
            S.add("pool", lambda e, SH=SH: e.tensor_tensor(out=hnb, in0=tmpA, in1=SH, op=ALU.add), r=[T_tmpA, T_SH], w=[T_hnb])
            pb = banks[4 + c % 2]
            pbv = pb[:].bitcast(BF16)

            def _tr(e, pbv=pbv):
                for k in range(8):
                    ins = e.transpose(out=pbv[:, k * 128:(k + 1) * 128], in_=hnb[:, k * 128:(k + 1) * 128], identity=identb_t[:])
                return ins
            S.add("pe", _tr, cost=0.9, r=[T_hnb, T_identb], w=[psT[c % 2]])
            S.add("act", lambda e, c=c, pbv=pbv: e.activation(out=hnT_t[:, :, c * 128:(c + 1) * 128],
                                                             in_=pbv[:, 0:1024].rearrange("p (k t) -> p k t", k=8), func=AF.Identity),
                  r=[psT[c % 2]], w=[hnT[c]])

        ckpt("A%d" % li)
        if is_ssd:
            S.add("sp", lambda e, lj=lj: e.dma_start(out=convw, in_=convw_d[lj]), w=[T_lvec], dma=1)
            S.add("sp", lambda e, lj=lj: e.dma_start(out=convb, in_=convb_d[lj]), w=[T_lvec], dma=1)
            S.add("sp", lambda e, lj=lj: e.dma_start(out=vec32, in_=vec32_d[lj]), w=[T_lvec], dma=1)
            S.add("pool", lambda e, lj=lj: e.dma_start(out=wdt_t[:].rearrange("p a b -> p (a b)"), in_=wsd[lj]), w=[T_wdt], dma=1)
            dtb, alog, Dcol = vec32[:, 0:32], vec32[:, 32:64], vec32[:, 64:96]
            S.add("act", lambda e: e.activation(out=Acol[:], in_=alog, func=AF.Exp), r=[T_lvec], w=[T_Acol])
            S.add("dve", lambda e: e.tensor_scalar(out=Acol[:], in0=Acol[:], scalar1=-1.0, scalar2=None, op0=ALU.mult),
                  r=[T_Acol], w=[T_Acol])
            T_psd = [TB[0], TB[1]]
            pd0 = banks[0][:, 0:512].rearrange("p (c h) -> p c h", c=16)
            pd1 = banks[1][:, 0:32]

            def _dtmm(e):
                for c in range(NCH):
                    o = pd0[:, c, :] if c < 16 else pd1
                    for k in range(8):
                        ins = e.matmul(o, lhsT=hnT_t[:, k, c * 128:(c + 1) * 128], rhs=wdt_t[:, k, :], start=(k == 0), stop=(k == 7))
                return ins
            S.add("pe", _dtmm, cost=8.0, r=hnT + [T_wdt], w=T_psd)
            ckpt("P1")
            def _v(e):
                e.tensor_tensor(out=scr_a[:, 0:16, :], in0=pd0, in1=dtb.unsqueeze(1).to_broadcast([128, 16, 32]), op=ALU.add)
                return e.tensor_tensor(out=scr_a[:, 16, :], in0=pd1, in1=dtb, op=ALU.add)
            S.add("dve", _v, r=T_psd + [T_lvec], w=[T_pp[5]])
            S.add("act", lambda e: e.activation(out=scr2, in_=scr2, func=AF.Exp), r=[T_pp[5]], w=[T_pp[5]])
            S.add("act", lambda e: e.activation(out=pp[:, 0, :], in_=scr2, func=AF.Ln, bias=small[:, 9:10]),
                  r=[T_pp[5], T_eps], w=[T_pp[0]])
            S.add("dve", lambda e: e.tensor_tensor(out=scr_a, in0=dt_a, in1=Acol[:].unsqueeze(1).to_broadcast([128, NCH, 32]), op=ALU.mult),
                  r=[T_pp[0], T_Acol], w=[T_pp[5]])
            ckpt("P2")
            T_pac = [TB[2], TB[3]]
            T_pat = [TB[4], TB[5]]
            pa0 = banks[2][:, 0:512].rearrange("p (c h) -> p c h", c=16); pa1 = banks[3][:, 0:32]
            pt0 = banks[4][:, 0:512].rearrange("p (c h) -> p c h", c=16); pt1 = banks[5][:, 0:32]

            def _acmm(e):
                for c in range(NCH):
                    oa = pa0[:, c, :] if c < 16 else pa1
                    ot = pt0[:, c, :] if c < 16 else pt1
                    e.matmul(oa, lhsT=(Lp_ap if c < 16 else Ls_ap), rhs=scr_a[:, c, :], start=True, stop=True)
                    ins = e.matmul(ot, lhsT=(onesf if c < 16 else Ss_ap), rhs=scr_a[:, c, :], start=True, stop=True)
                return ins
            S.add("pe", _acmm, cost=3.0, r=[T_pp[5], T_cbase, T_ones], w=T_pac + T_pat)
            ckpt("P3")

            def _nac(e):
                e.tensor_scalar(out=nac_a[:, 0:16, :], in0=pa0, scalar1=-1.0, scalar2=None, op0=ALU.mult)
                return e.tensor_scalar(out=nac_a[:, 16, :], in0=pa1, scalar1=-1.0, scalar2=None, op0=ALU.mult)
            S.add("dve", _nac, r=T_pac, w=[T_pp[2]])
            ckpt("P3a")

            def _eac(e):
                e.activation(out=eac_a[:, 0:16, :], in_=pa0, func=AF.Exp)
                e.activation(out=eac_a[:, 16, :], in_=pa1, func=AF.Exp)
                e.activation(out=dat_a[:, 0:16, :], in_=pt0, func=AF.Exp)
                return e.activation(out=dat_a[:, 16, :], in_=pt1, func=AF.Exp)
            S.add("act", _eac, r=T_pac + T_pat, w=[T_pp[3], T_pp[4]])
            ckpt("P3b")

            def _dd(e):
                e.tensor_tensor(out=dtd_a[:, 0:16, :], in0=pt0, in1=nac_a[:, 0:16, :], op=ALU.add)
                return e.tensor_tensor(out=dtd_a[:, 16, :], in0=pt1, in1=nac_a[:, 16, :], op=ALU.add)
            S.add("dve", _dd, r=T_pat + [T_pp[2]], w=[T_pp[1]])
            ckpt("P3c")
            S.add("act", lambda e: e.activation(out=pp[:, 1, :], in_=pp[:, 1, :], func=AF.Exp), r=[T_pp[1]], w=[T_pp[1]])
            S.add("dve", lambda e: e.tensor_tensor(out=pp[:, 1, :], in0=pp[:, 1, :], in1=pp[:, 0, :], op=ALU.mult),
                  r=[T_pp[1], T_pp[0]], w=[T_pp[1]])
            ckpt("P4")
            S.add("sp", lambda e, lj=lj: e.dma_start(out=big8[:], in_=sng_d[lj]), w=[T_big8], dma=1)
            S.add("dve", lambda e: e.tensor_tensor(out=Dg, in0=scr_a[:, 16:17, :].to_broadcast([128, 16, 32]),
                                                   in1=rowmask.unsqueeze(2).to_broadcast([128, 16, 32]), op=ALU.mult),
                  r=[T_pp[5], T_cbase], w=[T_Dg])
            T_pds = TB[0]
            S.add("pe", lambda e: e.matmul(banks[0][:], lhsT=onesf, rhs=Dg.rearrange("p a b -> p (a b)"), start=True, stop=True),
                  r=[T_Dg, T_ones], w=[T_pds])
            S.add("act", lambda e: e.activation(out=dats[:].rearrange("p a b -> p (a b)"), in_=banks[0][:], func=AF.Exp),
                  r=[T_pds], w=[T_dats])
            S.barrier()
            ckpt("PRE%d" % li)
            L2 = dict(L_); L2.update(locals())
            ssd_layer(nc, S, L2, lj)
        else:
            S.add("sp", lambda e, lj=lj: e.dma_start(out=pscT, in_=psc_d[lj]), w=[T_lvec], dma=1)
            S.add("pool", lambda e: e.dma_start(out=pmat[:, 0:3584].rearrange("p (a b) -> p a b", a=28), in_=pmat_d.rearrange("p (a b) -> p a b", a=28)), w=[T_big8], dma=1)
            S.barrier()
            ckpt("PRE%d" % li)
            L2 = dict(L_); L2.update(locals())
            pool_layer(nc, S, L2, lj)
        S.barrier()
        ckpt("L%d" % li)


def _final(L_):
    globals().update({k: v for k, v in L_.items() if not k.startswith("__")})
    wk_reset()
    junk = wk([128, D], BF16); T_junk = Tl(junk)
    yb = [wk([128, D]) for _ in range(2)]; T_yb = [Tl(a) for a in yb]
    S.add("sp", lambda e: e.dma_start(out=big8[:, 0:1024], in_=fng_d), w=[T_big8], dma=1)
    for c in range(NCH):
        rms_stats(res[c], res_t[:, c, :], T_junk, junk)
        S.add("dve", lambda e, c=c: e.scalar_tensor_tensor(out=yb[c % 2], in0=res_t[:, c, :], scalar=small[:, 2:3], in1=big8[:, 0:1024],
                                                           op0=ALU.mult, op1=ALU.mult),
              r=[res[c], T_rstd, T_big8], w=[T_yb[c % 2]])
        S.add("sp", lambda e, c=c: e.dma_start(out=yout[c * 128:(c + 1) * 128, :], in_=yb[c % 2]), r=[T_yb[c % 2]], dma=1, out=True)
    o = Op(); o.idx = len(S.ops); o.eng = "sp"; o.fn = lambda e: None
    o.dma = 0; o.deps = set(S.out_dmas); o.cost = 0.1
    S.ops.append(o)
    S.emit(nc)


def _cat_lvec(lj):
    return None


def ssd_layer(nc, S, L, lj):
    g_ = L
    (banks, bankGH, wk, wk_reset, wslot, wslot_t, next_slot, hnT, hnT_t, res, res_t, tmp, T_tmp) = (
        g_[k] for k in ("banks", "bankGH", "wk", "wk_reset", "wslot", "wslot_t", "next_slot", "hnT", "hnT_t", "res", "res_t", "tmp", "T_tmp"))
    identf, identb_t, T_identb, T_cbase, rowmask = g_["identf"], g_["identb_t"], g_["T_identb"], g_["T_cbase"], g_["rowmask"]
    cneg, T_cneg, ccol, T_ccol = g_["cneg"], g_["T_cneg"], g_["ccol"], g_["T_ccol"]
    negonesf, T_ones = g_["negonesf"], g_["T_ones"]
    convw, convb, vec32, T_lvec = g_["convw"], g_["convb"], g_["vec32"], g_["T_lvec"]
    dt_a, dtd_a, nac_a, eac_a, dat_a = g_["dt_a"], g_["dtd_a"], g_["nac_a"], g_["eac_a"], g_["dat_a"]
    T_pp, dats, T_dats = g_["T_pp"], g_["dats"], g_["T_dats"]
    gt1p, T_gt1p, gt1s, T_gt1s = g_["gt1p"], g_["T_gt1p"], g_["gt1s"], g_["T_gt1s"]
    big8, T_big8, small, T_eps = g_["big8"], g_["T_big8"], g_["small"], g_["T_eps"]
    st_ssm, st_conv, wsi, wso = g_["st_ssm"], g_["st_conv"], g_["wsi"], g_["wso"]
    o_ssm_p, o_ssm_s, o_conv_p, o_conv_s = g_["o_ssm_p"], g_["o_ssm_s"], g_["o_conv_p"], g_["o_conv_s"]
    Dcol = vec32[:, 64:96]

    def v3(ap, a):
        return ap.rearrange("p (a b) -> p a b", a=a)

    wk_reset()
    xps = wk([128, 4, 16, 11]); T_xp = Tl(xps)
    xp = xps.rearrange("p a b c -> p (a b c)")[:, 0:4 * 131].rearrange("p (a b) -> p a b", a=4)
    acc = wk([128, 4, 128]); T_acc = Tl(acc)
    xa = wk([128, 4, 128], BF16); T_xa = Tl(xa)
    xs = wk([128, 256], BF16); T_xs = Tl(xs)
    xdt = wk([128, 256], BF16); T_xdt = Tl(xdt)
    xdtd = wk([128, 256], BF16); T_xdtd = Tl(xdtd)
    Bsb = wk([128, 128], BF16); T_Bsb = Tl(Bsb)
    Dexp = wk([128, 4, 128]); T_Dexp = Tl(Dexp)
    Eb = wk([128, 4, 128]); T_E = Tl(Eb)
    MT = wk([128, 4, 128], BF16); T_MT = Tl(MT)
    y1 = wk([128, 256]); T_y1 = Tl(y1)
    sz = wk([128, 256]); T_sz = Tl(sz)
    junk = sz; T_junk = T_sz
    gn = wk([128, 256], BF16); T_gn = Tl(gn)
    gT = wk([128, 2, 128], BF16); T_gT = Tl(gT)
    hT = wk([128, 256]); T_hT = Tl(hT)
    hTb = wk([128, 256], BF16); T_hTb = Tl(hTb)
    h0f = [wk([128, 4, 256]) for _ in range(2)]; T_h0f = [Tl(a) for a in h0f]
    h0b = [wk([128, 4, 256], BF16) for _ in range(2)]; T_h0b = [Tl(a) for a in h0b]
    Bm = [wk([128, 4, 128], BF16) for _ in range(2)]; T_Bm = [Tl(a) for a in Bm]
    CTm = [wk([128, 4, 128], BF16) for _ in range(2)]; T_CTm = [Tl(a) for a in CTm]
    cvs = wk([128, 192]); T_cvs = Tl(cvs)
    cvp = wk([128, 12]); T_cvp = Tl(cvp)
    sq = wk([128, 4]); T_sq = Tl(sq)

    pA = Tl(banks[0][:], bank=0)
    pZ = Tl(banks[1][:, 0:256], bank=1); pCB = Tl(banks[1][:, 256:384], bank=1)
    pC = banks[2][:].bitcast(BF16)
    pTx = Tl(pC[:, 0:384], bank=2); pTg = Tl(pC[:, 512:768], bank=2)
    pD = Tl(banks[3][:], bank=3)
    pYa = Tl(banks[4][:, 0:256], bank=4); pYb = Tl(banks[4][:, 256:512], bank=4)
    pF = Tl(banks[5][:], bank=5)
    pO = Tl(bankGH[:], bank=6)
    hcnt = [0]

    sz2 = [sz, wk([128, 256])]; T_sz2 = [T_sz, Tl(sz2[1])]
    W = {}; SZ = {}; acnt = [0]
    cw = v3(convw, 32)

    def load_w(g):
        si = next_slot()
        win = wslot_t[si][:, 0:6144].rearrange("p (k n) -> p k n", k=8)
        wout = wslot_t[si][:, 6144:8192].rearrange("p (j n) -> p j n", j=2)
        T_wi = wslot[si][0:3]
        T_wo = wslot[si][3:4]
        S.add("pool", lambda e, win=win, g=g: e.dma_start(out=win, in_=wsi[lj, g].rearrange("p (k n) -> p k n", k=8)),
              w=T_wi, dma=1, nobar=True)
        S.add("pool", lambda e, wout=wout, g=g: e.dma_start(out=wout, in_=wso[lj, g].rearrange("p (j n) -> p j n", j=2)),
              w=T_wo, dma=1, nobar=True)
        W[g] = (win, wout, T_wi, T_wo)

    def emitA(g, c):
        g4 = g * 4
        win, wout, T_wi, T_wo = W[g]
        samp = (c == SC)
        tok = slice(c * 128, (c + 1) * 128)
        si_ = acnt[0] % 2; acnt[0] += 1; SZ[(g, c)] = si_
        sz = sz2[si_]; T_sz = T_sz2[si_]
        def _xbc(e, win=win, tok=tok):
            for j in range(4):
                for k in range(8):
                    ins = e.matmul(banks[0][:, j * 128:(j + 1) * 128], lhsT=win[:, k, j * 128:(j + 1) * 128], rhs=hnT_t[:, k, tok],
                                   start=(k == 0), stop=(k == 7))
            return ins
        S.add("pe", _xbc, cost=2.4, r=T_wi + [hnT[c]], w=[pA])
        def _z(e, win=win, tok=tok):
            for k in range(8):
                ins = e.matmul(banks[1][:, 0:256], lhsT=hnT_t[:, k, tok], rhs=win[:, k, 512:768], start=(k == 0), stop=(k == 7))
            return ins
        S.add("pe", _z, cost=1.9, r=T_wi + [hnT[c]], w=[pZ])
        if samp:
            S.add("sp", lambda e, g=g: e.dma_start(out=cvs, in_=st_conv[lj, g]), w=[T_cvs], dma=1)
            S.add("pool", lambda e: e.tensor_copy(out=xps[:, :, :, 0:3], in_=cvs.rearrange("p (a b c) -> p a b c", a=4, b=16)),
                  r=[T_cvs], w=[T_xp])
            S.add("act", lambda e: e.activation(out=xps[:, :, :, 3:11], in_=banks[0][:].rearrange("p (a b c) -> p a b c", a=4, b=16),
                                                func=AF.Identity), r=[pA], w=[T_xp])
            S.add("pool", lambda e: e.tensor_copy(out=cvs.rearrange("p (a b c) -> p a b c", a=4, b=16), in_=xps[:, :, :, 8:11]),
                  r=[T_xp], w=[T_cvs])
            S.add("sp", lambda e, g=g: e.dma_start(out=o_conv_s[lj, g], in_=cvs), r=[T_cvs], dma=1, out=True)
            src = lambda j, k: xps[:, j, :, k:k + 8]
            accv = lambda j: acc[:, j, :].rearrange("p (a b) -> p a b", a=16)
        else:
            if c == 0:
                S.add("pool", lambda e: e.memset(xp[:, :, 0:3], 0.0), w=[T_xp])
            S.add("act", lambda e: e.activation(out=xp[:, :, 3:131], in_=v3(banks[0][:], 4), func=AF.Identity), r=[pA], w=[T_xp])
            src = lambda j, k: xp[:, j, k:k + 128]
            accv = lambda j: acc[:, j, :]

        for k in range(4):
            def _conv(e, src=src, accv=accv, g4=g4, k=k):
                for j in range(4):
                    ti = g4 + j
                    if k == 0:
                        ins = e.tensor_scalar(out=accv(j), in0=src(j, 0), scalar1=cw[:, ti, 0:1], scalar2=convb[:, ti:ti + 1], op0=ALU.mult, op1=ALU.add)
                    else:
                        ins = e.scalar_tensor_tensor(out=accv(j), in0=src(j, k), scalar=cw[:, ti, k:k + 1], in1=accv(j), op0=ALU.mult, op1=ALU.add)
                return ins
            S.add("dve", _conv, cost=1.15, r=[T_xp, T_lvec] + ([T_acc] if k else []), w=[T_acc])
        if not samp:
            if c == 15:
                S.add("pool", lambda e: e.tensor_copy(out=v3(cvp, 4), in_=xp[:, :, 128:131]), r=[T_xp], w=[T_cvp])
                S.add("sp", lambda e, g=g: e.dma_start(out=o_conv_p[lj, g], in_=cvp), r=[T_cvp], dma=1, out=True)
            else:
                S.add("pool", lambda e: e.tensor_copy(out=xp[:, :, 0:3], in_=xp[:, :, 128:131]), r=[T_xp], w=[T_xp])
        S.add("act", lambda e: e.activation(out=xa, in_=acc, func=AF.Silu), r=[T_acc], w=[T_xa])
        S.add("act", lambda e: e.activation(out=sz, in_=banks[1][:, 0:256], func=AF.Silu), r=[pZ], w=[T_sz], cost=0.85)

    def emitM(g, c):
        g4 = g * 4
        win, wout, T_wi, T_wo = W[g]
        samp = (c == SC)
        tok = slice(c * 128, (c + 1) * 128)
        def _trx(e):
            e.transpose(out=pC[:, 0:128], in_=xa[:, 0, :], identity=identb_t[:])
            e.transpose(out=pC[:, 128:256], in_=xa[:, 1, :], identity=identb_t[:])
            return e.transpose(out=pC[:, 256:384], in_=xa[:, 2, :], identity=identb_t[:])
        S.add("pe", _trx, r=[T_xa, T_identb], w=[pTx])
        S.add("act", lambda e: e.activation(out=xs, in_=pC[:, 0:256], func=AF.Identity), r=[pTx], w=[T_xs])
        S.add("act", lambda e: e.activation(out=Bsb, in_=pC[:, 256:384], func=AF.Identity), r=[pTx], w=[T_Bsb])
        S.add("dve", lambda e, c=c, g4=g4: e.tensor_tensor(out=v3(xdt, 4), in0=v3(pC[:, 0:256], 4),
                                                          in1=dt_a[:, c, g4:g4 + 4].unsqueeze(2).to_broadcast([128, 4, 64]), op=ALU.mult),
              r=[pTx, T_pp[0]], w=[T_xdt])
        S.add("dve", lambda e, c=c, g4=g4: e.tensor_tensor(out=v3(xdtd, 4), in0=v3(pC[:, 0:256], 4),
                                                          in1=dtd_a[:, c, g4:g4 + 4].unsqueeze(2).to_broadcast([128, 4, 64]), op=ALU.mult),
              r=[pTx, T_pp[1]], w=[T_xdtd])
        S.add("pe", lambda e: e.matmul(banks[1][:, 256:384], lhsT=xa[:, 2, :], rhs=xa[:, 3, :], start=True, stop=True),
              r=[T_xa], w=[pCB])
        S.add("dve", lambda e, c=c, g4=g4: e.tensor_tensor(out=Dexp, in0=identf.unsqueeze(1).to_broadcast([128, 4, 128]),
                                                          in1=nac_a[:, c, g4:g4 + 4].unsqueeze(2).to_broadcast([128, 4, 128]), op=ALU.mult),
              r=[T_cbase, T_pp[2]], w=[T_Dexp], cost=0.7)
        ncol = slice(512, 1024) if samp else slice(0, 512)

        def _seg(e, ncol=ncol):
            e.matmul(banks[3][:], lhsT=negonesf, rhs=Dexp.rearrange("p a b -> p (a b)"), start=True, stop=False)
            return e.matmul(banks[3][:], lhsT=identb_t[:], rhs=cneg[:, ncol], start=False, stop=True)
        S.add("pe", _seg, cost=2.0, r=[T_Dexp, T_ones, T_identb, T_cneg], w=[pD])

        def _E(e, c=c, g4=g4):
            for h in range(4):
                ins = e.activation(out=Eb[:, h, :], in_=banks[3][:, h * 128:(h + 1) * 128], func=AF.Exp, bias=nac_a[:, c, g4 + h:g4 + h + 1])
            return ins
        S.add("act", _E, cost=2.2, r=[pD, T_pp[2]], w=[T_E])
        S.add("dve", lambda e: e.tensor_tensor(out=MT, in0=Eb, in1=banks[1][:, 256:384].unsqueeze(1).to_broadcast([128, 4, 128]), op=ALU.mult),
              r=[T_E, pCB], w=[T_MT], cost=0.6)
        def _yi(e):
            for h in range(4):
                ins = e.matmul(banks[4][:, h * 64:(h + 1) * 64], lhsT=MT[:, h, :], rhs=xdt[:, h * 64:(h + 1) * 64], start=True, stop=True)
            return ins
        S.add("pe", _yi, r=[T_MT, T_xdt], w=[pYa])
        if not samp:
            if c == 0:
                S.add("pool", lambda e: e.memset(hT, 0.0), w=[T_hT])
                S.add("pool", lambda e: e.memset(hTb, 0.0), w=[T_hTb])
            S.add("pe", lambda e: e.matmul(banks[4][:, 256:512], lhsT=xa[:, 3, :], rhs=hTb, start=True, stop=True), r=[T_xa, T_hTb], w=[pYb])
            S.add("pe", lambda e: e.matmul(banks[5][:, 0:256], lhsT=Bsb, rhs=xdtd, start=True, stop=True), r=[T_Bsb, T_xdtd], w=[pF])
            S.add("dve", lambda e, c=c, g4=g4: e.tensor_tensor(out=v3(hT, 4), in0=v3(hT, 4),
                                                              in1=dat_a[:, c, g4:g4 + 4].unsqueeze(2).to_broadcast([128, 4, 64]), op=ALU.mult),
                  r=[T_hT, T_pp[4], pYb], w=[T_hT])
            S.add("dve", lambda e: e.tensor_tensor(out=hT, in0=hT, in1=banks[5][:, 0:256], op=ALU.add), r=[T_hT, pF], w=[T_hT])
            if c == 15:
                S.add("sp", lambda e, g=g: e.dma_start(out=o_ssm_p[lj, g], in_=hT), r=[T_hT], dma=1, out=True)
            else:
                S.add("act", lambda e: e.activation(out=hTb, in_=hT, func=AF.Identity), r=[T_hT], w=[T_hTb])
        else:
            for pc in range(4):
                b = hcnt[0] % 2
                hcnt[0] += 1
                S.add("sp", lambda e, b=b, g=g, pc=pc: e.dma_start(out=h0f[b], in_=st_ssm[lj, g, :, pc * 4:(pc + 1) * 4, :]),
                      w=[T_h0f[b]], dma=1)
                S.add("pool", lambda e, b=b: e.tensor_copy(out=h0b[b], in_=h0f[b]), r=[T_h0f[b]], w=[T_h0b[b]])
                S.add("pool", lambda e, b=b, pc=pc: e.tensor_tensor(out=CTm[b], in0=xa[:, 3, :].unsqueeze(1).to_broadcast([128, 4, 128]),
                                                                   in1=ccol[:, :, 96 - 32 * pc: 224 - 32 * pc], op=ALU.mult),
                      r=[T_xa, T_ccol], w=[T_CTm[b]])
                S.add("pool", lambda e, b=b, pc=pc: e.tensor_tensor(out=Bm[b], in0=Bsb.unsqueeze(1).to_broadcast([128, 4, 128]),
                                                                   in1=rowmask[:, pc * 4:(pc + 1) * 4].unsqueeze(2).to_broadcast([128, 4, 128]), op=ALU.mult),
                      r=[T_Bsb, T_cbase], w=[T_Bm[b]])

                def _yis(e, b=b, pc=pc):
                    for s_ in range(4):
                        ins = e.matmul(banks[4][:, 256:512], lhsT=CTm[b][:, s_, :], rhs=h0b[b][:, s_, :],
                                       start=(pc == 0 and s_ == 0), stop=(pc == 3 and s_ == 3))
                    return ins
                S.add("pe", _yis, cost=0.8, r=[T_CTm[b], T_h0b[b]] + ([pYa] if pc == 0 else []), w=[pYb])
                for hp in range(2):
                    def _sts(e, b=b, hp=hp):
                        for s_ in range(2):
                            ins = e.matmul(banks[5][:, s_ * 256:(s_ + 1) * 256], lhsT=Bm[b][:, hp * 2 + s_, :], rhs=xdtd, start=True, stop=True)
                        return ins
                    S.add("pe", _sts, r=[T_Bm[b], T_xdtd], w=[pF])
                    seq0 = pc * 4 + hp * 2
                    hv = h0f[b][:, hp * 2:hp * 2 + 2, :].rearrange("p s (h q) -> p s h q", h=4)
                    S.add("dve", lambda e, hv=hv, seq0=seq0, g4=g4: e.tensor_tensor(
                        out=hv, in0=hv, in1=dats[:, seq0:seq0 + 2, g4:g4 + 4].unsqueeze(3).to_broadcast([128, 2, 4, 64]), op=ALU.mult),
                        r=[T_h0f[b], T_dats, T_h0b[b]], w=[T_h0f[b]])
                    hv2 = h0f[b][:, hp * 2:hp * 2 + 2, :]
                    S.add("dve", lambda e, hv2=hv2: e.tensor_tensor(out=hv2, in0=hv2, in1=banks[5][:].rearrange("p (s q) -> p s q", s=2), op=ALU.add),
                          r=[T_h0f[b], pF], w=[T_h0f[b]])
                S.add("sp", lambda e, b=b, g=g, pc=pc: e.dma_start(out=o_ssm_s[lj, g, :, pc * 4:(pc + 1) * 4, :], in_=h0f[b]),
                      r=[T_h0f[b]], dma=1, out=True)

    def emitT(g, c):
        g4 = g * 4
        win, wout, T_wi, T_wo = W[g]
        samp = (c == SC)
        tok = slice(c * 128, (c + 1) * 128)
        si_ = SZ[(g, c)]
        sz = sz2[si_]; T_sz = T_sz2[si_]; junk = sz; T_junk = T_sz
        S.add("dve", lambda e, c=c, g4=g4: e.tensor_tensor(out=v3(y1, 4), in0=v3(banks[4][:, 256:512], 4),
                                                          in1=eac_a[:, c, g4:g4 + 4].unsqueeze(2).to_broadcast([128, 4, 64]), op=ALU.mult),
              r=[pYb, T_pp[3]], w=[T_y1])
        S.add("dve", lambda e: e.tensor_tensor(out=y1, in0=y1, in1=banks[4][:, 0:256], op=ALU.add), r=[T_y1, pYa], w=[T_y1])

        def _dsk(e, g4=g4):
            for h in range(4):
                hs = slice(h * 64, (h + 1) * 64)
                ins = e.scalar_tensor_tensor(out=y1[:, hs], in0=xs[:, hs], scalar=Dcol[:, g4 + h:g4 + h + 1], in1=y1[:, hs], op0=ALU.mult, op1=ALU.add)
            return ins
        S.add("dve", _dsk, cost=0.8, r=[T_xs, T_y1, T_lvec], w=[T_y1])
        S.add("dve", lambda e: e.tensor_tensor(out=y1, in0=y1, in1=sz, op=ALU.mult), r=[T_y1, T_sz], w=[T_y1])
        S.add("act", lambda e: e.activation(out=junk, in_=y1, func=AF.Square, accum_out=sq[:, 0:1]), r=[T_y1], w=[T_junk, T_sq])
        S.add("pool", lambda e: e.tensor_scalar(out=sq[:, 1:2], in0=sq[:, 0:1], scalar1=1.0 / 256, scalar2=EPS, op0=ALU.mult, op1=ALU.add),
              r=[T_sq], w=[T_sq])
        S.add("pool", lambda e: e.tensor_tensor(out=sq[:, 2:3], in0=sq[:, 1:2], in1=small[:, 10:11], op=ALU.pow), r=[T_sq, T_eps], w=[T_sq])
        S.add("dve", lambda e, g=g: e.scalar_tensor_tensor(out=gn, in0=y1, scalar=sq[:, 2:3], in1=big8[:, g * 256:(g + 1) * 256],
                                                          op0=ALU.mult, op1=ALU.mult), r=[T_y1, T_sq, T_big8], w=[T_gn], cost=0.55)

        def _trg(e):
            e.transpose(out=pC[:, 512:640], in_=gn[:, 0:128], identity=identb_t[:])
            return e.transpose(out=pC[:, 640:768], in_=gn[:, 128:256], identity=identb_t[:])
        S.add("pe", _trg, r=[T_gn, T_identb], w=[pTg])
        S.add("act", lambda e: e.activation(out=gT, in_=v3(pC[:, 512:768], 2), func=AF.Identity), r=[pTg], w=[T_gT])

        def _out(e, wout=wout):
            for half in range(2):
                for j in range(2):
                    ins = e.matmul(bankGH[:, half * 512:(half + 1) * 512], lhsT=gT[:, j, :], rhs=wout[:, j, half * 512:(half + 1) * 512],
                                   start=(j == 0), stop=(j == 1))
            return ins
        S.add("pe", _out, cost=1.2, r=[T_gT] + T_wo, w=[pO])
        if samp:
            S.add("dve", lambda e: e.tensor_tensor(out=bankGH[:], in0=bankGH[:], in1=gt1s[:], op=ALU.mult), r=[pO, T_gt1s], w=[pO])
            S.add("dve", lambda e, c=c: e.tensor_tensor(out=res_t[:, c, :], in0=res_t[:, c, :], in1=bankGH[:], op=ALU.add),
                  r=[res[c], pO], w=[res[c]])
            S.add("pool", lambda e, wout=wout: e.tensor_tensor(out=wout, in0=wout, in1=gt1p[:].unsqueeze(1).to_broadcast([128, 2, 1024]), op=ALU.mult),
                  r=T_wo + [T_gt1p], w=T_wo)
        else:
            S.add("dve", lambda e, c=c: e.tensor_tensor(out=res_t[:, c, :], in0=res_t[:, c, :], in1=bankGH[:], op=ALU.add),
                  r=[res[c], pO], w=[res[c]])

    units = [(g, c) for g in range(8) for c in [SC] + list(range(16))]
    load_w(0)
    emitA(*units[0])
    for i_, (g, c) in enumerate(units):
        if c == SC and g + 1 < 8:
            load_w(g + 1)
        emitM(g, c)
        if i_ + 1 < len(units):
            emitA(*units[i_ + 1])
        emitT(g, c)


def pool_layer(nc, S, L, lj):
    g_ = L
    (banks, bankGH, wk, wk_reset, wslot, wslot_t, next_slot, hnT, hnT_t, res, res_t, tmp, T_tmp) = (
        g_[k] for k in ("banks", "bankGH", "wk", "wk_reset", "wslot", "wslot_t", "next_slot", "hnT", "hnT_t", "res", "res_t", "tmp", "T_tmp"))
    pmat, T_big8, pscT, T_lvec = g_["pmat"], g_["T_big8"], g_["pscT"], g_["T_lvec"]
    gt1p, T_gt1p, gt1s, T_gt1s = g_["gt1p"], g_["T_gt1p"], g_["gt1s"], g_["T_gt1s"]
    st_pool, wpi, wpm, wpo, o_pool_p, o_pool_s = g_["st_pool"], g_["wpi"], g_["wpm"], g_["wpo"], g_["o_pool_p"], g_["o_pool_s"]

    def v3(ap, a):
        return ap.rearrange("p (a b) -> p a b", a=a)

    wk_reset()
    ub = [wk([128, 512], BF16) for _ in range(2)]; T_ub = [Tl(a) for a in ub]
    uf = wk([128, 512]); T_uf = Tl(uf)
    plT = wk([128, 4, 128], BF16); T_plT = Tl(plT)
    szT = wk([128, 4, 128]); T_szT = Tl(szT)
    m2T = wk([128, 4, 128], BF16); T_m2T = Tl(m2T)
    prevS = wk([128, 2, 512], BF16); T_prevS = Tl(prevS)

    pU = Tl(banks[0][:], bank=0); pP = Tl(banks[1][:], bank=1); pM = Tl(banks[2][:], bank=2); pZ = Tl(banks[3][:], bank=3); pO = Tl(bankGH[:], bank=6)
    S.add("sp", lambda e: e.dma_start(out=o_pool_s[lj, :, 0:7, :], in_=st_pool[lj].rearrange("(s j) c -> s j c", j=15)[:, 8:15, :]),
          dma=1, out=True)
    ucnt = [0]
    ckpt("Q0")
    for g in range(4):
        if g == 1:
            ckpt("Q4")
        si = next_slot()
        win = wslot_t[si][:].rearrange("p (k n) -> p k n", k=8)
        T_wi = wslot[si][0:4]
        S.add("pool", lambda e, win=win, g=g: e.dma_start(out=win, in_=wpi[lj, g].rearrange("p (k n) -> p k n", k=8)),
              w=T_wi, dma=1, nobar=True)
        si2 = next_slot()
        wmix = wslot_t[si2][:, 0:2048].rearrange("p (k n) -> p k n", k=4)
        wout = wslot_t[si2][:, 2048:6144].rearrange("p (k n) -> p k n", k=4)
        T_wm = wslot[si2][0:1]
        T_wo = wslot[si2][1:3]
        S.add("pool", lambda e, wmix=wmix, g=g: e.dma_start(out=wmix, in_=wpm[lj, g].rearrange("p (k n) -> p k n", k=4)),
              w=T_wm, dma=1, nobar=True)
        S.add("pool", lambda e, wout=wout, g=g: e.dma_start(out=wout, in_=wpo[lj, g].rearrange("p (k n) -> p k n", k=4)),
              w=T_wo, dma=1, nobar=True)
        S.add("pool", lambda e, g=g: e.dma_start(out=prevS[0:120, :, :],
                                                 in_=st_pool[lj].rearrange("(h r) c -> r h c", h=2)[:, :, g * 512:(g + 1) * 512]),
              w=[T_prevS], dma=1)
        mb = g * 7 * 128
        Pcur, P0hi, P0lo, Pprev, PcS, PpS0, PpS1 = (pmat[:, mb + i * 128: mb + (i + 1) * 128] for i in range(7))
        prev_u = None
        for c in [SC] + list(range(16)):
            samp = (c == SC)
            tok = slice(c * 128, (c + 1) * 128)
            bi = ucnt[0] % 2
            ucnt[0] += 1
            cur = ub[bi]; T_cur = T_ub[bi]

            def _u(e, win=win, tok=tok):
                for k in range(8):
                    ins = e.matmul(banks[0][:], lhsT=hnT_t[:, k, tok], rhs=win[:, k, 0:512], start=(k == 0), stop=(k == 7))
                return ins
            S.add("pe", _u, cost=2.0, r=T_wi + [hnT[c]], w=[pU])
            S.add("act", lambda e, cur=cur: e.activation(out=cur, in_=banks[0][:], func=AF.Identity), r=[pU], w=[T_cur])
            if samp or c == 15:
                S.add("dve", lambda e: e.tensor_copy(out=uf, in_=banks[0][:]), r=[pU], w=[T_uf])
                if samp:
                    def _us(e, g=g):
                        return [e.dma_start(out=o_pool_s[lj, s_, 7:15, g * 512:(g + 1) * 512], in_=uf[s_ * 8:(s_ + 1) * 8, :]) for s_ in range(16)]
                    S.add("sp", _us, r=[T_uf], dma=16, out=True)
                else:
                    S.add("sp", lambda e, g=g: e.dma_start(out=o_pool_p[lj, :, g * 512:(g + 1) * 512], in_=uf[113:128, :]), r=[T_uf], dma=1, out=True)
            if samp:
                def _pl(e, cur=cur, PcS=PcS, PpS0=PpS0, PpS1=PpS1):
                    for ct in range(4):
                        cs = slice(ct * 128, (ct + 1) * 128)
                        o = banks[1][:, cs]
                        e.matmul(o, lhsT=cur[:, cs], rhs=PcS, start=True, stop=False)
                        e.matmul(o, lhsT=prevS[0:120, 0, cs], rhs=PpS0[0:120, :], start=False, stop=False)
                        ins = e.matmul(o, lhsT=prevS[0:120, 1, cs], rhs=PpS1[0:120, :], start=False, stop=True)
                    return ins
                S.add("pe", _pl, cost=1.0, r=[T_cur, T_prevS, T_big8], w=[pP])
            elif c == 0:
                def _pl(e, cur=cur, P0hi=P0hi, P0lo=P0lo):
                    for ct in range(4):
                        cs = slice(ct * 128, (ct + 1) * 128)
                        o = banks[1][:, cs]
                        e.matmul(o, lhsT=cur[:, cs], rhs=P0hi, start=True, stop=False)
                        ins = e.matmul(o, lhsT=cur[:, cs], rhs=P0lo, start=False, stop=True)
                    return ins
                S.add("pe", _pl, cost=1.0, r=[T_cur, T_big8], w=[pP])
            else:
                pu, T_pu = prev_u

                def _pl(e, cur=cur, pu=pu, Pcur=Pcur, Pprev=Pprev):
                    for ct in range(4):
                        cs = slice(ct * 128, (ct + 1) * 128)
                        o = banks[1][:, cs]
                        e.matmul(o, lhsT=cur[:, cs], rhs=Pcur, start=True, stop=False)
                        ins = e.matmul(o, lhsT=pu[:, cs], rhs=Pprev, start=False, stop=True)
                    return ins
                S.add("pe", _pl, cost=1.0, r=[T_cur, T_pu, T_big8], w=[pP])
            prev_u = (cur, T_cur)
            S.add("act", lambda e: e.activation(out=plT, in_=v3(banks[1][:], 4), func=AF.Identity), r=[pP], w=[T_plT])

            def _mx(e, wmix=wmix):
                for dt_ in range(4):
                    for ct in range(4):
                        ins = e.matmul(banks[2][:, dt_ * 128:(dt_ + 1) * 128], lhsT=wmix[:, ct, dt_ * 128:(dt_ + 1) * 128], rhs=plT[:, ct, :],
                                       start=(ct == 0), stop=(ct == 3))
                return ins
            S.add("pe", _mx, cost=0.9, r=T_wm + [T_plT], w=[pM])

            def _zt(e, win=win, tok=tok):
                for dt_ in range(4):
                    for k in range(8):
                        ins = e.matmul(banks[3][:, dt_ * 128:(dt_ + 1) * 128], lhsT=win[:, k, 512 + dt_ * 128: 512 + (dt_ + 1) * 128], rhs=hnT_t[:, k, tok],
                                       start=(k == 0), stop=(k == 7))
                return ins
            S.add("pe", _zt, cost=1.9, r=T_wi + [hnT[c]], w=[pZ])
            S.add("act", lambda e: e.activation(out=szT, in_=v3(banks[3][:], 4), func=AF.Silu), r=[pZ], w=[T_szT])

            def _m2(e, g=g):
                for dt_ in range(4):
                    ins = e.scalar_tensor_tensor(out=m2T[:, dt_, :], in0=banks[2][:, dt_ * 128:(dt_ + 1) * 128], scalar=pscT[:, g * 4 + dt_: g * 4 + dt_ + 1],
                                                 in1=szT[:, dt_, :], op0=ALU.mult, op1=ALU.mult)
                return ins
            S.add("dve", _m2, cost=1.1, r=[pM, T_szT, T_lvec], w=[T_m2T])

            def _out(e, wout=wout):
                for half in range(2):
                    for dt_ in range(4):
                        ins = e.matmul(bankGH[:, half * 512:(half + 1) * 512], lhsT=m2T[:, dt_, :], rhs=wout[:, dt_, half * 512:(half + 1) * 512],
                                       start=(dt_ == 0), stop=(dt_ == 3))
                return ins
            S.add("pe", _out, cost=1.2, r=[T_m2T] + T_wo, w=[pO])
            if samp:
                S.add("dve", lambda e: e.tensor_tensor(out=bankGH[:], in0=bankGH[:], in1=gt1s[:], op=ALU.mult), r=[pO, T_gt1s], w=[pO])
                S.add("dve", lambda e, c=c: e.tensor_tensor(out=res_t[:, c, :], in0=res_t[:, c, :], in1=bankGH[:], op=ALU.add),
                      r=[res[c], pO], w=[res[c]], cost=1.2)
                S.add("pool", lambda e, wout=wout: e.tensor_tensor(out=wout, in0=wout, in1=gt1p[:].unsqueeze(1).to_broadcast([128, 4, 1024]), op=ALU.mult),
                      r=T_wo + [T_gt1p], w=T_wo)
            else:
                S.add("dve", lambda e, c=c: e.tensor_tensor(out=res_t[:, c, :], in0=res_t[:, c, :], in1=bankGH[:], op=ALU.add),
                      r=[res[c], pO], w=[res[c]], cost=1.2)
            if g == 0 and c == SC:
                ckpt("Q1")
            if g == 0 and c == 0:
                ckpt("Q2")
            if g == 0 and c == 1:
                ckpt("Q3")


def _prep_shared(inp):
    f = lambda a: np.ascontiguousarray(a, dtype=np.float32)
    sh = {}
    sh["ada_w"] = f(inp["ada_w"])
    sh["ada_b"] = f(inp["ada_b"])
    sh["normg_bc"] = f(np.broadcast_to(inp["norm_g"][:, None, :], (4, 128, 1024)))
    sh["fng_bc"] = f(np.broadcast_to(inp["final_norm_g"][None, :], (128, 1024)))
    w_in = inp["ssd_w_in"]
    wsi = np.empty((2, 8, 128, 8, 768), np.float32)
    for g in range(8):
        cols = np.concatenate([2048 + 256 * g + np.arange(256), 4096 + 128 * g + np.arange(128),
                               5120 + 128 * g + np.arange(128), 256 * g + np.arange(256)])
        blk = w_in[:, :, cols].reshape(2, 8, 128, 768)
        wsi[:, g] = blk.transpose(0, 2, 1, 3)
    sh["w_ssd_in"] = wsi.reshape(2, 8, 128, 8 * 768)
    sh["w_ssd_dt"] = f(w_in[:, :, 6144:6176].reshape(2, 8, 128, 32).transpose(0, 2, 1, 3)).reshape(2, 128, 256)
    wo = inp["ssd_w_out"].reshape(2, 8, 2, 128, 1024)
    sh["w_ssd_out"] = f(wo.transpose(0, 1, 3, 2, 4)).reshape(2, 8, 128, 2048)
    cwv = inp["ssd_conv_w"]
    tiles = []
    for g in range(8):
        tiles += [2 * g, 2 * g + 1, 16 + g, 24 + g]
    tiles = np.array(tiles)
    cw = cwv.reshape(2, 4, 32, 128)[:, :, tiles, :]
    sh["convw"] = f(cw.transpose(0, 3, 2, 1)).reshape(2, 128, 128)
    cb = inp["ssd_conv_b"].reshape(2, 32, 128)[:, tiles, :]
    sh["convb"] = f(cb.transpose(0, 2, 1))
    v = np.concatenate([inp["ssd_dt_bias"], inp["ssd_a_log"], inp["ssd_d"]], axis=1)
    sh["vec32"] = f(np.broadcast_to(v[:, None, :], (2, 128, 96)))
    sh["ssd_normg_bc"] = f(np.broadcast_to(inp["ssd_norm_g"][:, None, :], (2, 128, 2048)))
    pw = inp["pool_w_in"]
    wpi = np.empty((2, 4, 128, 8, 1024), np.float32)
    for g in range(4):
        cols = np.concatenate([512 * g + np.arange(512), 2048 + 512 * g + np.arange(512)])
        wpi[:, g] = pw[:, :, cols].reshape(2, 8, 128, 1024).transpose(0, 2, 1, 3)
    sh["w_pool_in"] = wpi.reshape(2, 4, 128, 8192)
    sh["w_pool_mix"] = f(inp["pool_w_group"].reshape(2, 4, 4, 128, 512).transpose(0, 1, 3, 2, 4)).reshape(2, 4, 128, 2048)
    sh["w_pool_out"] = f(inp["pool_w_out"].reshape(2, 4, 4, 128, 1024).transpose(0, 1, 3, 2, 4)).reshape(2, 4, 128, 4096)
    sh["pscaleT"] = f(inp["pool_scale"].reshape(2, 16, 128).transpose(0, 2, 1))
    cb_, cn_, cc_ = _consts()
    sh["cbase"], sh["cneg"], sh["ccol"] = cb_, cn_, cc_
    sh["pmats"] = _pool_mats()
    return sh


def _prep_core(inp, i):
    f = lambda a: np.ascontiguousarray(a, dtype=np.float32)
    d = {}
    xs = inp["x_sample"][16 * i:16 * i + 16].reshape(128, D)
    d["xin"] = f(np.concatenate([inp["x_prompt"][i], xs], axis=0))
    c = np.concatenate([inp["c_prompt"][i:i + 1], inp["c_sample"][16 * i:16 * i + 16]], axis=0)
    d["cT"] = f(c.reshape(17, 8, 128).transpose(2, 1, 0)).reshape(128, 136)
    ss = inp["state_ssm"][:, 16 * i:16 * i + 16]
    ss = ss.reshape(2, 16, 8, 256, 128).transpose(0, 2, 4, 1, 3)
    d["st_ssm"] = f(ss)
    sc = inp["state_conv"][:, 16 * i:16 * i + 16]
    sc = sc.reshape(2, 16, 3, 32, 128)
    tiles = []
    for g in range(8):
        tiles += [2 * g, 2 * g + 1, 16 + g, 24 + g]
    sc = sc[:, :, :, np.array(tiles), :].reshape(2, 16, 3, 8, 4, 128)
    d["st_conv"] = f(sc.transpose(0, 3, 5, 4, 1, 2)).reshape(2, 8, 128, 192)
    d["st_pool"] = f(inp["state_pool"][:, 16 * i:16 * i + 16].reshape(2, 240, 2048))
    return d


_TILES = None


def _conv_tiles():
    t = []
    for g in range(8):
        t += [2 * g, 2 * g + 1, 16 + g, 24 + g]
    return np.array(t)


def _assemble(results, NL=4):
    nssd, npool = (NL + 1) // 2, NL // 2
    y_p = np.stack([r["yout"][:2048] for r in results]).astype(np.float32)
    y_s = np.concatenate([r["yout"][2048:].reshape(16, 8, D) for r in results]).astype(np.float32)
    sp = np.stack([r["o_ssm_p"] for r in results], axis=1)
    ssm_p = sp.reshape(2, 8, 8, 128, 4, 64).transpose(0, 1, 2, 4, 5, 3).reshape(2, 8, 32, 64, 128)
    ss = np.stack([r["o_ssm_s"] for r in results], axis=1)
    ssm_s = ss.reshape(2, 8, 8, 128, 16, 4, 64).transpose(0, 1, 4, 2, 5, 6, 3).reshape(2, 128, 32, 64, 128)
    tiles = _conv_tiles()
    inv = np.argsort(tiles)
    cp = np.stack([r["o_conv_p"] for r in results], axis=1)
    cp = cp.reshape(2, 8, 8, 128, 4, 3).transpose(0, 1, 5, 2, 4, 3).reshape(2, 8, 3, 32, 128)[:, :, :, inv, :]
    conv_p = cp.reshape(2, 8, 3, 4096)
    cs = np.stack([r["o_conv_s"] for r in results], axis=1)
    cs = cs.reshape(2, 8, 8, 128, 4, 16, 3).transpose(0, 1, 5, 6, 2, 4, 3).reshape(2, 128, 3, 32, 128)[:, :, :, inv, :]
    conv_s = cs.reshape(2, 128, 3, 4096)
    pool_p = np.stack([r["o_pool_p"] for r in results], axis=1)
    pool_s = np.concatenate([r["o_pool_s"] for r in results], axis=1)
    c = lambda a: np.ascontiguousarray(a, dtype=np.float32)
    return (c(y_p), c(y_s), c(ssm_p[:nssd]), c(conv_p[:nssd]), c(pool_p[:npool]), c(ssm_s[:nssd]), c(conv_s[:nssd]), c(pool_s[:npool]))


_NC_CACHE = {}


def kernel(_NL=4, **inputs):
    inputs = {k: np.asarray(v) for k, v in inputs.items()}
    if _NL not in _NC_CACHE:
        _NC_CACHE[_NL] = build_nc(_NL)
    nc = _NC_CACHE[_NL]
    sh = _prep_shared(inputs)
    in_maps = []
    for i in range(NCORES):
        d = dict(sh)
        d.update(_prep_core(inputs, i))
        in_maps.append(d)
    res = run_bass_kernel_spmd(nc, in_maps, core_ids=list(range(NCORES)))
    return _assemble(res.results, _NL)
```
